# Optimizing a Trainium2 kernel written in Bass

```python
import jax, jax.numpy as jnp
from jax import lax
import numpy as np

D_MODEL = 1024
BATCH = 32
SEQ = 2048
DEPTH = 1

SSD_EXPAND = 2
D_INNER = SSD_EXPAND * D_MODEL
SSD_HEAD_DIM = 64
SSD_HEADS = D_INNER // SSD_HEAD_DIM
SSD_GROUPS = 4
SSD_HEADS_PER_GROUP = SSD_HEADS // SSD_GROUPS
SSD_STATE = 128
SSD_CONV = 4
SSD_CHUNK = 128
SSD_CONV_DIM = D_INNER + 2 * SSD_GROUPS * SSD_STATE
MLA_HEADS = 16
MLA_Q_LORA = 256
MLA_KV_LORA = 128
MLA_NOPE = 64
MLA_ROPE = 32
MLA_QK_DIM = MLA_NOPE + MLA_ROPE
MLA_V_DIM = 64
MLA_WIDTH = MLA_HEADS * MLA_V_DIM
ROPE_THETA = 10000.0
ATTN_Q_BLOCK = 128
D_FF = 2816
FFN_CONV = 3
NORM_EPS = 1e-6

OFF_Z = D_INNER
OFF_XBC = OFF_Z + SSD_CONV_DIM
OFF_DT = OFF_XBC + SSD_HEADS
OFF_QA = OFF_DT + MLA_Q_LORA
OFF_KVA = OFF_QA + MLA_KV_LORA
OFF_KR = OFF_KVA + MLA_ROPE
IN_COLS = OFF_KR + 2 * D_MODEL

kernel_name = "hybrid_ssd_mla_gated_convffn"


def rms_norm(x, g):
    xf = x.astype(jnp.float32)
    xf = xf * lax.rsqrt(jnp.mean(xf * xf, axis=-1, keepdims=True) + NORM_EPS)
    return (xf * g.astype(jnp.float32)).astype(x.dtype)


def causal_depthwise_conv(x, w, b):
    k, c = w.shape
    y = lax.conv_general_dilated(x, w[:, None, :].astype(x.dtype), window_strides=(1,),
                                 padding=[(k - 1, 0)], dimension_numbers=('NWC', 'WIO', 'NWC'),
                                 feature_group_count=c)
    return y + b


def rope_tables(seq, dim):
    inv = 1.0 / (ROPE_THETA ** (jnp.arange(0, dim, 2, dtype=jnp.float32) / dim))
    ang = jnp.arange(seq, dtype=jnp.float32)[:, None] * inv[None, :]
    return jnp.cos(ang)[None, :, None, :], jnp.sin(ang)[None, :, None, :]


def apply_rope(x, cos, sin):
    x1, x2 = jnp.split(x.astype(jnp.float32), 2, axis=-1)
    return jnp.concatenate([x1 * cos - x2 * sin, x1 * sin + x2 * cos], axis=-1).astype(x.dtype)


def ssd_chunked_scan(xdt, a, bm, cm):
    b, l, h, p = xdt.shape
    nc = l // SSD_CHUNK
    f32 = jnp.float32
    xc = xdt.astype(f32).reshape(b, nc, SSD_CHUNK, SSD_GROUPS, SSD_HEADS_PER_GROUP, p)
    bc = bm.astype(f32).reshape(b, nc, SSD_CHUNK, SSD_GROUPS, SSD_STATE)
    cc = cm.astype(f32).reshape(b, nc, SSD_CHUNK, SSD_GROUPS, SSD_STATE)
    ac = a.astype(f32).reshape(b, nc, SSD_CHUNK, SSD_GROUPS, SSD_HEADS_PER_GROUP).transpose(0, 1, 3, 4, 2)
    a_cum = jnp.cumsum(ac, axis=-1)
    causal = jnp.tril(jnp.ones((SSD_CHUNK, SSD_CHUNK), dtype=bool))
    seg = a_cum[..., :, None] - a_cum[..., None, :]
    decay_in = jnp.exp(jnp.where(causal, seg, -jnp.inf))
    scores = jnp.einsum('bclgn,bcsgn->bcgls', cc, bc)
    y_diag = jnp.einsum('bcgls,bcgels,bcsgep->bclgep', scores, decay_in, xc)
    decay_to_end = jnp.exp(a_cum[..., -1:] - a_cum)
    chunk_states = jnp.einsum('bclgn,bcgel,bclgep->bcgepn', bc, decay_to_end, xc)
    chunk_decay = jnp.exp(a_cum[..., -1])

    def carry_state(state, inp):
        new_states, decay = inp
        return state * decay[..., None, None] + new_states, state

    init = jnp.zeros_like(chunk_states[:, 0])
    _, prev = lax.scan(carry_state, init,
                       (jnp.moveaxis(chunk_states, 1, 0), jnp.moveaxis(chunk_decay, 1, 0)))
    y_off = jnp.einsum('bclgn,cbgepn,bcgel->bclgep', cc, prev, jnp.exp(a_cum))
    return (y_diag + y_off).reshape(b, l, h, p).astype(xdt.dtype)


def ssd_mixer(z, xbc, dt_raw, conv_w, conv_b, dt_bias, a_log, d_skip, norm_g):
    b, l, _ = z.shape
    xbc = jax.nn.silu(causal_depthwise_conv(xbc, conv_w, conv_b))
    xs, bm, cm = jnp.split(xbc, [D_INNER, D_INNER + SSD_GROUPS * SSD_STATE], axis=-1)
    xs = xs.reshape(b, l, SSD_HEADS, SSD_HEAD_DIM)
    bm = bm.reshape(b, l, SSD_GROUPS, SSD_STATE)
    cm = cm.reshape(b, l, SSD_GROUPS, SSD_STATE)
    dt = jax.nn.softplus(dt_raw.astype(jnp.float32) + dt_bias.astype(jnp.float32))
    a = -jnp.exp(a_log.astype(jnp.float32))
    y = ssd_chunked_scan(xs * dt[..., None].astype(xs.dtype), dt * a, bm, cm)
    y = y + xs * d_skip[:, None]
    y = y.reshape(b, l, D_INNER) * jax.nn.silu(z)
    y = rms_norm(y.reshape(b, l, SSD_GROUPS, D_INNER // SSD_GROUPS),
                 norm_g.reshape(SSD_GROUPS, D_INNER // SSD_GROUPS))
    return y.reshape(b, l, D_INNER)


def blocked_causal_attention(q, k, v):
    b, l, h, dk = q.shape
    scale = dk ** -0.5
    key_pos = jnp.arange(l)

    def one_block(i):
        start = i * ATTN_Q_BLOCK
        qb = lax.dynamic_slice_in_dim(q, start, ATTN_Q_BLOCK, axis=1)
        s = jnp.einsum('bqhd,bkhd->bhqk', qb, k).astype(jnp.float32) * scale
        qpos = start + jnp.arange(ATTN_Q_BLOCK)
        s = jnp.where(qpos[:, None] >= key_pos[None, :], s, -jnp.inf)
        p = jax.nn.softmax(s, axis=-1).astype(v.dtype)
        return jnp.einsum('bhqk,bkhd->bqhd', p, v)

    out = lax.map(one_block, jnp.arange(l // ATTN_Q_BLOCK))
    return jnp.moveaxis(out, 0, 1).reshape(b, l, h * v.shape[-1])


def mla_mixer(q_a, kv_a, k_rope_raw, q_a_norm_g, w_uq, kv_a_norm_g, w_ukv, q_norm_g, k_norm_g, cos, sin):
    b, l, _ = q_a.shape
    q = (rms_norm(q_a, q_a_norm_g) @ w_uq).reshape(b, l, MLA_HEADS, MLA_QK_DIM)
    kv = (rms_norm(kv_a, kv_a_norm_g) @ w_ukv).reshape(b, l, MLA_HEADS, MLA_NOPE + MLA_V_DIM)
    k_nope, v = jnp.split(kv, [MLA_NOPE], axis=-1)
    k_rope = jnp.broadcast_to(k_rope_raw[:, :, None, :], (b, l, MLA_HEADS, MLA_ROPE))
    k = jnp.concatenate([k_nope, k_rope], axis=-1)
    q = rms_norm(q, q_norm_g)
    k = rms_norm(k, k_norm_g)
    q = jnp.concatenate([q[..., :MLA_NOPE], apply_rope(q[..., MLA_NOPE:], cos, sin)], axis=-1)
    k = jnp.concatenate([k[..., :MLA_NOPE], apply_rope(k[..., MLA_NOPE:], cos, sin)], axis=-1)
    return blocked_causal_attention(q, k, v)


def setup_inputs(seed: int = 0) -> dict:
    key = jax.random.key(seed)
    ks = jax.random.split(key, 24)
    f32 = jnp.float32

    def dense(k, shape, fan_in):
        return jax.random.normal(k, (DEPTH,) + shape, f32) * fan_in ** -0.5

    def gain(k, n):
        return 1.0 + 0.02 * jax.random.normal(k, (DEPTH, n), f32)

    def bias(k, n):
        return 0.02 * jax.random.normal(k, (DEPTH, n), f32)

    dt0 = jnp.exp(jax.random.uniform(ks[5], (DEPTH, SSD_HEADS), f32, np.log(1e-3), np.log(1e-1)))
    return {
        "x": jax.random.normal(ks[0], (BATCH, SEQ, D_MODEL), f32),
        "norm_mix_g": gain(ks[1], D_MODEL),
        "w_in": dense(ks[2], (D_MODEL, IN_COLS), D_MODEL),
        "conv_ssd_w": dense(ks[3], (SSD_CONV, SSD_CONV_DIM), SSD_CONV),
        "conv_ssd_b": bias(ks[4], SSD_CONV_DIM),
        "dt_bias": dt0 + jnp.log(-jnp.expm1(-dt0)),
        "a_log": jnp.log(jax.random.uniform(ks[6], (DEPTH, SSD_HEADS), f32, 1.0, 16.0)),
        "d_skip": 1.0 + 0.1 * jax.random.normal(ks[7], (DEPTH, SSD_HEADS), f32),
        "ssd_norm_g": gain(ks[8], D_INNER),
        "w_ssd_proj": dense(ks[9], (D_INNER, D_MODEL), D_INNER),
        "q_a_norm_g": gain(ks[10], MLA_Q_LORA),
        "w_uq": dense(ks[11], (MLA_Q_LORA, MLA_HEADS * MLA_QK_DIM), MLA_Q_LORA),
        "kv_a_norm_g": gain(ks[12], MLA_KV_LORA),
        "w_ukv": dense(ks[13], (MLA_KV_LORA, MLA_HEADS * (MLA_NOPE + MLA_V_DIM)), MLA_KV_LORA),
        "q_norm_g": gain(ks[14], MLA_QK_DIM),
        "k_norm_g": gain(ks[15], MLA_QK_DIM),
        "w_mla_proj": dense(ks[16], (MLA_WIDTH, D_MODEL), MLA_WIDTH),
        "gate_b": bias(ks[17], 2 * D_MODEL),
        "w_o": dense(ks[18], (D_MODEL, D_MODEL), D_MODEL),
        "norm_ffn_g": gain(ks[19], D_MODEL),
        "w_up": dense(ks[20], (D_MODEL, 2 * D_FF), D_MODEL),
        "conv_ffn_w": dense(ks[21], (FFN_CONV, 2 * D_FF), FFN_CONV),
        "conv_ffn_b": bias(ks[22], 2 * D_FF),
        "w_down": dense(ks[23], (D_FF, D_MODEL), D_FF),
    }


def reference(x, norm_mix_g, w_in, conv_ssd_w, conv_ssd_b, dt_bias, a_log, d_skip, ssd_norm_g,
              w_ssd_proj, q_a_norm_g, w_uq, kv_a_norm_g, w_ukv, q_norm_g, k_norm_g, w_mla_proj,
              gate_b, w_o, norm_ffn_g, w_up, conv_ffn_w, conv_ffn_b, w_down):
    cos, sin = rope_tables(x.shape[1], MLA_ROPE)
    for i in range(DEPTH):
        h = rms_norm(x, norm_mix_g[i])
        proj = h @ w_in[i]
        z, xbc, dt_raw, q_a, kv_a, k_rope_raw, gates = jnp.split(
            proj, [OFF_Z, OFF_XBC, OFF_DT, OFF_QA, OFF_KVA, OFF_KR], axis=-1)
        y_ssd = ssd_mixer(z, xbc, dt_raw, conv_ssd_w[i], conv_ssd_b[i], dt_bias[i], a_log[i],
                          d_skip[i], ssd_norm_g[i]) @ w_ssd_proj[i]
        y_mla = mla_mixer(q_a, kv_a, k_rope_raw, q_a_norm_g[i], w_uq[i], kv_a_norm_g[i], w_ukv[i],
                          q_norm_g[i], k_norm_g[i], cos, sin) @ w_mla_proj[i]
        g_ssd, g_mla = jnp.split(jax.nn.sigmoid(gates + gate_b[i]), 2, axis=-1)
        x = x + (g_ssd * y_ssd + g_mla * y_mla) @ w_o[i]
        h = rms_norm(x, norm_ffn_g[i])
        u = causal_depthwise_conv(h @ w_up[i], conv_ffn_w[i], conv_ffn_b[i])
        u_gate, u_val = jnp.split(u, 2, axis=-1)
        x = x + (jax.nn.silu(u_gate) * u_val) @ w_down[i]
    return x
```

```python
import numpy as np
from contextlib import ExitStack
import concourse.bass as bass
import concourse.mybir as mybir
from concourse.bass_utils import run_bass_kernel_spmd

F32 = mybir.dt.float32
BF16 = mybir.dt.bfloat16
AF = mybir.ActivationFunctionType
ALU = mybir.AluOpType
AX = mybir.AxisListType

D_MODEL = 1024
D_INNER = 2048
IN_COLS = 7616
D_FF = 2816
EPS = 1e-6
OFF_XBC = 2048
OFF_DT = 5120
OFF_G = 5568
NCORES = 8
WSHAPES = {
    "w_in": (1024, IN_COLS), "w_ssd_proj": (2048, 1024), "w_uq": (256, 1536), "w_ukv": (128, 2048),
    "w_mla_proj": (1024, 1024), "w_o": (1024, 1024), "w_up": (1024, 2 * D_FF), "w_down": (D_FF, 1024),
}
SLOT_ELEMS = 4096


def weight_chunks():
    ch = [("w_in", 128, 8, 0, ((OFF_DT, 448),))]
    ch += [("w_in", 128, 8, 0, ((OFF_XBC + i * 512, 512),)) for i in range(6)]
    ch += [("w_in", 128, 8, 0, ((i * 512, 512),)) for i in range(4)]
    for mb in range(4):
        ch += [("w_ssd_proj", 128, 16, 0, ((mb * 256, 256),)), ("w_in", 128, 8, 0, ((OFF_G + mb * 256, 256),))]
    ch += [("w_uq", 128, 2, 0, ((0, 1536),)), ("w_ukv", 128, 1, 0, ((0, 2048),))]
    for mb in range(4):
        ch += [("w_mla_proj", 64, 16, 0, ((mb * 256, 256),)), ("w_in", 128, 8, 0, ((OFF_G + 1024 + mb * 256, 256),))]
    ch += [("w_o", 128, 8, 0, ((nb * 512, 512),)) for nb in range(2)]
    ch += [("w_up", 128, 8, 0, ((i * 256, 256), (D_FF + i * 256, 256))) for i in range(11)]
    for m4 in range(4):
        for kh in range(2):
            ch.append(("w_down", 128, 11, kh * 1408, ((m4 * 256, 256),)))
    return ch
_UID = [0]


def un(name):
    _UID[0] += 1
    return "%s_u%d" % (name, _UID[0])

NSLOT = 4


class Ctx:
    def __init__(self, nc, es):
        self.nc = nc
        self.es = es
        self.engs = {'pe': nc.tensor, 'act': nc.scalar, 'dve': nc.vector, 'pool': nc.gpsimd, 'sp': nc.sync}
        self.sem = {}
        for k in ('pe', 'act', 'dve', 'pool'):
            self.sem[k] = es.enter_context(nc.semaphore("s_" + k))
        self.cnt = {k: 0 for k in self.sem}
        self.waited = {k: {} for k in self.engs}
        self.lastw = {}
        self.readers = {}
        self.dsem = {}
        self.dcnt = {}
        self.ninst = {k: 0 for k in self.engs}
        self.rr = 0

    def sb(self, name, shape, dt=F32):
        return self.es.enter_context(self.nc.sbuf_tensor(name, list(shape), dt))

    def _deps(self, reads, writes):
        deps = {}

        def add(d):
            if d is None:
                return
            k, v = d
            if deps.get(k, 0) < v:
                deps[k] = v
        for r in reads:
            add(self.lastw.get(r))
        for w in writes:
            add(self.lastw.get(w))
            for d in self.readers.get(w, ()):
                add(d)
        return deps

    def _wait(self, eng, deps):
        h = self.engs[eng]
        wd = self.waited[eng]
        for k, v in deps.items():
            if wd.get(k, 0) >= v:
                continue
            if k == 'pe' and eng == 'pe':
                continue
            s = self.sem[k] if k in self.sem else self.dsem[k]
            h.wait_ge(s, v)
            wd[k] = v
            self.ninst[eng] += 1

    def _commit(self, tok, reads, writes):
        for r in reads:
            self.readers.setdefault(r, []).append(tok)
        for w in writes:
            self.lastw[w] = tok
            self.readers[w] = []

    def op(self, eng, fn, reads=(), writes=()):
        self._wait(eng, self._deps(reads, writes))
        ins = fn(self.engs[eng])
        self.cnt[eng] += 1
        self.ninst[eng] += 1
        ins.then_inc(self.sem[eng], 1)
        self._commit((eng, self.cnt[eng]), reads, writes)
        return ins

    def any2(self, fn, reads=(), writes=()):
        self.rr += 1
        return self.op('dve' if self.rr % 3 else 'pool', fn, reads, writes)

    def mm(self, fn, reads=(), writes=(), last=True):
        self._wait('pe', self._deps(reads, writes))
        ins = fn(self.engs['pe'])
        self.ninst['pe'] += 1
        if last:
            self.cnt['pe'] += 1
            ins.then_inc(self.sem['pe'], 1)
            tok = ('pe', self.cnt['pe'])
        else:
            tok = ('pe', self.cnt['pe'] + 1)
        self._commit(tok, reads, writes)
        return ins

    def dma(self, q, out, in_, reads=(), writes=(), key=None, **kw):
        if key not in self.dsem:
            self.dsem[key] = self.es.enter_context(self.nc.semaphore("d_" + key))
            self.dcnt[key] = 0
        self._wait(q, self._deps(reads, writes))
        ins = self.engs[q].dma_start(out=out, in_=in_, **kw)
        self.dcnt[key] += 16
        self.ninst[q] += 1
        ins.then_inc(self.dsem[key], 16)
        self._commit((key, self.dcnt[key]), reads, writes)
        return ins

    def barrier(self):
        deps = {k: self.cnt[k] for k in self.sem if self.cnt[k]}
        for k in self.dsem:
            deps[k] = self.dcnt[k]
        for e in self.engs:
            self._wait(e, dict(deps))


def host_consts(S):
    import ml_dtypes
    i = np.arange(128)
    c = {}
    c["c_ident"] = np.eye(128, dtype=np.float32)
    c["c_tri_incl"] = (i[:, None] <= i[None, :]).astype(np.float32)
    c["c_tri_strict"] = (i[:, None] > i[None, :]).astype(np.float32)
    neg = np.where(i[None, :] < i[:, None], -30000.0, 0.0).astype(np.float32)
    c["c_negmask"] = np.tile(neg, (1, 4))
    inv = (1.0 / (np.float32(10000.0) ** (np.arange(0, 32, 2, dtype=np.float32) / np.float32(32)))).astype(np.float32)
    ang = np.arange(S, dtype=np.float32)[:, None] * inv[None, :]
    cos = np.cos(ang).astype(np.float32).reshape(S // 128, 128, 16).transpose(1, 0, 2)
    sin = np.sin(ang).astype(np.float32).reshape(S // 128, 128, 16).transpose(1, 0, 2)
    c["c_cos"] = np.ascontiguousarray(cos)
    c["c_sin"] = np.ascontiguousarray(sin)
    return c


def host_params(inp):
    f = lambda a: np.ascontiguousarray(np.asarray(a, dtype=np.float32))
    rep = lambda v: f(np.broadcast_to(np.asarray(v, np.float32).reshape(1, -1), (128, v.size)))
    colT = lambda v: f(np.asarray(v, np.float32).reshape(-1, 128).T)
    p = {}
    p["p_gmix"] = colT(inp["norm_mix_g"][0])
    p["p_cw_ssd"] = f(np.asarray(inp["conv_ssd_w"][0]).reshape(4, 24, 128).transpose(2, 1, 0))
    p["p_cb_ssd"] = colT(inp["conv_ssd_b"][0])
    p["p_dtb"] = rep(inp["dt_bias"][0])
    p["p_alog"] = rep(inp["a_log"][0])
    p["p_dskip"] = colT(np.repeat(np.asarray(inp["d_skip"][0]), 64))
    p["p_gssd"] = colT(inp["ssd_norm_g"][0])
    p["p_gateb"] = colT(inp["gate_b"][0])
    p["p_gffn"] = colT(inp["norm_ffn_g"][0])
    p["p_cw_ffn"] = f(np.asarray(inp["conv_ffn_w"][0]).reshape(3, 44, 128).transpose(2, 1, 0))
    p["p_cb_ffn"] = colT(inp["conv_ffn_b"][0])
    p["p_gqa"] = rep(inp["q_a_norm_g"][0])
    p["p_gkva"] = rep(inp["kv_a_norm_g"][0])
    p["p_gq"] = rep(inp["q_norm_g"][0])
    p["p_gk"] = rep(inp["k_norm_g"][0])
    return p


PSHAPES = {
    "p_gmix": (128, 8), "p_cw_ssd": (128, 24, 4), "p_cb_ssd": (128, 24), "p_dtb": (128, 32), "p_alog": (128, 32),
    "p_dskip": (128, 16), "p_gssd": (128, 16), "p_gateb": (128, 16), "p_gffn": (128, 8),
    "p_cw_ffn": (128, 44, 3), "p_cb_ffn": (128, 44), "p_gqa": (128, 256), "p_gkva": (128, 128),
    "p_gq": (128, 96), "p_gk": (128, 96),
}


def CSHAPES(S):
    return {"c_ident": (128, 128), "c_tri_incl": (128, 128), "c_tri_strict": (128, 128), "c_negmask": (128, 512),
            "c_cos": (128, S // 128, 16), "c_sin": (128, S // 128, 16)}


def build(NSEQ=4, S=2048, TB=2, dbg=(), stop_after=None):
    BLK = TB * 128
    NT = S // 128
    NB = S // BLK
    nc = bass.Bass("TRN2", target_bir_lowering=False)

    def din(name, shape, dt=F32):
        return nc.dram_tensor(name, list(shape), dt, kind="ExternalInput").ap()

    x_d = din("x", [NSEQ * S, D_MODEL])
    out_d = nc.dram_tensor("out", [NSEQ * S, D_MODEL], F32, kind="ExternalOutput").ap()
    W = {k: din(k, v) for k, v in WSHAPES.items()}
    CHUNKS = weight_chunks()
    NCH = len(CHUNKS)
    NTOTCH = NCH * NSEQ * (S // (TB * 128))
    Wc = nc.dram_tensor("wchunks_bf", [NCH, 128, SLOT_ELEMS], BF16, kind="Internal").ap()
    Pd = {k: din(k, v) for k, v in PSHAPES.items()}
    Cd = {k: din(k, v) for k, v in CSHAPES(S).items()}
    kc_d = nc.dram_tensor("kcache", [16, 96, S], BF16, kind="Internal").ap()
    acs_d = [nc.dram_tensor("acs%d" % i, [32, 2, 128], BF16, kind="Internal").ap() for i in range(2)]
    dbg_d = {k: nc.dram_tensor("dbg_" + k, list(shp), dt_, kind="ExternalOutput").ap() for k, shp, dt_ in dbg}

    with ExitStack() as es:
        c = Ctx(nc, es)
        pall = es.enter_context(nc.psum_tensor("pall", [128, 4096], F32))

        def PB(b, n0=0, n1=512, p0=0, p1=128):
            return pall[p0:p1, b * 512 + n0: b * 512 + n1]

        def PBh(b):
            return pall[:, b * 512:(b + 1) * 512].bitcast(BF16)

        ident_f = c.sb("ident_f", [128, 128]); ident_b = c.sb("ident_b", [128, 128], BF16)
        triI = c.sb("triI", [128, 128]); triS = c.sb("triS", [128, 128])
        ones_f = c.sb("ones_f", [128, 128]); ones_b = c.sb("ones_b", [128, 128], BF16)
        negm_f = c.sb("negm_f", [128, 512]); negm_b = c.sb("negm_b", [128, 512], BF16)
        cmask_b = c.sb("cmask_b", [128, 128], BF16)
        cos_t = c.sb("cos_t", [128, NT, 16]); sin_t = c.sb("sin_t", [128, NT, 16])
        PR = {k: c.sb("sb_" + k, list(v)) for k, v in PSHAPES.items()}
        A_rep = c.sb("A_rep", [128, 32])
        xbuf = c.sb("xbuf", [128, TB, 1024])
        hT = c.sb("hT", [128, 8, BLK], BF16)
        Vc = c.sb("Vc", [128, NT, 16, 64], BF16)
        state = c.sb("state", [128, 4, 512]); state_b = c.sb("state_b", [128, 4, 512], BF16)
        halo_s = c.sb("halo_s", [128, 24, 3]); halo_f = c.sb("halo_f", [128, 44, 2])
        wslot = [c.sb("wslot%d" % i, [128, SLOT_ELEMS], BF16) for i in range(NSLOT)]
        mix = c.sb("mix", [128, 8, BLK])
        small = c.sb("small", [128, TB, 448])
        ss1 = c.sb("ss1", [128, 8]); rs1 = c.sb("rs1", [128, 8])
        junk = c.sb("junk", [128, 1024], BF16)
        rhs2 = c.sb("rhs2", [2, 4096], BF16)
        ones2 = c.sb("ones2", [2, 128], BF16)

        c.dma('sp', ident_f[:], Cd["c_ident"], writes=['ident_f'], key='ld')
        c.dma('sp', triI[:], Cd["c_tri_incl"], writes=['triI'], key='ld')
        c.dma('sp', triS[:], Cd["c_tri_strict"], writes=['triS'], key='ld')
        c.dma('sp', negm_f[:], Cd["c_negmask"], writes=['negm_f'], key='ld')
        c.dma('sp', cos_t[:], Cd["c_cos"], writes=['cos'], key='ld')
        c.dma('sp', sin_t[:], Cd["c_sin"], writes=['sin'], key='ld')
        for k in PSHAPES:
            c.dma('sp', PR[k][:], Pd[k], writes=[k], key='ld')
        c.barrier()
        c.op('dve', lambda e: e.tensor_copy(out=ident_b[:], in_=ident_f[:]), reads=['ident_f'], writes=['ident_b'])
        c.op('dve', lambda e: e.tensor_copy(out=cmask_b[:], in_=triI[:]), reads=['triI'], writes=['cmask_b'])
        c.op('dve', lambda e: e.tensor_copy(out=negm_b[:], in_=negm_f[:]), reads=['negm_f'], writes=['negm_b'])
        c.op('dve', lambda e: e.memset(ones_f[:], 1.0), writes=['ones_f'])
        c.op('dve', lambda e: e.memset(ones_b[:], 1.0), writes=['ones_b'])
        c.op('dve', lambda e: e.memset(ones2[:], 1.0), writes=['ones2'])
        c.op('act', lambda e: e.activation(out=A_rep[:], in_=PR["p_alog"][:], func=AF.Exp), reads=['p_alog'], writes=['A_rep'])
        c.op('dve', lambda e: e.tensor_scalar(out=A_rep[:], in0=A_rep[:], scalar1=-1.0, scalar2=None, op0=ALU.mult),
             reads=['A_rep'], writes=['A_rep'])

        with ExitStack() as es2:
            st32 = [es2.enter_context(nc.sbuf_tensor("st32_%d" % i, [128, SLOT_ELEMS], F32)) for i in range(2)]
            st16 = [es2.enter_context(nc.sbuf_tensor("st16_%d" % i, [128, SLOT_ELEMS], BF16)) for i in range(2)]
            for ci, (name, P_, KT, r0, segs) in enumerate(CHUNKS):
                i = ci % 2
                ntot = sum(n for _, n in segs)
                assert KT * ntot <= SLOT_ELEMS
                v32 = st32[i][0:P_, 0:KT * ntot].rearrange("p (k n) -> p k n", k=KT)
                o = 0
                for (c0, n) in segs:
                    c.dma('sp', v32[:, :, o:o + n], W[name][r0:r0 + KT * P_, c0:c0 + n].rearrange("(k p) n -> p k n", p=P_),
                          writes=['st32_%d' % i], key='pl%d' % i)
                    o += n
                eng = ('dve', 'pool', 'act')[ci % 3]
                if eng == 'act':
                    c.op('act', lambda e: e.activation(out=st16[i][0:P_, 0:KT * ntot], in_=st32[i][0:P_, 0:KT * ntot], func=AF.Copy),
                         reads=['st32_%d' % i], writes=['st16_%d' % i])
                else:
                    c.op(eng, lambda e: e.tensor_copy(out=st16[i][0:P_, 0:KT * ntot], in_=st32[i][0:P_, 0:KT * ntot]),
                         reads=['st32_%d' % i], writes=['st16_%d' % i])
                c.dma('sp', Wc[ci, 0:P_, 0:KT * ntot], st16[i][0:P_, 0:KT * ntot], reads=['st16_%d' % i], writes=['wc%d' % ci], key='ps%d' % i)
        c.barrier()

        wstate = {'next': 0, 'loaded': 0}

        def wissue(upto):
            while wstate['loaded'] < upto:
                gi = wstate['loaded']
                ci = gi % NCH
                name, P_, KT, r0, segs = CHUNKS[ci]
                ntot = sum(n for _, n in segs)
                i = gi % NSLOT
                c.dma('sp', wslot[i][0:P_, 0:KT * ntot], Wc[ci, 0:P_, 0:KT * ntot], reads=['wc%d' % ci], writes=['wslot%d' % i], key='w%d' % i)
                wstate['loaded'] += 1

        def wload(name, P_, KT, segs, r0=0):
            gi = wstate['next']
            wstate['next'] += 1
            spec = CHUNKS[gi % NCH]
            assert spec == (name, P_, KT, r0, tuple(segs)), (spec, name, segs)
            wissue(min(gi + 3, NTOTCH))
            ntot = sum(n for _, n in segs)
            i = gi % NSLOT
            return wslot[i][0:P_, 0:KT * ntot].rearrange("p (k n) -> p k n", k=KT), 'wslot%d' % i

        def rstd(ss_ap, out_ap, n, reads, writes):
            c.op('act', lambda e: e.activation(out=out_ap, in_=ss_ap, func=AF.Sqrt, bias=EPS, scale=1.0 / n), reads=reads, writes=writes)
            c.op('dve', lambda e: e.reciprocal(out=out_ap, in_=out_ap), reads=writes, writes=writes)

        def rmsnorm_to_hT(gname, xn_bufs):
            for t in range(TB):
                c.op('act', lambda e: e.activation(out=junk[:], in_=xbuf[:, t, :], func=AF.Square, accum_out=ss1[:, t:t + 1]),
                     reads=['xbuf%d' % t], writes=['junk', 'ss1_%d' % t])
                rstd(ss1[:, t:t + 1], rs1[:, t:t + 1], 1024, ['ss1_%d' % t], ['rs1_%d' % t])
                xn = xn_bufs[t % 2]
                c.op('act', lambda e: e.activation(out=xn[:], in_=xbuf[:, t, :], func=AF.Copy, scale=rs1[:, t:t + 1]),
                     reads=['xbuf%d' % t, 'rs1_%d' % t], writes=['xn%d' % (t % 2)])
                pt = PBh(t % 2)
                for kt in range(8):
                    c.mm(lambda e: e.transpose(out=pt[:, kt * 128:(kt + 1) * 128], in_=xn[:, kt * 128:(kt + 1) * 128], identity=ident_b[:]),
                         reads=['xn%d' % (t % 2), 'ident_b'], writes=['P%d' % (t % 2)], last=(kt == 7))
                c.op('dve', lambda e: e.tensor_tensor(out=hT[:, :, t * 128:(t + 1) * 128], in0=pt.rearrange("p (k n) -> p k n", k=8),
                                                      in1=PR[gname][:, :].unsqueeze(2).to_broadcast([128, 8, 128]), op=ALU.mult),
                     reads=['P%d' % (t % 2), gname], writes=['hT'])

        def conv_tile(ps_ap, stage, halo_ap, nh, wt, bt, acc, sname, aname, hname, pname):
            K = nh + 1
            c.op('pool', lambda e: e.tensor_copy(out=stage[:, 0:nh], in_=halo_ap), reads=[hname], writes=[sname + 'h'])
            c.op('act', lambda e: e.activation(out=stage[:, nh:nh + BLK], in_=ps_ap, func=AF.Copy), reads=[pname], writes=[sname])
            c.op('pool', lambda e: e.tensor_copy(out=halo_ap, in_=stage[:, BLK:BLK + nh]), reads=[sname, sname + 'h'], writes=[hname])
            c.op('dve', lambda e: e.tensor_scalar(out=acc[:], in0=stage[:, nh:nh + BLK], scalar1=wt[:, nh:nh + 1], scalar2=bt,
                                                  op0=ALU.mult, op1=ALU.add), reads=[sname], writes=[aname])
            for k in range(nh - 1, -1, -1):
                c.op('dve', lambda e: e.scalar_tensor_tensor(out=acc[:], in0=stage[:, k:k + BLK], scalar=wt[:, k:k + 1], in1=acc[:],
                                                        op0=ALU.mult, op1=ALU.add), reads=[sname, sname + 'h', aname], writes=[aname])

        def dump(name, ap_sb, res):
            if name in dbg_d:
                c.dma('sp', dbg_d[name], ap_sb, reads=res, key='dbg')

        nchunk = 0
        for seq in range(NSEQ):
            for blk in range(NB):
                tok0 = seq * S + blk * BLK
                c.dma('sp', xbuf[:], x_d[tok0:tok0 + BLK, :].rearrange("(t p) f -> p t f", p=128),
                      writes=['xbuf%d' % t for t in range(TB)], key='xld')
                if blk == 0:
                    c.op('pool', lambda e: e.memset(state[:], 0.0), writes=['state%d' % g for g in range(4)])
                    c.op('pool', lambda e: e.memset(state_b[:], 0.0), writes=['stateb%d' % g for g in range(4)])
                    c.op('pool', lambda e: e.memset(halo_s[:], 0.0), writes=['halo_s%d' % j for j in range(24)])
                    c.op('pool', lambda e: e.memset(halo_f[:], 0.0), writes=['halo_f%d' % j for j in range(44)])
                with ExitStack() as esA:
                    def sbA(name, shape, dt=F32):
                        return esA.enter_context(nc.sbuf_tensor(un(name), list(shape), dt))
                    xn_bufs = [sbA("xnA%d" % i, [128, 1024], BF16) for i in range(2)]
                    xbcT = sbA("xbcT", [128, 24, BLK], BF16)
                    szT = sbA("szT", [128, 16, BLK], BF16)
                    ynT = sbA("ynT", [128, 16, BLK], BF16)
                    stg = [sbA("stg%d" % i, [128, BLK + 3]) for i in range(3)]
                    accs = [sbA("acc%d" % i, [128, BLK]) for i in range(2)]
                    dtt = sbA("dtt", [128, TB, 32]); a_tok = sbA("a_tok", [128, TB, 32])
                    rmsnorm_to_hT("p_gmix", xn_bufs)
                    dump("hT", hT[:, 0, :], ['hT'])
                    wv, wr = wload("w_in", 128, 8, [(OFF_DT, 448)])
                    for t in range(TB):
                        for kt in range(8):
                            c.mm(lambda e: e.matmul(PB(2 + t % 2, 0, 448), lhsT=hT[:, kt, t * 128:(t + 1) * 128], rhs=wv[:, kt, :],
                                                    start=(kt == 0), stop=(kt == 7)), reads=['hT', wr], writes=['P%d' % (2 + t % 2)], last=(kt == 7))
                        c.op('act', lambda e: e.activation(out=small[:, t, :], in_=PB(2 + t % 2, 0, 448), func=AF.Copy),
                             reads=['P%d' % (2 + t % 2)], writes=['small%d' % t])
                    allsmall = ['small%d' % t for t in range(TB)]
                    c.op('dve', lambda e: e.tensor_tensor(out=dtt[:], in0=small[:, :, 0:32], in1=PR["p_dtb"][:, :].unsqueeze(1).to_broadcast([128, TB, 32]),
                                                          op=ALU.add), reads=allsmall + ['p_dtb'], writes=['dtt'])
                    c.op('act', lambda e: e.activation(out=dtt[:], in_=dtt[:], func=AF.Exp), reads=['dtt'], writes=['dtt'])
                    c.op('act', lambda e: e.activation(out=dtt[:], in_=dtt[:], func=AF.Ln, bias=1.0, scale=1.0), reads=['dtt'], writes=['dtt'])
                    c.op('dve', lambda e: e.tensor_tensor(out=a_tok[:], in0=dtt[:], in1=A_rep[:, :].unsqueeze(1).to_broadcast([128, TB, 32]),
                                                          op=ALU.mult), reads=['dtt', 'A_rep'], writes=['a_tok'])
                    for ch in range(6):
                        wv, wr = wload("w_in", 128, 8, [(OFF_XBC + ch * 512, 512)])
                        for jj in range(4):
                            j = ch * 4 + jj
                            pb = j % 2
                            for kt in range(8):
                                c.mm(lambda e: e.matmul(PB(pb, 0, BLK), lhsT=wv[:, kt, jj * 128:(jj + 1) * 128], rhs=hT[:, kt, :],
                                                        start=(kt == 0), stop=(kt == 7)), reads=['hT', wr], writes=['P%d' % pb], last=(kt == 7))
                            acc = accs[j % 2]
                            conv_tile(PB(pb, 0, BLK), stg[j % 3], halo_s[:, j, :], 3, PR["p_cw_ssd"][:, j, :], PR["p_cb_ssd"][:, j:j + 1],
                                      acc, 'stg%d' % (j % 3), 'acc%d' % (j % 2), 'halo_s%d' % j, 'P%d' % pb)
                            c.op('act', lambda e: e.activation(out=xbcT[:, j, :], in_=acc[:], func=AF.Silu), reads=['acc%d' % (j % 2)], writes=['xbcT%d' % j])
                    dump("xbcT", xbcT[:, 0, :], ['xbcT0'])
                    for ch in range(4):
                        wv, wr = wload("w_in", 128, 8, [(ch * 512, 512)])
                        for jj in range(4):
                            j = ch * 4 + jj
                            pb = j % 2
                            for kt in range(8):
                                c.mm(lambda e: e.matmul(PB(pb, 0, BLK), lhsT=wv[:, kt, jj * 128:(jj + 1) * 128], rhs=hT[:, kt, :],
                                                        start=(kt == 0), stop=(kt == 7)), reads=['hT', wr], writes=['P%d' % pb], last=(kt == 7))
                            c.op('act', lambda e: e.activation(out=szT[:, j, :], in_=PB(pb, 0, BLK), func=AF.Silu), reads=['P%d' % pb], writes=['szT%d' % j])
                    with ExitStack() as esS:
                        def sbS(name, shape, dt=F32):
                            return esS.enter_context(nc.sbuf_tensor(un(name), list(shape), dt))
                        eac = sbS("eac", [128, 32]); dte = sbS("dte", [128, 32]); cdr = sbS("cdr", [128, 32]); nac = sbS("nac", [128, 32])
                        dtdte = sbS("dtdte", [128, 32])
                        hl = sbS("hl", [32, 2, 128], BF16); rres = sbS("rres", [32, 128])
                        sc = sbS("sc", [128, 4, 128])
                        xdt = sbS("xdt", [128, 32, 64], BF16); xdtd = sbS("xdtd", [128, 32, 64], BF16)
                        btok = sbS("btok", [128, 4, 128], BF16)
                        dec = [sbS("dec%d" % i, [128, 8, 128]) for i in range(2)]
                        LT = [sbS("LT%d" % i, [128, 8, 128], BF16) for i in range(2)]
                        yoff = sbS("yoff", [128, 8, 64], BF16)
                        tt = sbS("tt", [128, 16, 128]); sq = sbS("sq", [128, 16, 128], BF16)
                        xd = sbS("xd", [128, 4, 128]); sttmp = sbS("sttmp", [128, 512]); rrr = sbS("rrr", [128, 4, 128])
                        for t in range(TB):
                            cs = slice(t * 128, (t + 1) * 128)
                            par = nchunk % 2
                            nchunk += 1
                            a_c = a_tok[:, t, :]
                            c.mm(lambda e: e.matmul(PB(2, 0, 32), lhsT=triI[:], rhs=a_c, start=True, stop=True), reads=['a_tok', 'triI'], writes=['P2'], last=False)
                            c.mm(lambda e: e.matmul(PB(2, 32, 64), lhsT=triS[:], rhs=a_c, start=True, stop=True), reads=['a_tok', 'triS'], writes=['P2'], last=False)
                            c.mm(lambda e: e.matmul(PB(2, 64, 96), lhsT=ones_f[:], rhs=a_c, start=True, stop=True), reads=['a_tok', 'ones_f'], writes=['P2'], last=False)
                            c.mm(lambda e: e.matmul(PB(2, 128, 256, 0, 32), lhsT=a_c, rhs=triI[:], start=True, stop=True), reads=['a_tok', 'triI'], writes=['P2'], last=True)
                            c.op('act', lambda e: e.activation(out=eac[:], in_=PB(2, 0, 32), func=AF.Exp), reads=['P2'], writes=['eac'])
                            c.op('act', lambda e: e.activation(out=dte[:], in_=PB(2, 32, 64), func=AF.Exp), reads=['P2'], writes=['dte'])
                            c.op('act', lambda e: e.activation(out=cdr[:], in_=PB(2, 64, 96), func=AF.Exp), reads=['P2'], writes=['cdr'])
                            c.op('dve', lambda e: e.tensor_scalar(out=nac[:], in0=PB(2, 0, 32), scalar1=-1.0, scalar2=None, op0=ALU.mult), reads=['P2'], writes=['nac'])
                            c.op('dve', lambda e: e.tensor_tensor(out=dtdte[:], in0=dtt[:, t, :], in1=dte[:], op=ALU.mult), reads=['dtt', 'dte'], writes=['dtdte'])
                            c.op('dve', lambda e: e.tensor_copy(out=hl[:, 0, :], in_=PB(2, 128, 256, 0, 32)), reads=['P2'], writes=['hl0'])
                            c.op('dve', lambda e: e.tensor_tensor(out=rres[:], in0=PB(2, 128, 256, 0, 32), in1=hl[:, 0, :], op=ALU.subtract), reads=['P2', 'hl0'], writes=['rres'])
                            c.op('dve', lambda e: e.tensor_copy(out=hl[:, 1, :], in_=rres[:]), reads=['rres'], writes=['hl1'])
                            c.dma('pool', acs_d[par], hl[:], reads=['hl0', 'hl1'], writes=['acs%d' % par], key='acs_w')
                            c.dma('sp', rhs2[:, :].rearrange("j (h l) -> j h l", h=32), acs_d[par].rearrange("h j l -> j h l"), reads=['acs%d' % par], writes=['rhs2'], key='acs_r')
                            for g in range(4):
                                c.mm(lambda e: e.matmul(PB(3, g * 128, (g + 1) * 128), lhsT=xbcT[:, 16 + g, cs], rhs=xbcT[:, 20 + g, cs], start=True, stop=True),
                                     reads=['xbcT%d' % (16 + g), 'xbcT%d' % (20 + g)], writes=['P3'], last=(g == 3))
                            c.op('act', lambda e: e.activation(out=sc[:], in_=PB(3).rearrange("p (g n) -> p g n", g=4), func=AF.Copy), reads=['P3'], writes=['sc'])
                            xtp = pall[:, 4 * 512:6 * 512].bitcast(BF16)
                            for i in range(16):
                                c.mm(lambda e: e.transpose(out=xtp[:, i * 128:(i + 1) * 128], in_=xbcT[:, i, cs], identity=ident_b[:]),
                                     reads=['xbcT%d' % i, 'ident_b'], writes=['P4', 'P5'], last=(i == 15))
                            xtp3 = xtp.rearrange("p (h d) -> p h d", h=32)
                            c.op('dve', lambda e: e.tensor_tensor(out=xdt[:], in0=xtp3, in1=dtt[:, t, :].unsqueeze(2).to_broadcast([128, 32, 64]), op=ALU.mult),
                                 reads=['P4', 'P5', 'dtt'], writes=['xdt'])
                            c.op('dve', lambda e: e.tensor_tensor(out=xdtd[:], in0=xtp3, in1=dtdte[:, :].unsqueeze(2).to_broadcast([128, 32, 64]), op=ALU.mult),
                                 reads=['P4', 'P5', 'dtdte'], writes=['xdtd'])
                            btp = PBh(6)
                            for g in range(4):
                                c.mm(lambda e: e.transpose(out=btp[:, g * 128:(g + 1) * 128], in_=xbcT[:, 16 + g, cs], identity=ident_b[:]),
                                     reads=['xbcT%d' % (16 + g), 'ident_b'], writes=['P6'], last=(g == 3))
                            c.op('act', lambda e: e.activation(out=btok[:], in_=btp[:, 0:512].rearrange("p (g n) -> p g n", g=4), func=AF.Copy), reads=['P6'], writes=['btok'])
                            for g in range(4):
                                d_ = dec[g % 2]; L_ = LT[g % 2]
                                dn = 'dec%d' % (g % 2); Ln_ = 'LT%d' % (g % 2)
                                for half in range(2):
                                    h0 = g * 8 + half * 4
                                    c.mm(lambda e: e.matmul(PB(half), lhsT=ones2[:, :], rhs=rhs2[:, h0 * 128:(h0 + 4) * 128], start=True, stop=False),
                                         reads=['rhs2', 'ones2'], writes=['P%d' % half], last=False)
                                    c.mm(lambda e: e.matmul(PB(half), lhsT=ident_b[:], rhs=negm_b[:], start=False, stop=True),
                                         reads=['ident_b', 'negm_b'], writes=['P%d' % half], last=True)
                                for hh in range(8):
                                    h = g * 8 + hh
                                    c.op('act', lambda e: e.activation(out=d_[:, hh, :], in_=PB(hh // 4, (hh % 4) * 128, (hh % 4 + 1) * 128), func=AF.Exp,
                                                                       bias=nac[:, h:h + 1], scale=1.0), reads=['P%d' % (hh // 4), 'nac'], writes=[dn])
                                c.any2(lambda e: e.tensor_tensor(out=L_[:], in0=d_[:], in1=sc[:, g, :].unsqueeze(1).to_broadcast([128, 8, 128]), op=ALU.mult),
                                       reads=[dn, 'sc'], writes=[Ln_])
                                c.mm(lambda e: e.matmul(PB(6), lhsT=xbcT[:, 20 + g, cs], rhs=state_b[:, g, :], start=True, stop=True),
                                     reads=['xbcT%d' % (20 + g), 'stateb%d' % g], writes=['P6'], last=True)
                                c.op('dve', lambda e: e.tensor_tensor(out=yoff[:], in0=PB(6).rearrange("p (h d) -> p h d", h=8),
                                                                      in1=eac[:, g * 8:(g + 1) * 8].unsqueeze(2).to_broadcast([128, 8, 64]), op=ALU.mult),
                                     reads=['P6', 'eac'], writes=['yoff'])
                                yof = yoff[:].rearrange("p h d -> p (h d)")
                                for il in range(4):
                                    c.mm(lambda e: e.matmul(PB(7, il * 128, (il + 1) * 128), lhsT=yof[:, il * 128:(il + 1) * 128], rhs=ident_b[:],
                                                            start=True, stop=False), reads=['yoff', 'ident_b'], writes=['P7'], last=False)
                                    for hf in range(2):
                                        hh = il * 2 + hf
                                        h = g * 8 + hh
                                        c.mm(lambda e: e.matmul(PB(7, il * 128, (il + 1) * 128, hf * 64, hf * 64 + 64), lhsT=xdt[:, h, :], rhs=L_[:, hh, :],
                                                                start=False, stop=(hf == 1), tile_position=(0, hf * 64)),
                                             reads=['xdt', Ln_], writes=['P7'], last=(hh == 7))
                                tl = ['xbcT%d' % (g * 4 + i) for i in range(4)]
                                c.op('pool', lambda e: e.tensor_tensor(out=xd[:], in0=xbcT[:, g * 4:(g + 1) * 4, cs],
                                                                       in1=PR["p_dskip"][:, g * 4:(g + 1) * 4].unsqueeze(2).to_broadcast([128, 4, 128]), op=ALU.mult),
                                     reads=tl + ['p_dskip'], writes=['xd'])
                                c.op('dve', lambda e: e.tensor_tensor(out=tt[:, g * 4:(g + 1) * 4, :], in0=PB(7).rearrange("p (i n) -> p i n", i=4), in1=xd[:], op=ALU.add),
                                     reads=['P7', 'xd'], writes=['tt%d' % g])
                                c.op('pool', lambda e: e.tensor_tensor(out=tt[:, g * 4:(g + 1) * 4, :], in0=tt[:, g * 4:(g + 1) * 4, :], in1=szT[:, g * 4:(g + 1) * 4, cs], op=ALU.mult),
                                     reads=['tt%d' % g] + ['szT%d' % (g * 4 + i) for i in range(4)], writes=['tt%d' % g])
                                c.mm(lambda e: e.matmul(PB(6), lhsT=btok[:, g, :], rhs=xdtd[:, g * 8:(g + 1) * 8, :].rearrange("p h d -> p (h d)"), start=True, stop=True),
                                     reads=['btok', 'xdtd'], writes=['P6'], last=True)
                                c.op('pool', lambda e: e.tensor_tensor(out=sttmp[:].rearrange("p (h d) -> p h d", h=8), in0=state[:, g, :].rearrange("p (h d) -> p h d", h=8),
                                                                       in1=cdr[:, g * 8:(g + 1) * 8].unsqueeze(2).to_broadcast([128, 8, 64]), op=ALU.mult),
                                     reads=['state%d' % g, 'cdr'], writes=['sttmp'])
                                c.op('dve', lambda e: e.tensor_tensor(out=state[:, g, :], in0=PB(6), in1=sttmp[:], op=ALU.add), reads=['P6', 'sttmp'], writes=['state%d' % g])
                                c.op('act', lambda e: e.activation(out=state_b[:, g, :], in_=state[:, g, :], func=AF.Copy), reads=['state%d' % g], writes=['stateb%d' % g])
                            allt = ['tt%d' % g for g in range(4)]
                            c.op('act', lambda e: e.activation(out=sq[:], in_=tt[:], func=AF.Square), reads=allt, writes=['sq'])
                            for g in range(4):
                                for il in range(4):
                                    c.mm(lambda e: e.matmul(PB(3, g * 128, (g + 1) * 128), lhsT=ones_b[:], rhs=sq[:, g * 4 + il, :], start=(il == 0), stop=(il == 3)),
                                         reads=['sq', 'ones_b'], writes=['P3'], last=(g == 3 and il == 3))
                            rstd(PB(3).rearrange("p (g n) -> p g n", g=4), rrr[:], 512, ['P3'], ['rrr'])
                            c.op('pool', lambda e: e.tensor_tensor(out=tt[:], in0=tt[:], in1=PR["p_gssd"][:, :].unsqueeze(2).to_broadcast([128, 16, 128]), op=ALU.mult),
                                 reads=allt + ['p_gssd'], writes=allt)
                            for g in range(4):
                                c.op('dve', lambda e: e.tensor_tensor(out=ynT[:, g * 4:(g + 1) * 4, cs], in0=tt[:, g * 4:(g + 1) * 4, :],
                                                                      in1=rrr[:, g, :].unsqueeze(1).to_broadcast([128, 4, 128]), op=ALU.mult),
                                     reads=['tt%d' % g, 'rrr'], writes=['ynT'])
                    dump("ynT", ynT[:, 0, :], ['ynT'])
                    dump("state0", state[:, 0, :], ['state0'])
                    for mb in range(4):
                        wv, wr = wload("w_ssd_proj", 128, 16, [(mb * 256, 256)])
                        wg, wgr = wload("w_in", 128, 8, [(OFF_G + mb * 256, 256)])
                        for mm_ in range(2):
                            m = mb * 2 + mm_
                            for kt in range(16):
                                c.mm(lambda e: e.matmul(PB(0, 0, BLK), lhsT=wv[:, kt, mm_ * 128:(mm_ + 1) * 128], rhs=ynT[:, kt, :], start=(kt == 0), stop=(kt == 15)),
                                     reads=['ynT', wr], writes=['P0'], last=(kt == 15))
                            for kt in range(8):
                                c.mm(lambda e: e.matmul(PB(1, 0, BLK), lhsT=wg[:, kt, mm_ * 128:(mm_ + 1) * 128], rhs=hT[:, kt, :], start=(kt == 0), stop=(kt == 7)),
                                     reads=['hT', wgr], writes=['P1'], last=(kt == 7))
                            gsb = accs[m % 2]
                            c.op('act', lambda e: e.activation(out=gsb[:], in_=PB(1, 0, BLK), func=AF.Sigmoid, bias=PR["p_gateb"][:, m:m + 1], scale=1.0),
                                 reads=['P1', 'p_gateb'], writes=['acc%d' % (m % 2)])
                            c.op('dve', lambda e: e.tensor_tensor(out=mix[:, m, :], in0=PB(0, 0, BLK), in1=gsb[:], op=ALU.mult),
                                 reads=['P0', 'acc%d' % (m % 2)], writes=['mix%d' % m])
                c.barrier()
                dump("mix_ssd", mix[:, 0, :], ['mix0'])
                if stop_after == 'A':
                    continue
                SCALE = 96.0 ** -0.5
                with ExitStack() as esA:
                    def sbB(name, shape, dt=F32):
                        return esA.enter_context(nc.sbuf_tensor(un(name), list(shape), dt))
                    qkT = sbB("qkT", [128, 3, BLK], BF16)
                    qT = sbB("qT", [96, 16, BLK], BF16); kTc = sbB("kTc", [96, 16, BLK], BF16)
                    attnT = sbB("attnT", [64, 16, BLK], BF16)
                    qan = sbB("qan", [128, 384], BF16)
                    ssq = sbB("ssq", [128, 2 * TB]); rq = sbB("rq", [128, 2 * TB])
                    qsb = sbB("qsb", [128, 16, 96]); qtmp = sbB("qtmp", [128, 16, 96])
                    ssh = sbB("ssh", [128, 16]); rh = sbB("rh", [128, 16])
                    qr = sbB("qr", [128, 16, 96], BF16)
                    r1 = sbB("r1", [128, 16, 16]); r2 = sbB("r2", [128, 16, 16])
                    pts = [sbB("pt%d" % i, [128, BLK], BF16) for i in range(3)]
                    kst2 = [sbB("kstp%d" % i, [96, 2, max(1, NB - 1) * BLK], BF16) for i in range(2)]
                    rD = sbB("rD", [64, BLK])
                    gsb2 = [sbB("gsb%d" % i, [128, BLK]) for i in range(2)]
                    tmpm = sbB("tmpm", [128, BLK])
                    wuq, wuqr = wload("w_uq", 128, 2, [(0, 1536)])
                    wukv, wukvr = wload("w_ukv", 128, 1, [(0, 2048)])
                    for t in range(TB):
                        tg = blk * TB + t
                        c.op('act', lambda e: e.activation(out=junk[:, 0:256], in_=small[:, t, 32:288], func=AF.Square, accum_out=ssq[:, 2 * t:2 * t + 1]),
                             reads=['small%d' % t], writes=['junk', 'ssq%d' % t])
                        c.op('act', lambda e: e.activation(out=junk[:, 256:384], in_=small[:, t, 288:416], func=AF.Square, accum_out=ssq[:, 2 * t + 1:2 * t + 2]),
                             reads=['small%d' % t], writes=['junk', 'ssq%d' % t])
                        rstd(ssq[:, 2 * t:2 * t + 1], rq[:, 2 * t:2 * t + 1], 256, ['ssq%d' % t], ['rq%da' % t])
                        rstd(ssq[:, 2 * t + 1:2 * t + 2], rq[:, 2 * t + 1:2 * t + 2], 128, ['ssq%d' % t], ['rq%db' % t])
                        c.op('dve', lambda e: e.scalar_tensor_tensor(out=qan[:, 0:256], in0=small[:, t, 32:288], scalar=rq[:, 2 * t:2 * t + 1], in1=PR["p_gqa"][:],
                                                                     op0=ALU.mult, op1=ALU.mult), reads=['small%d' % t, 'rq%da' % t, 'p_gqa'], writes=['qan'])
                        c.op('dve', lambda e: e.scalar_tensor_tensor(out=qan[:, 256:384], in0=small[:, t, 288:416], scalar=rq[:, 2 * t + 1:2 * t + 2], in1=PR["p_gkva"][:],
                                                                     op0=ALU.mult, op1=ALU.mult), reads=['small%d' % t, 'rq%db' % t, 'p_gkva'], writes=['qan'])
                        pt_ = PBh(0)
                        for i in range(3):
                            c.mm(lambda e: e.transpose(out=pt_[:, i * 128:(i + 1) * 128], in_=qan[:, i * 128:(i + 1) * 128], identity=ident_b[:]),
                                 reads=['qan', 'ident_b'], writes=['P0'], last=(i == 2))
                        c.op('act', lambda e: e.activation(out=qkT[:, :, t * 128:(t + 1) * 128], in_=pt_[:, 0:384].rearrange("p (k n) -> p k n", k=3), func=AF.Copy),
                             reads=['P0'], writes=['qkT'])
                        for nb in range(3):
                            for kt in range(2):
                                c.mm(lambda e: e.matmul(PB(1 + nb), lhsT=qkT[:, kt, t * 128:(t + 1) * 128], rhs=wuq[:, kt, nb * 512:(nb + 1) * 512], start=(kt == 0), stop=(kt == 1)),
                                     reads=['qkT', wuqr], writes=['P%d' % (1 + nb)], last=(kt == 1))
                        for nb in range(4):
                            c.mm(lambda e: e.matmul(PB(4 + nb), lhsT=qkT[:, 2, t * 128:(t + 1) * 128], rhs=wukv[:, 0, nb * 512:(nb + 1) * 512], start=True, stop=True),
                                 reads=['qkT', wukvr], writes=['P%d' % (4 + nb)], last=True)
                        kvp = pall[:, 4 * 512:8 * 512].rearrange("p (h d) -> p h d", h=16)
                        c.op('act', lambda e: e.activation(out=Vc[:, tg, :, :], in_=kvp[:, :, 64:128], func=AF.Copy), reads=['P4', 'P5', 'P6', 'P7'], writes=['Vc%d' % tg])
                        for which in range(2):
                            gname = "p_gq" if which == 0 else "p_gk"
                            if which == 0:
                                c.op('act', lambda e: e.activation(out=qsb[:], in_=pall[:, 512:4 * 512].rearrange("p (h d) -> p h d", h=16), func=AF.Copy),
                                     reads=['P1', 'P2', 'P3'], writes=['qsb'])
                            else:
                                c.op('act', lambda e: e.activation(out=qsb[:, :, 0:64], in_=kvp[:, :, 0:64], func=AF.Copy), reads=['P4', 'P5', 'P6', 'P7'], writes=['qsb'])
                                c.op('pool', lambda e: e.tensor_copy(out=qsb[:, :, 64:96], in_=small[:, t, 416:448].unsqueeze(1).to_broadcast([128, 16, 32])),
                                     reads=['small%d' % t, 'qsb'], writes=['qsb'])
                            c.op('pool', lambda e: e.tensor_tensor(out=qtmp[:], in0=qsb[:], in1=qsb[:], op=ALU.mult), reads=['qsb'], writes=['qtmp'])
                            c.op('dve', lambda e: e.tensor_reduce(out=ssh[:], in_=qtmp[:], axis=AX.X, op=ALU.add), reads=['qtmp'], writes=['ssh'])
                            rstd(ssh[:], rh[:], 96, ['ssh'], ['rh'])
                            c.op('dve', lambda e: e.tensor_tensor(out=qtmp[:], in0=qsb[:], in1=rh[:, :].unsqueeze(2).to_broadcast([128, 16, 96]), op=ALU.mult),
                                 reads=['qsb', 'rh'], writes=['qtmp'])
                            c.op('pool', lambda e: e.tensor_tensor(out=qtmp[:], in0=qtmp[:], in1=PR[gname][:, :].unsqueeze(1).to_broadcast([128, 16, 96]), op=ALU.mult),
                                 reads=['qtmp', gname], writes=['qtmp'])
                            x1 = qtmp[:, :, 64:80]; x2 = qtmp[:, :, 80:96]
                            cb = cos_t[:, tg, :].unsqueeze(1).to_broadcast([128, 16, 16]); sb_ = sin_t[:, tg, :].unsqueeze(1).to_broadcast([128, 16, 16])
                            c.op('dve', lambda e: e.tensor_tensor(out=r1[:], in0=x1, in1=cb, op=ALU.mult), reads=['qtmp', 'cos'], writes=['r1'])
                            c.op('pool', lambda e: e.tensor_tensor(out=r2[:], in0=x2, in1=sb_, op=ALU.mult), reads=['qtmp', 'sin'], writes=['r2'])
                            c.op('dve', lambda e: e.tensor_tensor(out=qr[:, :, 64:80], in0=r1[:], in1=r2[:], op=ALU.subtract), reads=['r1', 'r2'], writes=['qr'])
                            c.op('dve', lambda e: e.tensor_tensor(out=r1[:], in0=x1, in1=sb_, op=ALU.mult), reads=['qtmp', 'sin', 'qr'], writes=['r1'])
                            c.op('pool', lambda e: e.tensor_tensor(out=r2[:], in0=x2, in1=cb, op=ALU.mult), reads=['qtmp', 'cos', 'qr'], writes=['r2'])
                            c.op('dve', lambda e: e.tensor_tensor(out=qr[:, :, 80:96], in0=r1[:], in1=r2[:], op=ALU.add), reads=['r1', 'r2'], writes=['qr'])
                            c.op('act', lambda e: e.activation(out=qr[:, :, 0:64], in_=qtmp[:, :, 0:64], func=AF.Copy), reads=['qtmp'], writes=['qr'])
                            tp = pall[:, 0:1024].bitcast(BF16) if which == 0 else pall[:, 1024:2048].bitcast(BF16)
                            tres = ['P0', 'P1'] if which == 0 else ['P2', 'P3']
                            for h in range(16):
                                c.mm(lambda e: e.transpose(out=tp[0:96, h * 128:(h + 1) * 128], in_=qr[:, h, :], identity=ident_b[:]),
                                     reads=['qr', 'ident_b'], writes=tres, last=(h == 15))
                            dst = qT if which == 0 else kTc
                            c.op('act' if which == 0 else 'dve', (lambda e: e.activation(out=dst[:, :, t * 128:(t + 1) * 128], in_=tp[0:96, :].rearrange("p (h n) -> p h n", h=16), func=AF.Copy))
                                 if which == 0 else (lambda e: e.tensor_copy(out=dst[:, :, t * 128:(t + 1) * 128], in_=tp[0:96, :].rearrange("p (h n) -> p h n", h=16))),
                                 reads=tres, writes=['qT' if which == 0 else 'kTc'])
                    if blk < NB - 1:
                        c.dma('pool', kc_d[:, :, blk * BLK:(blk + 1) * BLK].rearrange("h d s -> d h s"), kTc[:], reads=['kTc'], writes=['kc%d' % blk], key='kcw')
                    def kissue(hp):
                        if blk == 0 or hp >= 8:
                            return
                        c.dma('sp', kst2[hp % 2][:, :, 0:blk * BLK], kc_d[2 * hp:2 * hp + 2, :, 0:blk * BLK].rearrange("h d s -> d h s"),
                              reads=['kc%d' % bp_ for bp_ in range(blk)], writes=['kstp%d' % (hp % 2)], key='kst%d' % (hp % 2))
                    kissue(0)
                    for h in range(16):
                        if h % 2 == 0:
                            kissue(h // 2 + 1)
                        ob = 4 + (h % 2) * 2
                        ktiles = []
                        for bp in range(blk):
                            for j in range(TB):
                                ktiles.append((kst2[(h // 2) % 2][:, h % 2, bp * BLK + j * 128: bp * BLK + (j + 1) * 128], 'kstp%d' % ((h // 2) % 2), bp * TB + j, 0, False, None))
                        for j in range(TB):
                            ktiles.append((kTc[:, h, j * 128:(j + 1) * 128], 'kTc', blk * TB + j, j * 128, True, None))
                        for idx, (kap, kres, tgk, q0, diag, ci) in enumerate(ktiles):
                            first = idx == 0; lastk = idx == len(ktiles) - 1
                            sbk = idx % 2
                            p_ = pts[idx % 3]; pn = 'pt%d' % (idx % 3)
                            c.mm(lambda e: e.matmul(PB(sbk, q0, BLK), lhsT=kap, rhs=qT[:, h, q0:BLK], start=True, stop=True), reads=[kres, 'qT'], writes=['P%d' % sbk], last=True)
                            c.op('act', lambda e: e.activation(out=p_[:, q0:BLK], in_=PB(sbk, q0, BLK), func=AF.Exp, scale=SCALE), reads=['P%d' % sbk], writes=[pn])
                            if diag:
                                c.op('pool', lambda e: e.tensor_tensor(out=p_[:, q0:q0 + 128], in0=p_[:, q0:q0 + 128], in1=cmask_b[:], op=ALU.mult), reads=[pn, 'cmask_b'], writes=[pn])
                            c.mm(lambda e: e.matmul(PB(ob, q0, BLK, 0, 64), lhsT=Vc[:, tgk, h, :], rhs=p_[:, q0:BLK], start=first, stop=lastk),
                                 reads=['Vc%d' % tgk, pn], writes=['P%d' % ob], last=lastk)
                            c.mm(lambda e: e.matmul(PB(ob + 1, q0, BLK, 0, 64), lhsT=ones_b[:, 0:64], rhs=p_[:, q0:BLK], start=first, stop=lastk),
                                 reads=['ones_b', pn], writes=['P%d' % (ob + 1)], last=lastk)
                        c.op('dve', lambda e: e.reciprocal(out=rD[:], in_=PB(ob + 1, 0, BLK, 0, 64)), reads=['P%d' % (ob + 1)], writes=['rD'])
                        c.op('dve', lambda e: e.tensor_tensor(out=attnT[:, h, :], in0=PB(ob, 0, BLK, 0, 64), in1=rD[:], op=ALU.mult), reads=['P%d' % ob, 'rD'], writes=['attnT'])
                    dump("attnT", attnT[:, 0, :], ['attnT'])
                    dump("qT", qT[:, 0, :], ['qT'])
                    dump("kT", kTc[:, 0, :], ['kTc'])
                    mixb = sbB("mixb", [128, 8, BLK], BF16)
                    for mb in range(4):
                        wv, wr = wload("w_mla_proj", 64, 16, [(mb * 256, 256)])
                        wg, wgr = wload("w_in", 128, 8, [(OFF_G + 1024 + mb * 256, 256)])
                        for mm_ in range(2):
                            m = mb * 2 + mm_
                            for kt in range(16):
                                c.mm(lambda e: e.matmul(PB(0, 0, BLK), lhsT=wv[:, kt, mm_ * 128:(mm_ + 1) * 128], rhs=attnT[:, kt, :], start=(kt == 0), stop=(kt == 15)),
                                     reads=['attnT', wr], writes=['P0'], last=(kt == 15))
                            for kt in range(8):
                                c.mm(lambda e: e.matmul(PB(1, 0, BLK), lhsT=wg[:, kt, mm_ * 128:(mm_ + 1) * 128], rhs=hT[:, kt, :], start=(kt == 0), stop=(kt == 7)),
                                     reads=['hT', wgr], writes=['P1'], last=(kt == 7))
                            gs = gsb2[m % 2]
                            c.op('act', lambda e: e.activation(out=gs[:], in_=PB(1, 0, BLK), func=AF.Sigmoid, bias=PR["p_gateb"][:, 8 + m:9 + m], scale=1.0),
                                 reads=['P1', 'p_gateb'], writes=['gsb%d' % (m % 2)])
                            c.op('dve', lambda e: e.tensor_tensor(out=tmpm[:], in0=PB(0, 0, BLK), in1=gs[:], op=ALU.mult), reads=['P0', 'gsb%d' % (m % 2)], writes=['tmpm'])
                            c.op('pool', lambda e: e.tensor_tensor(out=mixb[:, m, :], in0=tmpm[:], in1=mix[:, m, :], op=ALU.add), reads=['tmpm', 'mix%d' % m], writes=['mixb'])
                    for nb in range(2):
                        wv, wr = wload("w_o", 128, 8, [(nb * 512, 512)])
                        for t in range(TB):
                            pb = 2 + t % 2
                            for kt in range(8):
                                c.mm(lambda e: e.matmul(PB(pb), lhsT=mixb[:, kt, t * 128:(t + 1) * 128], rhs=wv[:, kt, :], start=(kt == 0), stop=(kt == 7)),
                                     reads=['mixb', wr], writes=['P%d' % pb], last=(kt == 7))
                            c.op('dve', lambda e: e.tensor_tensor(out=xbuf[:, t, nb * 512:(nb + 1) * 512], in0=PB(pb), in1=xbuf[:, t, nb * 512:(nb + 1) * 512], op=ALU.add),
                                 reads=['P%d' % pb, 'xbuf%d' % t], writes=['xbuf%d' % t])
                dump("x1", xbuf[:, 0, :], ['xbuf0'])
                c.barrier()
                with ExitStack() as esA:
                    def sbC(name, shape, dt=F32):
                        return esA.enter_context(nc.sbuf_tensor(un(name), list(shape), dt))
                    xn_bufs = [sbC("xnC%d" % i, [128, 1024], BF16) for i in range(2)]
                    actT = sbC("actT", [128, 22, BLK], BF16)
                    stg = [sbC("stgC%d" % i, [128, BLK + 2]) for i in range(4)]
                    accs = [sbC("accC%d" % i, [128, BLK]) for i in range(4)]
                    sg = [sbC("sg%d" % i, [128, BLK]) for i in range(2)]
                    rmsnorm_to_hT("p_gffn", xn_bufs)
                    for ch in range(11):
                        wv, wr = wload("w_up", 128, 8, [(ch * 256, 256), (D_FF + ch * 256, 256)])
                        for jj in range(2):
                            j = ch * 2 + jj
                            for half in range(2):
                                pb = half
                                jc = j + 22 * half
                                for kt in range(8):
                                    c.mm(lambda e: e.matmul(PB(pb, 0, BLK), lhsT=wv[:, kt, half * 256 + jj * 128: half * 256 + (jj + 1) * 128], rhs=hT[:, kt, :],
                                                            start=(kt == 0), stop=(kt == 7)), reads=['hT', wr], writes=['P%d' % pb], last=(kt == 7))
                                si = (j % 2) * 2 + half
                                conv_tile(PB(pb, 0, BLK), stg[si], halo_f[:, jc, :], 2, PR["p_cw_ffn"][:, jc, :], PR["p_cb_ffn"][:, jc:jc + 1],
                                          accs[si], 'stgC%d' % si, 'accC%d' % si, 'halo_f%d' % jc, 'P%d' % pb)
                            sgi = sg[j % 2]
                            c.op('act', lambda e: e.activation(out=sgi[:], in_=accs[(j % 2) * 2][:], func=AF.Silu), reads=['accC%d' % ((j % 2) * 2)], writes=['sg%d' % (j % 2)])
                            c.op('pool', lambda e: e.tensor_tensor(out=actT[:, j, :], in0=sgi[:], in1=accs[(j % 2) * 2 + 1][:], op=ALU.mult),
                                 reads=['sg%d' % (j % 2), 'accC%d' % ((j % 2) * 2 + 1)], writes=['actT'])
                    dump("actT", actT[:, 0, :], ['actT'])
                    for m4 in range(4):
                        for kh in range(2):
                            wv, wr = wload("w_down", 128, 11, [(m4 * 256, 256)], r0=kh * 1408)
                            for t in range(TB):
                                for kt in range(11):
                                    c.mm(lambda e: e.matmul(PB(2 + t, 0, 256), lhsT=actT[:, kh * 11 + kt, t * 128:(t + 1) * 128], rhs=wv[:, kt, :],
                                                            start=(kh == 0 and kt == 0), stop=(kh == 1 and kt == 10)),
                                         reads=['actT', wr], writes=['P%d' % (2 + t)], last=(kt == 10))
                        for t in range(TB):
                            c.op('dve', lambda e: e.tensor_tensor(out=xbuf[:, t, m4 * 256:(m4 + 1) * 256], in0=PB(2 + t, 0, 256),
                                                                  in1=xbuf[:, t, m4 * 256:(m4 + 1) * 256], op=ALU.add),
                                 reads=['P%d' % (2 + t), 'xbuf%d' % t], writes=['xbuf%d' % t])
                    c.dma('pool', out_d[tok0:tok0 + BLK, :].rearrange("(t p) f -> p t f", p=128), xbuf[:], reads=['xbuf%d' % t for t in range(TB)], key='ost')
                c.barrier()
        c.barrier()
        print("ninst", c.ninst, "cnt", c.cnt)
    return nc


def kernel(**inputs):
    x = np.asarray(inputs["x"], dtype=np.float32)
    B, S, D = x.shape
    NSEQ = B // NCORES
    nc = build(NSEQ=NSEQ, S=S, TB=2)
    consts = host_consts(S)
    params = host_params(inputs)
    maps = []
    for i in range(NCORES):
        m = {"x": np.ascontiguousarray(x[i * NSEQ:(i + 1) * NSEQ].reshape(NSEQ * S, D))}
        for k in WSHAPES:
            m[k] = np.ascontiguousarray(np.asarray(inputs[k], dtype=np.float32)[0])
        m.update(params)
        m.update(consts)
        maps.append(m)
    res = run_bass_kernel_spmd(nc, maps, core_ids=list(range(NCORES)))
    out = np.concatenate([np.asarray(r["out"]).reshape(NSEQ, S, D) for r in res.results], axis=0)
    return out.astype(np.float32)
```

```python
import numpy as np
from contextlib import ExitStack
import concourse.bass as bass
import concourse.mybir as mybir
from concourse.bass_utils import run_bass_kernel_spmd

F32 = mybir.dt.float32
BF16 = mybir.dt.bfloat16
AF = mybir.ActivationFunctionType
ALU = mybir.AluOpType
AX = mybir.AxisListType

D_MODEL = 1024
D_INNER = 2048
IN_COLS = 7616
D_FF = 2816
EPS = 1e-6
OFF_XBC = 2048
OFF_DT = 5120
OFF_G = 5568
NCORES = 8
WSHAPES = {
    "w_in": (1024, IN_COLS), "w_ssd_proj": (2048, 1024), "w_uq": (256, 1536), "w_ukv": (128, 2048),
    "w_mla_proj": (1024, 1024), "w_o": (1024, 1024), "w_up": (1024, 2 * D_FF), "w_down": (D_FF, 1024),
}
SLOT_ELEMS = 4096


def weight_chunks():
    ch = [("w_in", 128, 8, 0, ((OFF_DT, 448),))]
    ch += [("w_in", 128, 8, 0, ((OFF_XBC + i * 512, 512),)) for i in range(6)]
    ch += [("w_in", 128, 8, 0, ((i * 512, 512),)) for i in range(4)]
    for mb in range(4):
        ch += [("w_ssd_proj", 128, 16, 0, ((mb * 256, 256),)), ("w_in", 128, 8, 0, ((OFF_G + mb * 256, 256),))]
    ch += [("w_uq", 128, 2, 0, ((0, 1536),)), ("w_ukv", 128, 1, 0, ((0, 2048),))]
    for mb in range(4):
        ch += [("w_mla_proj", 64, 16, 0, ((mb * 256, 256),)), ("w_in", 128, 8, 0, ((OFF_G + 1024 + mb * 256, 256),))]
    ch += [("w_o", 128, 8, 0, ((nb * 512, 512),)) for nb in range(2)]
    ch += [("w_up", 128, 8, 0, ((i * 256, 256), (D_FF + i * 256, 256))) for i in range(11)]
    for m4 in range(4):
        for kh in range(2):
            ch.append(("w_down", 128, 11, kh * 1408, ((m4 * 256, 256),)))
    return ch
_UID = [0]


def un(name):
    _UID[0] += 1
    return "%s_u%d" % (name, _UID[0])

NSLOT = 4


class Ctx:
    def __init__(self, nc, es):
        self.nc = nc
        self.es = es
        self.engs = {'pe': nc.tensor, 'act': nc.scalar, 'dve': nc.vector, 'pool': nc.gpsimd, 'sp': nc.sync}
        self.sem = {}
        for k in ('pe', 'act', 'dve', 'pool'):
            self.sem[k] = es.enter_context(nc.semaphore("s_" + k))
        self.cnt = {k: 0 for k in self.sem}
        self.waited = {k: {} for k in self.engs}
        self.lastw = {}
        self.readers = {}
        self.dsem = {}
        self.dcnt = {}
        self.ninst = {k: 0 for k in self.engs}
        self.rr = 0

    def sb(self, name, shape, dt=F32):
        return self.es.enter_context(self.nc.sbuf_tensor(name, list(shape), dt))

    def _deps(self, reads, writes):
        deps = {}

        def add(d):
            if d is None:
                return
            k, v = d
            if deps.get(k, 0) < v:
                deps[k] = v
        for r in reads:
            add(self.lastw.get(r))
        for w in writes:
            add(self.lastw.get(w))
            for d in self.readers.get(w, ()):
                add(d)
        return deps

    def _wait(self, eng, deps):
        h = self.engs[eng]
        wd = self.waited[eng]
        for k, v in deps.items():
            if wd.get(k, 0) >= v:
                continue
            if k == 'pe' and eng == 'pe':
                continue
            s = self.sem[k] if k in self.sem else self.dsem[k]
            h.wait_ge(s, v)
            wd[k] = v
            self.ninst[eng] += 1

    def _commit(self, tok, reads, writes):
        for r in reads:
            self.readers.setdefault(r, []).append(tok)
        for w in writes:
            self.lastw[w] = tok
            self.readers[w] = []

    def op(self, eng, fn, reads=(), writes=()):
        self._wait(eng, self._deps(reads, writes))
        ins = fn(self.engs[eng])
        self.cnt[eng] += 1
        self.ninst[eng] += 1
        ins.then_inc(self.sem[eng], 1)
        self._commit((eng, self.cnt[eng]), reads, writes)
        return ins

    def any2(self, fn, reads=(), writes=()):
        self.rr += 1
        return self.op('dve' if self.rr % 3 else 'pool', fn, reads, writes)

    def mm(self, fn, reads=(), writes=(), last=True):
        self._wait('pe', self._deps(reads, writes))
        ins = fn(self.engs['pe'])
        self.ninst['pe'] += 1
        if last:
            self.cnt['pe'] += 1
            ins.then_inc(self.sem['pe'], 1)
            tok = ('pe', self.cnt['pe'])
        else:
            tok = ('pe', self.cnt['pe'] + 1)
        self._commit(tok, reads, writes)
        return ins

    def dma(self, q, out, in_, reads=(), writes=(), key=None, **kw):
        if key not in self.dsem:
            self.dsem[key] = self.es.enter_context(self.nc.semaphore("d_" + key))
            self.dcnt[key] = 0
        self._wait(q, self._deps(reads, writes))
        ins = self.engs[q].dma_start(out=out, in_=in_, **kw)
        self.dcnt[key] += 16
        self.ninst[q] += 1
        ins.then_inc(self.dsem[key], 16)
        self._commit((key, self.dcnt[key]), reads, writes)
        return ins

    def barrier(self):
        deps = {k: self.cnt[k] for k in self.sem if self.cnt[k]}
        for k in self.dsem:
            deps[k] = self.dcnt[k]
        for e in self.engs:
            self._wait(e, dict(deps))


def host_consts(S):
    import ml_dtypes
    i = np.arange(128)
    c = {}
    c["c_ident"] = np.eye(128, dtype=np.float32)
    c["c_tri_incl"] = (i[:, None] <= i[None, :]).astype(np.float32)
    c["c_tri_strict"] = (i[:, None] > i[None, :]).astype(np.float32)
    neg = np.where(i[None, :] < i[:, None], -30000.0, 0.0).astype(np.float32)
    c["c_negmask"] = np.tile(neg, (1, 4))
    inv = (1.0 / (np.float32(10000.0) ** (np.arange(0, 32, 2, dtype=np.float32) / np.float32(32)))).astype(np.float32)
    ang = np.arange(S, dtype=np.float32)[:, None] * inv[None, :]
    cos = np.cos(ang).astype(np.float32).reshape(S // 128, 128, 16).transpose(1, 0, 2)
    sin = np.sin(ang).astype(np.float32).reshape(S // 128, 128, 16).transpose(1, 0, 2)
    c["c_cos"] = np.ascontiguousarray(cos)
    c["c_sin"] = np.ascontiguousarray(sin)
    return c


def host_params(inp):
    f = lambda a: np.ascontiguousarray(np.asarray(a, dtype=np.float32))
    rep = lambda v: f(np.broadcast_to(np.asarray(v, np.float32).reshape(1, -1), (128, v.size)))
    colT = lambda v: f(np.asarray(v, np.float32).reshape(-1, 128).T)
    p = {}
    p["p_gmix"] = colT(inp["norm_mix_g"][0])
    p["p_cw_ssd"] = f(np.asarray(inp["conv_ssd_w"][0]).reshape(4, 24, 128).transpose(2, 1, 0))
    p["p_cb_ssd"] = colT(inp["conv_ssd_b"][0])
    p["p_dtb"] = rep(inp["dt_bias"][0])
    p["p_alog"] = rep(inp["a_log"][0])
    p["p_dskip"] = colT(np.repeat(np.asarray(inp["d_skip"][0]), 64))
    p["p_gssd"] = colT(inp["ssd_norm_g"][0])
    p["p_gateb"] = colT(inp["gate_b"][0])
    p["p_gffn"] = colT(inp["norm_ffn_g"][0])
    p["p_cw_ffn"] = f(np.asarray(inp["conv_ffn_w"][0]).reshape(3, 44, 128).transpose(2, 1, 0))
    p["p_cb_ffn"] = colT(inp["conv_ffn_b"][0])
    p["p_gqa"] = rep(inp["q_a_norm_g"][0])
    p["p_gkva"] = rep(inp["kv_a_norm_g"][0])
    p["p_gq"] = rep(inp["q_norm_g"][0])
    p["p_gk"] = rep(inp["k_norm_g"][0])
    return p


PSHAPES = {
    "p_gmix": (128, 8), "p_cw_ssd": (128, 24, 4), "p_cb_ssd": (128, 24), "p_dtb": (128, 32), "p_alog": (128, 32),
    "p_dskip": (128, 16), "p_gssd": (128, 16), "p_gateb": (128, 16), "p_gffn": (128, 8),
    "p_cw_ffn": (128, 44, 3), "p_cb_ffn": (128, 44), "p_gqa": (128, 256), "p_gkva": (128, 128),
    "p_gq": (128, 96), "p_gk": (128, 96),
}


def CSHAPES(S):
    return {"c_ident": (128, 128), "c_tri_incl": (128, 128), "c_tri_strict": (128, 128), "c_negmask": (128, 512),
            "c_cos": (128, S // 128, 16), "c_sin": (128, S // 128, 16)}


def build(NSEQ=4, S=2048, TB=2, dbg=(), stop_after=None):
    BLK = TB * 128
    NT = S // 128
    NB = S // BLK
    nc = bass.Bass("TRN2", target_bir_lowering=False)

    def din(name, shape, dt=F32):
        return nc.dram_tensor(name, list(shape), dt, kind="ExternalInput").ap()

    x_d = din("x", [NSEQ * S, D_MODEL])
    out_d = nc.dram_tensor("out", [NSEQ * S, D_MODEL], F32, kind="ExternalOutput").ap()
    W = {k: din(k, v) for k, v in WSHAPES.items()}
    CHUNKS = weight_chunks()
    NCH = len(CHUNKS)
    NTOTCH = NCH * NSEQ * (S // (TB * 128))
    Wc = nc.dram_tensor("wchunks_bf", [NCH, 128, SLOT_ELEMS], BF16, kind="Internal").ap()
    Pd = {k: din(k, v) for k, v in PSHAPES.items()}
    Cd = {k: din(k, v) for k, v in CSHAPES(S).items()}
    kc_d = nc.dram_tensor("kcache", [16, 96, S], BF16, kind="Internal").ap()
    acs_d = [nc.dram_tensor("acs%d" % i, [32, 2, 128], BF16, kind="Internal").ap() for i in range(2)]
    dbg_d = {k: nc.dram_tensor("dbg_" + k, list(shp), dt_, kind="ExternalOutput").ap() for k, shp, dt_ in dbg}

    with ExitStack() as es:
        c = Ctx(nc, es)
        pall = es.enter_context(nc.psum_tensor("pall", [128, 4096], F32))

        def PB(b, n0=0, n1=512, p0=0, p1=128):
            return pall[p0:p1, b * 512 + n0: b * 512 + n1]

        def PBh(b):
            return pall[:, b * 512:(b + 1) * 512].bitcast(BF16)

        ident_f = c.sb("ident_f", [128, 128]); ident_b = c.sb("ident_b", [128, 128], BF16)
        triI = c.sb("triI", [128, 128]); triS = c.sb("triS", [128, 128])
        ones_f = c.sb("ones_f", [128, 128]); ones_b = c.sb("ones_b", [128, 128], BF16)
        negm_f = c.sb("negm_f", [128, 512]); negm_b = c.sb("negm_b", [128, 512], BF16)
        cmask_b = c.sb("cmask_b", [128, 128], BF16)
        cos_t = c.sb("cos_t", [128, NT, 16]); sin_t = c.sb("sin_t", [128, NT, 16])
        PR = {k: c.sb("sb_" + k, list(v)) for k, v in PSHAPES.items()}
        A_rep = c.sb("A_rep", [128, 32])
        xbuf = c.sb("xbuf", [128, TB, 1024])
        hT = c.sb("hT", [128, 8, BLK], BF16)
        Vc = c.sb("Vc", [128, NT, 16, 64], BF16)
        state = c.sb("state", [128, 4, 512]); state_b = c.sb("state_b", [128, 4, 512], BF16)
        halo_s = c.sb("halo_s", [128, 24, 3]); halo_f = c.sb("halo_f", [128, 44, 2])
        wslot = [c.sb("wslot%d" % i, [128, SLOT_ELEMS], BF16) for i in range(NSLOT)]
        mix = c.sb("mix", [128, 8, BLK])
        small = c.sb("small", [128, TB, 448])
        ss1 = c.sb("ss1", [128, 8]); rs1 = c.sb("rs1", [128, 8])
        junk = c.sb("junk", [128, 1024], BF16)
        rhs2 = c.sb("rhs2", [2, 4096], BF16)
        ones2 = c.sb("ones2", [2, 128], BF16)

        c.dma('sp', ident_f[:], Cd["c_ident"], writes=['ident_f'], key='ld')
        c.dma('sp', triI[:], Cd["c_tri_incl"], writes=['triI'], key='ld')
        c.dma('sp', triS[:], Cd["c_tri_strict"], writes=['triS'], key='ld')
        c.dma('sp', negm_f[:], Cd["c_negmask"], writes=['negm_f'], key='ld')
        c.dma('sp', cos_t[:], Cd["c_cos"], writes=['cos'], key='ld')
        c.dma('sp', sin_t[:], Cd["c_sin"], writes=['sin'], key='ld')
        for k in PSHAPES:
            c.dma('sp', PR[k][:], Pd[k], writes=[k], key='ld')
        c.barrier()
        c.op('dve', lambda e: e.tensor_copy(out=ident_b[:], in_=ident_f[:]), reads=['ident_f'], writes=['ident_b'])
        c.op('dve', lambda e: e.tensor_copy(out=cmask_b[:], in_=triI[:]), reads=['triI'], writes=['cmask_b'])
        c.op('dve', lambda e: e.tensor_copy(out=negm_b[:], in_=negm_f[:]), reads=['negm_f'], writes=['negm_b'])
        c.op('dve', lambda e: e.memset(ones_f[:], 1.0), writes=['ones_f'])
        c.op('dve', lambda e: e.memset(ones_b[:], 1.0), writes=['ones_b'])
        c.op('dve', lambda e: e.memset(ones2[:], 1.0), writes=['ones2'])
        c.op('act', lambda e: e.activation(out=A_rep[:], in_=PR["p_alog"][:], func=AF.Exp), reads=['p_alog'], writes=['A_rep'])
        c.op('dve', lambda e: e.tensor_scalar(out=A_rep[:], in0=A_rep[:], scalar1=-1.0, scalar2=None, op0=ALU.mult),
             reads=['A_rep'], writes=['A_rep'])

        with ExitStack() as es2:
            st32 = [es2.enter_context(nc.sbuf_tensor("st32_%d" % i, [128, SLOT_ELEMS], F32)) for i in range(2)]
            st16 = [es2.enter_context(nc.sbuf_tensor("st16_%d" % i, [128, SLOT_ELEMS], BF16)) for i in range(2)]
            for ci, (name, P_, KT, r0, segs) in enumerate(CHUNKS):
                i = ci % 2
                ntot = sum(n for _, n in segs)
                assert KT * ntot <= SLOT_ELEMS
                v32 = st32[i][0:P_, 0:KT * ntot].rearrange("p (k n) -> p k n", k=KT)
                o = 0
                for (c0, n) in segs:
                    c.dma('sp', v32[:, :, o:o + n], W[name][r0:r0 + KT * P_, c0:c0 + n].rearrange("(k p) n -> p k n", p=P_),
                          writes=['st32_%d' % i], key='pl%d' % i)
                    o += n
                eng = ('dve', 'pool', 'act')[ci % 3]
                if eng == 'act':
                    c.op('act', lambda e: e.activation(out=st16[i][0:P_, 0:KT * ntot], in_=st32[i][0:P_, 0:KT * ntot], func=AF.Copy),
                         reads=['st32_%d' % i], writes=['st16_%d' % i])
                else:
                    c.op(eng, lambda e: e.tensor_copy(out=st16[i][0:P_, 0:KT * ntot], in_=st32[i][0:P_, 0:KT * ntot]),
                         reads=['st32_%d' % i], writes=['st16_%d' % i])
                c.dma('sp', Wc[ci, 0:P_, 0:KT * ntot], st16[i][0:P_, 0:KT * ntot], reads=['st16_%d' % i], writes=['wc%d' % ci], key='ps%d' % i)
        c.barrier()

        wstate = {'next': 0, 'loaded': 0}

        def wissue(upto):
            while wstate['loaded'] < upto:
                gi = wstate['loaded']
                ci = gi % NCH
                name, P_, KT, r0, segs = CHUNKS[ci]
                ntot = sum(n for _, n in segs)
                i = gi % NSLOT
                c.dma('sp', wslot[i][0:P_, 0:KT * ntot], Wc[ci, 0:P_, 0:KT * ntot], reads=['wc%d' % ci], writes=['wslot%d' % i], key='w%d' % i)
                wstate['loaded'] += 1

        def wload(name, P_, KT, segs, r0=0):
            gi = wstate['next']
            wstate['next'] += 1
            spec = CHUNKS[gi % NCH]
            assert spec == (name, P_, KT, r0, tuple(segs)), (spec, name, segs)
            wissue(min(gi + 3, NTOTCH))
            ntot = sum(n for _, n in segs)
            i = gi % NSLOT
            return wslot[i][0:P_, 0:KT * ntot].rearrange("p (k n) -> p k n", k=KT), 'wslot%d' % i

        def rstd(ss_ap, out_ap, n, reads, writes):
            c.op('act', lambda e: e.activation(out=out_ap, in_=ss_ap, func=AF.Sqrt, bias=EPS, scale=1.0 / n), reads=reads, writes=writes)
            c.op('dve', lambda e: e.reciprocal(out=out_ap, in_=out_ap), reads=writes, writes=writes)

        def rmsnorm_to_hT(gname, xn_bufs):
            for t in range(TB):
                c.op('act', lambda e: e.activation(out=junk[:], in_=xbuf[:, t, :], func=AF.Square, accum_out=ss1[:, t:t + 1]),
                     reads=['xbuf%d' % t], writes=['junk', 'ss1_%d' % t])
                rstd(ss1[:, t:t + 1], rs1[:, t:t + 1], 1024, ['ss1_%d' % t], ['rs1_%d' % t])
                xn = xn_bufs[t % 2]
                c.op('act', lambda e: e.activation(out=xn[:], in_=xbuf[:, t, :], func=AF.Copy, scale=rs1[:, t:t + 1]),
                     reads=['xbuf%d' % t, 'rs1_%d' % t], writes=['xn%d' % (t % 2)])
                pt = PBh(t % 2)
                for kt in range(8):
                    c.mm(lambda e: e.transpose(out=pt[:, kt * 128:(kt + 1) * 128], in_=xn[:, kt * 128:(kt + 1) * 128], identity=ident_b[:]),
                         reads=['xn%d' % (t % 2), 'ident_b'], writes=['P%d' % (t % 2)], last=(kt == 7))
                c.op('dve', lambda e: e.tensor_tensor(out=hT[:, :, t * 128:(t + 1) * 128], in0=pt.rearrange("p (k n) -> p k n", k=8),
                                                      in1=PR[gname][:, :].unsqueeze(2).to_broadcast([128, 8, 128]), op=ALU.mult),
                     reads=['P%d' % (t % 2), gname], writes=['hT'])

        def conv_multi(items, nh):
            for (ps_ap, stage, halo_ap, wt, bt, acc, sname, aname, hname, pname) in items:
                c.op('pool', lambda e: e.tensor_copy(out=stage[:, 0:nh], in_=halo_ap), reads=[hname], writes=[sname + 'h'])
            for (ps_ap, stage, halo_ap, wt, bt, acc, sname, aname, hname, pname) in items:
                c.op('act', lambda e: e.activation(out=stage[:, nh:nh + BLK], in_=ps_ap, func=AF.Copy), reads=[pname], writes=[sname])
            for (ps_ap, stage, halo_ap, wt, bt, acc, sname, aname, hname, pname) in items:
                c.op('pool', lambda e: e.tensor_copy(out=halo_ap, in_=stage[:, BLK:BLK + nh]), reads=[sname, sname + 'h'], writes=[hname])
            for (ps_ap, stage, halo_ap, wt, bt, acc, sname, aname, hname, pname) in items:
                c.op('pool', lambda e: e.tensor_scalar(out=acc[:], in0=stage[:, nh:nh + BLK], scalar1=wt[:, nh:nh + 1], scalar2=bt,
                                                       op0=ALU.mult, op1=ALU.add), reads=[sname], writes=[aname])
            for k in range(nh - 1, -1, -1):
                for (ps_ap, stage, halo_ap, wt, bt, acc, sname, aname, hname, pname) in items:
                    c.op('dve', lambda e: e.scalar_tensor_tensor(out=acc[:], in0=stage[:, k:k + BLK], scalar=wt[:, k:k + 1], in1=acc[:],
                                                                 op0=ALU.mult, op1=ALU.add), reads=[sname, sname + 'h', aname], writes=[aname])

        def dump(name, ap_sb, res):
            if name in dbg_d:
                c.dma('sp', dbg_d[name], ap_sb, reads=res, key='dbg')

        nchunk = 0
        for seq in range(NSEQ):
            for blk in range(NB):
                tok0 = seq * S + blk * BLK
                c.dma('sp', xbuf[:], x_d[tok0:tok0 + BLK, :].rearrange("(t p) f -> p t f", p=128),
                      writes=['xbuf%d' % t for t in range(TB)], key='xld')
                if blk == 0:
                    c.op('pool', lambda e: e.memset(state[:], 0.0), writes=['state%d' % g for g in range(4)])
                    c.op('pool', lambda e: e.memset(state_b[:], 0.0), writes=['stateb%d' % g for g in range(4)])
                    c.op('pool', lambda e: e.memset(halo_s[:], 0.0), writes=['halo_s%d' % j for j in range(24)])
                    c.op('pool', lambda e: e.memset(halo_f[:], 0.0), writes=['halo_f%d' % j for j in range(44)])
                with ExitStack() as esA:
                    def sbA(name, shape, dt=F32):
                        return esA.enter_context(nc.sbuf_tensor(un(name), list(shape), dt))
                    xn_bufs = [sbA("xnA%d" % i, [128, 1024], BF16) for i in range(2)]
                    xbcT = sbA("xbcT", [128, 24, BLK], BF16)
                    szT = sbA("szT", [128, 16, BLK], BF16)
                    ynT = sbA("ynT", [128, 16, BLK], BF16)
                    stg = [sbA("stg%d" % i, [128, BLK + 3]) for i in range(4)]
                    accs = [sbA("acc%d" % i, [128, BLK]) for i in range(4)]
                    dtt = sbA("dtt", [128, TB, 32]); a_tok = sbA("a_tok", [128, TB, 32])
                    rmsnorm_to_hT("p_gmix", xn_bufs)
                    dump("hT", hT[:, 0, :], ['hT'])
                    wv, wr = wload("w_in", 128, 8, [(OFF_DT, 448)])
                    for t in range(TB):
                        for kt in range(8):
                            c.mm(lambda e: e.matmul(PB(2 + t % 2, 0, 448), lhsT=hT[:, kt, t * 128:(t + 1) * 128], rhs=wv[:, kt, :],
                                                    start=(kt == 0), stop=(kt == 7)), reads=['hT', wr], writes=['P%d' % (2 + t % 2)], last=(kt == 7))
                        c.op('act', lambda e: e.activation(out=small[:, t, :], in_=PB(2 + t % 2, 0, 448), func=AF.Copy),
                             reads=['P%d' % (2 + t % 2)], writes=['small%d' % t])
                    allsmall = ['small%d' % t for t in range(TB)]
                    c.op('dve', lambda e: e.tensor_tensor(out=dtt[:], in0=small[:, :, 0:32], in1=PR["p_dtb"][:, :].unsqueeze(1).to_broadcast([128, TB, 32]),
                                                          op=ALU.add), reads=allsmall + ['p_dtb'], writes=['dtt'])
                    c.op('act', lambda e: e.activation(out=dtt[:], in_=dtt[:], func=AF.Exp), reads=['dtt'], writes=['dtt'])
                    c.op('act', lambda e: e.activation(out=dtt[:], in_=dtt[:], func=AF.Ln, bias=1.0, scale=1.0), reads=['dtt'], writes=['dtt'])
                    c.op('dve', lambda e: e.tensor_tensor(out=a_tok[:], in0=dtt[:], in1=A_rep[:, :].unsqueeze(1).to_broadcast([128, TB, 32]),
                                                          op=ALU.mult), reads=['dtt', 'A_rep'], writes=['a_tok'])
                    for ch in range(6):
                        wv, wr = wload("w_in", 128, 8, [(OFF_XBC + ch * 512, 512)])
                        for pr in range(2):
                            items = []
                            for jj in (2 * pr, 2 * pr + 1):
                                j = ch * 4 + jj
                                pb = j % 4
                                for kt in range(8):
                                    c.mm(lambda e: e.matmul(PB(pb, 0, BLK), lhsT=wv[:, kt, jj * 128:(jj + 1) * 128], rhs=hT[:, kt, :],
                                                            start=(kt == 0), stop=(kt == 7)), reads=['hT', wr], writes=['P%d' % pb], last=(kt == 7))
                                items.append((PB(pb, 0, BLK), stg[j % 4], halo_s[:, j, :], PR["p_cw_ssd"][:, j, :], PR["p_cb_ssd"][:, j:j + 1],
                                              accs[j % 4], 'stg%d' % (j % 4), 'acc%d' % (j % 4), 'halo_s%d' % j, 'P%d' % pb))
                            conv_multi(items, 3)
                            for jj in (2 * pr, 2 * pr + 1):
                                j = ch * 4 + jj
                                c.op('act', lambda e: e.activation(out=xbcT[:, j, :], in_=accs[j % 4][:], func=AF.Silu), reads=['acc%d' % (j % 4)], writes=['xbcT%d' % j])
                    dump("xbcT", xbcT[:, 0, :], ['xbcT0'])
                    for ch in range(4):
                        wv, wr = wload("w_in", 128, 8, [(ch * 512, 512)])
                        for jj in range(4):
                            j = ch * 4 + jj
                            pb = j % 2
                            for kt in range(8):
                                c.mm(lambda e: e.matmul(PB(pb, 0, BLK), lhsT=wv[:, kt, jj * 128:(jj + 1) * 128], rhs=hT[:, kt, :],
                                                        start=(kt == 0), stop=(kt == 7)), reads=['hT', wr], writes=['P%d' % pb], last=(kt == 7))
                            c.op('act', lambda e: e.activation(out=szT[:, j, :], in_=PB(pb, 0, BLK), func=AF.Silu), reads=['P%d' % pb], writes=['szT%d' % j])
                    with ExitStack() as esS:
                        def sbS(name, shape, dt=F32):
                            return esS.enter_context(nc.sbuf_tensor(un(name), list(shape), dt))
                        eac = sbS("eac", [128, 32]); dte = sbS("dte", [128, 32]); cdr = sbS("cdr", [128, 32]); nac = sbS("nac", [128, 32])
                        dtdte = sbS("dtdte", [128, 32])
                        hl = sbS("hl", [32, 2, 128], BF16); rres = sbS("rres", [32, 128])
                        sc = sbS("sc", [128, 4, 128])
                        xdt = sbS("xdt", [128, 32, 64], BF16); xdtd = sbS("xdtd", [128, 32, 64], BF16)
                        btok = sbS("btok", [128, 4, 128], BF16)
                        dec = [sbS("dec%d" % i, [128, 8, 128]) for i in range(2)]
                        LT = [sbS("LT%d" % i, [128, 8, 128], BF16) for i in range(2)]
                        yoff = sbS("yoff", [128, 8, 64], BF16)
                        tt = sbS("tt", [128, 16, 128]); sq = sbS("sq", [128, 16, 128], BF16)
                        xd = sbS("xd", [128, 4, 128]); sttmp = sbS("sttmp", [128, 512]); rrr = sbS("rrr", [128, 4, 128])
                        for t in range(TB):
                            cs = slice(t * 128, (t + 1) * 128)
                            par = nchunk % 2
                            nchunk += 1
                            a_c = a_tok[:, t, :]
                            c.mm(lambda e: e.matmul(PB(2, 0, 32), lhsT=triI[:], rhs=a_c, start=True, stop=True), reads=['a_tok', 'triI'], writes=['P2'], last=False)
                            c.mm(lambda e: e.matmul(PB(2, 32, 64), lhsT=triS[:], rhs=a_c, start=True, stop=True), reads=['a_tok', 'triS'], writes=['P2'], last=False)
                            c.mm(lambda e: e.matmul(PB(2, 64, 96), lhsT=ones_f[:], rhs=a_c, start=True, stop=True), reads=['a_tok', 'ones_f'], writes=['P2'], last=False)
                            c.mm(lambda e: e.matmul(PB(2, 128, 256, 0, 32), lhsT=a_c, rhs=triI[:], start=True, stop=True), reads=['a_tok', 'triI'], writes=['P2'], last=True)
                            c.op('act', lambda e: e.activation(out=eac[:], in_=PB(2, 0, 32), func=AF.Exp), reads=['P2'], writes=['eac'])
                            c.op('act', lambda e: e.activation(out=dte[:], in_=PB(2, 32, 64), func=AF.Exp), reads=['P2'], writes=['dte'])
                            c.op('act', lambda e: e.activation(out=cdr[:], in_=PB(2, 64, 96), func=AF.Exp), reads=['P2'], writes=['cdr'])
                            c.op('dve', lambda e: e.tensor_scalar(out=nac[:], in0=PB(2, 0, 32), scalar1=-1.0, scalar2=None, op0=ALU.mult), reads=['P2'], writes=['nac'])
                            c.op('dve', lambda e: e.tensor_tensor(out=dtdte[:], in0=dtt[:, t, :], in1=dte[:], op=ALU.mult), reads=['dtt', 'dte'], writes=['dtdte'])
                            c.op('dve', lambda e: e.tensor_copy(out=hl[:, 0, :], in_=PB(2, 128, 256, 0, 32)), reads=['P2'], writes=['hl0'])
                            c.op('dve', lambda e: e.tensor_tensor(out=rres[:], in0=PB(2, 128, 256, 0, 32), in1=hl[:, 0, :], op=ALU.subtract), reads=['P2', 'hl0'], writes=['rres'])
                            c.op('dve', lambda e: e.tensor_copy(out=hl[:, 1, :], in_=rres[:]), reads=['rres'], writes=['hl1'])
                            c.dma('pool', acs_d[par], hl[:], reads=['hl0', 'hl1'], writes=['acs%d' % par], key='acs_w')
                            c.dma('sp', rhs2[:, :].rearrange("j (h l) -> j h l", h=32), acs_d[par].rearrange("h j l -> j h l"), reads=['acs%d' % par], writes=['rhs2'], key='acs_r')
                            for g in range(4):
                                c.mm(lambda e: e.matmul(PB(3, g * 128, (g + 1) * 128), lhsT=xbcT[:, 16 + g, cs], rhs=xbcT[:, 20 + g, cs], start=True, stop=True),
                                     reads=['xbcT%d' % (16 + g), 'xbcT%d' % (20 + g)], writes=['P3'], last=(g == 3))
                            c.op('act', lambda e: e.activation(out=sc[:], in_=PB(3).rearrange("p (g n) -> p g n", g=4), func=AF.Copy), reads=['P3'], writes=['sc'])
                            xtp = pall[:, 4 * 512:6 * 512].bitcast(BF16)
                            for i in range(16):
                                c.mm(lambda e: e.transpose(out=xtp[:, i * 128:(i + 1) * 128], in_=xbcT[:, i, cs], identity=ident_b[:]),
                                     reads=['xbcT%d' % i, 'ident_b'], writes=['P4', 'P5'], last=(i == 15))
                            xtp3 = xtp.rearrange("p (h d) -> p h d", h=32)
                            c.op('dve', lambda e: e.tensor_tensor(out=xdt[:], in0=xtp3, in1=dtt[:, t, :].unsqueeze(2).to_broadcast([128, 32, 64]), op=ALU.mult),
                                 reads=['P4', 'P5', 'dtt'], writes=['xdt'])
                            c.op('dve', lambda e: e.tensor_tensor(out=xdtd[:], in0=xtp3, in1=dtdte[:, :].unsqueeze(2).to_broadcast([128, 32, 64]), op=ALU.mult),
                                 reads=['P4', 'P5', 'dtdte'], writes=['xdtd'])
                            btp = PBh(6)
                            for g in range(4):
                                c.mm(lambda e: e.transpose(out=btp[:, g * 128:(g + 1) * 128], in_=xbcT[:, 16 + g, cs], identity=ident_b[:]),
                                     reads=['xbcT%d' % (16 + g), 'ident_b'], writes=['P6'], last=(g == 3))
                            c.op('act', lambda e: e.activation(out=btok[:], in_=btp[:, 0:512].rearrange("p (g n) -> p g n", g=4), func=AF.Copy), reads=['P6'], writes=['btok'])
                            for g in range(4):
                                d_ = dec[g % 2]; L_ = LT[g % 2]
                                dn = 'dec%d' % (g % 2); Ln_ = 'LT%d' % (g % 2)
                                for half in range(2):
                                    h0 = g * 8 + half * 4
                                    c.mm(lambda e: e.matmul(PB(half), lhsT=ones2[:, :], rhs=rhs2[:, h0 * 128:(h0 + 4) * 128], start=True, stop=False),
                                         reads=['rhs2', 'ones2'], writes=['P%d' % half], last=False)
                                    c.mm(lambda e: e.matmul(PB(half), lhsT=ident_b[:], rhs=negm_b[:], start=False, stop=True),
                                         reads=['ident_b', 'negm_b'], writes=['P%d' % half], last=True)
                                for hh in range(8):
                                    h = g * 8 + hh
                                    c.op('act', lambda e: e.activation(out=d_[:, hh, :], in_=PB(hh // 4, (hh % 4) * 128, (hh % 4 + 1) * 128), func=AF.Exp,
                                                                       bias=nac[:, h:h + 1], scale=1.0), reads=['P%d' % (hh // 4), 'nac'], writes=[dn])
                                c.any2(lambda e: e.tensor_tensor(out=L_[:], in0=d_[:], in1=sc[:, g, :].unsqueeze(1).to_broadcast([128, 8, 128]), op=ALU.mult),
                                       reads=[dn, 'sc'], writes=[Ln_])
                                c.mm(lambda e: e.matmul(PB(6), lhsT=xbcT[:, 20 + g, cs], rhs=state_b[:, g, :], start=True, stop=True),
                                     reads=['xbcT%d' % (20 + g), 'stateb%d' % g], writes=['P6'], last=True)
                                c.op('dve', lambda e: e.tensor_tensor(out=yoff[:], in0=PB(6).rearrange("p (h d) -> p h d", h=8),
                                                                      in1=eac[:, g * 8:(g + 1) * 8].unsqueeze(2).to_broadcast([128, 8, 64]), op=ALU.mult),
                                     reads=['P6', 'eac'], writes=['yoff'])
                                yof = yoff[:].rearrange("p h d -> p (h d)")
                                for il in range(4):
                                    c.mm(lambda e: e.matmul(PB(7, il * 128, (il + 1) * 128), lhsT=yof[:, il * 128:(il + 1) * 128], rhs=ident_b[:],
                                                            start=True, stop=False), reads=['yoff', 'ident_b'], writes=['P7'], last=False)
                                    for hf in range(2):
                                        hh = il * 2 + hf
                                        h = g * 8 + hh
                                        c.mm(lambda e: e.matmul(PB(7, il * 128, (il + 1) * 128, hf * 64, hf * 64 + 64), lhsT=xdt[:, h, :], rhs=L_[:, hh, :],
                                                                start=False, stop=(hf == 1), tile_position=(0, hf * 64)),
                                             reads=['xdt', Ln_], writes=['P7'], last=(hh == 7))
                                tl = ['xbcT%d' % (g * 4 + i) for i in range(4)]
                                c.op('pool', lambda e: e.tensor_tensor(out=xd[:], in0=xbcT[:, g * 4:(g + 1) * 4, cs],
                                                                       in1=PR["p_dskip"][:, g * 4:(g + 1) * 4].unsqueeze(2).to_broadcast([128, 4, 128]), op=ALU.mult),
                                     reads=tl + ['p_dskip'], writes=['xd'])
                                c.op('dve', lambda e: e.tensor_tensor(out=tt[:, g * 4:(g + 1) * 4, :], in0=PB(7).rearrange("p (i n) -> p i n", i=4), in1=xd[:], op=ALU.add),
                                     reads=['P7', 'xd'], writes=['tt%d' % g])
                                c.op('pool', lambda e: e.tensor_tensor(out=tt[:, g * 4:(g + 1) * 4, :], in0=tt[:, g * 4:(g + 1) * 4, :], in1=szT[:, g * 4:(g + 1) * 4, cs], op=ALU.mult),
                                     reads=['tt%d' % g] + ['szT%d' % (g * 4 + i) for i in range(4)], writes=['tt%d' % g])
                                c.mm(lambda e: e.matmul(PB(6), lhsT=btok[:, g, :], rhs=xdtd[:, g * 8:(g + 1) * 8, :].rearrange("p h d -> p (h d)"), start=True, stop=True),
                                     reads=['btok', 'xdtd'], writes=['P6'], last=True)
                                c.op('pool', lambda e: e.tensor_tensor(out=sttmp[:].rearrange("p (h d) -> p h d", h=8), in0=state[:, g, :].rearrange("p (h d) -> p h d", h=8),
                                                                       in1=cdr[:, g * 8:(g + 1) * 8].unsqueeze(2).to_broadcast([128, 8, 64]), op=ALU.mult),
                                     reads=['state%d' % g, 'cdr'], writes=['sttmp'])
                                c.op('dve', lambda e: e.tensor_tensor(out=state[:, g, :], in0=PB(6), in1=sttmp[:], op=ALU.add), reads=['P6', 'sttmp'], writes=['state%d' % g])
                                c.op('act', lambda e: e.activation(out=state_b[:, g, :], in_=state[:, g, :], func=AF.Copy), reads=['state%d' % g], writes=['stateb%d' % g])
                            allt = ['tt%d' % g for g in range(4)]
                            c.op('act', lambda e: e.activation(out=sq[:], in_=tt[:], func=AF.Square), reads=allt, writes=['sq'])
                            for g in range(4):
                                for il in range(4):
                                    c.mm(lambda e: e.matmul(PB(3, g * 128, (g + 1) * 128), lhsT=ones_b[:], rhs=sq[:, g * 4 + il, :], start=(il == 0), stop=(il == 3)),
                                         reads=['sq', 'ones_b'], writes=['P3'], last=(g == 3 and il == 3))
                            rstd(PB(3).rearrange("p (g n) -> p g n", g=4), rrr[:], 512, ['P3'], ['rrr'])
                            c.op('pool', lambda e: e.tensor_tensor(out=tt[:], in0=tt[:], in1=PR["p_gssd"][:, :].unsqueeze(2).to_broadcast([128, 16, 128]), op=ALU.mult),
                                 reads=allt + ['p_gssd'], writes=allt)
                            for g in range(4):
                                c.op('dve', lambda e: e.tensor_tensor(out=ynT[:, g * 4:(g + 1) * 4, cs], in0=tt[:, g * 4:(g + 1) * 4, :],
                                                                      in1=rrr[:, g, :].unsqueeze(1).to_broadcast([128, 4, 128]), op=ALU.mult),
                                     reads=['tt%d' % g, 'rrr'], writes=['ynT'])
                    dump("ynT", ynT[:, 0, :], ['ynT'])
                    dump("state0", state[:, 0, :], ['state0'])
                    for mb in range(4):
                        wv, wr = wload("w_ssd_proj", 128, 16, [(mb * 256, 256)])
                        wg, wgr = wload("w_in", 128, 8, [(OFF_G + mb * 256, 256)])
                        for mm_ in range(2):
                            m = mb * 2 + mm_
                            for kt in range(16):
                                c.mm(lambda e: e.matmul(PB(0, 0, BLK), lhsT=wv[:, kt, mm_ * 128:(mm_ + 1) * 128], rhs=ynT[:, kt, :], start=(kt == 0), stop=(kt == 15)),
                                     reads=['ynT', wr], writes=['P0'], last=(kt == 15))
                            for kt in range(8):
                                c.mm(lambda e: e.matmul(PB(1, 0, BLK), lhsT=wg[:, kt, mm_ * 128:(mm_ + 1) * 128], rhs=hT[:, kt, :], start=(kt == 0), stop=(kt == 7)),
                                     reads=['hT', wgr], writes=['P1'], last=(kt == 7))
                            gsb = accs[m % 2]
                            c.op('act', lambda e: e.activation(out=gsb[:], in_=PB(1, 0, BLK), func=AF.Sigmoid, bias=PR["p_gateb"][:, m:m + 1], scale=1.0),
                                 reads=['P1', 'p_gateb'], writes=['acc%d' % (m % 2)])
                            c.op('dve', lambda e: e.tensor_tensor(out=mix[:, m, :], in0=PB(0, 0, BLK), in1=gsb[:], op=ALU.mult),
                                 reads=['P0', 'acc%d' % (m % 2)], writes=['mix%d' % m])
                c.barrier()
                dump("mix_ssd", mix[:, 0, :], ['mix0'])
                if stop_after == 'A':
                    continue
                SCALE = 96.0 ** -0.5
                with ExitStack() as esA:
                    def sbB(name, shape, dt=F32):
                        return esA.enter_context(nc.sbuf_tensor(un(name), list(shape), dt))
                    qkT = sbB("qkT", [128, 3, BLK], BF16)
                    qT = sbB("qT", [96, 16, BLK], BF16); kTc = sbB("kTc", [96, 16, BLK], BF16)
                    attnT = sbB("attnT", [64, 16, BLK], BF16)
                    qan = sbB("qan", [128, 384], BF16)
                    ssq = sbB("ssq", [128, 2 * TB]); rq = sbB("rq", [128, 2 * TB])
                    qsb = sbB("qsb", [128, 16, 96]); qtmp = sbB("qtmp", [128, 16, 96])
                    ssh = sbB("ssh", [128, 16]); rh = sbB("rh", [128, 16])
                    qr = sbB("qr", [128, 16, 96], BF16)
                    r1 = sbB("r1", [128, 16, 16]); r2 = sbB("r2", [128, 16, 16])
                    pts = [sbB("pt%d" % i, [128, BLK], BF16) for i in range(3)]
                    kst2 = [sbB("kstp%d" % i, [96, 2, max(1, NB - 1) * BLK], BF16) for i in range(2)]
                    rD = sbB("rD", [64, BLK])
                    gsb2 = [sbB("gsb%d" % i, [128, BLK]) for i in range(2)]
                    tmpm = sbB("tmpm", [128, BLK])
                    wuq, wuqr = wload("w_uq", 128, 2, [(0, 1536)])
                    wukv, wukvr = wload("w_ukv", 128, 1, [(0, 2048)])
                    for t in range(TB):
                        tg = blk * TB + t
                        c.op('act', lambda e: e.activation(out=junk[:, 0:256], in_=small[:, t, 32:288], func=AF.Square, accum_out=ssq[:, 2 * t:2 * t + 1]),
                             reads=['small%d' % t], writes=['junk', 'ssq%d' % t])
                        c.op('act', lambda e: e.activation(out=junk[:, 256:384], in_=small[:, t, 288:416], func=AF.Square, accum_out=ssq[:, 2 * t + 1:2 * t + 2]),
                             reads=['small%d' % t], writes=['junk', 'ssq%d' % t])
                        rstd(ssq[:, 2 * t:2 * t + 1], rq[:, 2 * t:2 * t + 1], 256, ['ssq%d' % t], ['rq%da' % t])
                        rstd(ssq[:, 2 * t + 1:2 * t + 2], rq[:, 2 * t + 1:2 * t + 2], 128, ['ssq%d' % t], ['rq%db' % t])
                        c.op('dve', lambda e: e.scalar_tensor_tensor(out=qan[:, 0:256], in0=small[:, t, 32:288], scalar=rq[:, 2 * t:2 * t + 1], in1=PR["p_gqa"][:],
                                                                     op0=ALU.mult, op1=ALU.mult), reads=['small%d' % t, 'rq%da' % t, 'p_gqa'], writes=['qan'])
                        c.op('dve', lambda e: e.scalar_tensor_tensor(out=qan[:, 256:384], in0=small[:, t, 288:416], scalar=rq[:, 2 * t + 1:2 * t + 2], in1=PR["p_gkva"][:],
                                                                     op0=ALU.mult, op1=ALU.mult), reads=['small%d' % t, 'rq%db' % t, 'p_gkva'], writes=['qan'])
                        pt_ = PBh(0)
                        for i in range(3):
                            c.mm(lambda e: e.transpose(out=pt_[:, i * 128:(i + 1) * 128], in_=qan[:, i * 128:(i + 1) * 128], identity=ident_b[:]),
                                 reads=['qan', 'ident_b'], writes=['P0'], last=(i == 2))
                        c.op('act', lambda e: e.activation(out=qkT[:, :, t * 128:(t + 1) * 128], in_=pt_[:, 0:384].rearrange("p (k n) -> p k n", k=3), func=AF.Copy),
                             reads=['P0'], writes=['qkT'])
                        for nb in range(3):
                            for kt in range(2):
                                c.mm(lambda e: e.matmul(PB(1 + nb), lhsT=qkT[:, kt, t * 128:(t + 1) * 128], rhs=wuq[:, kt, nb * 512:(nb + 1) * 512], start=(kt == 0), stop=(kt == 1)),
                                     reads=['qkT', wuqr], writes=['P%d' % (1 + nb)], last=(kt == 1))
                        for nb in range(4):
                            c.mm(lambda e: e.matmul(PB(4 + nb), lhsT=qkT[:, 2, t * 128:(t + 1) * 128], rhs=wukv[:, 0, nb * 512:(nb + 1) * 512], start=True, stop=True),
                                 reads=['qkT', wukvr], writes=['P%d' % (4 + nb)], last=True)
                        kvp = pall[:, 4 * 512:8 * 512].rearrange("p (h d) -> p h d", h=16)
                        c.op('act', lambda e: e.activation(out=Vc[:, tg, :, :], in_=kvp[:, :, 64:128], func=AF.Copy), reads=['P4', 'P5', 'P6', 'P7'], writes=['Vc%d' % tg])
                        for which in range(2):
                            gname = "p_gq" if which == 0 else "p_gk"
                            if which == 0:
                                c.op('act', lambda e: e.activation(out=qsb[:], in_=pall[:, 512:4 * 512].rearrange("p (h d) -> p h d", h=16), func=AF.Copy),
                                     reads=['P1', 'P2', 'P3'], writes=['qsb'])
                            else:
                                c.op('act', lambda e: e.activation(out=qsb[:, :, 0:64], in_=kvp[:, :, 0:64], func=AF.Copy), reads=['P4', 'P5', 'P6', 'P7'], writes=['qsb'])
                                c.op('pool', lambda e: e.tensor_copy(out=qsb[:, :, 64:96], in_=small[:, t, 416:448].unsqueeze(1).to_broadcast([128, 16, 32])),
                                     reads=['small%d' % t, 'qsb'], writes=['qsb'])
                            c.op('pool', lambda e: e.tensor_tensor(out=qtmp[:], in0=qsb[:], in1=qsb[:], op=ALU.mult), reads=['qsb'], writes=['qtmp'])
                            c.op('dve', lambda e: e.tensor_reduce(out=ssh[:], in_=qtmp[:], axis=AX.X, op=ALU.add), reads=['qtmp'], writes=['ssh'])
                            rstd(ssh[:], rh[:], 96, ['ssh'], ['rh'])
                            c.op('dve', lambda e: e.tensor_tensor(out=qtmp[:], in0=qsb[:], in1=rh[:, :].unsqueeze(2).to_broadcast([128, 16, 96]), op=ALU.mult),
                                 reads=['qsb', 'rh'], writes=['qtmp'])
                            c.op('pool', lambda e: e.tensor_tensor(out=qtmp[:], in0=qtmp[:], in1=PR[gname][:, :].unsqueeze(1).to_broadcast([128, 16, 96]), op=ALU.mult),
                                 reads=['qtmp', gname], writes=['qtmp'])
                            x1 = qtmp[:, :, 64:80]; x2 = qtmp[:, :, 80:96]
                            cb = cos_t[:, tg, :].unsqueeze(1).to_broadcast([128, 16, 16]); sb_ = sin_t[:, tg, :].unsqueeze(1).to_broadcast([128, 16, 16])
                            c.op('dve', lambda e: e.tensor_tensor(out=r1[:], in0=x1, in1=cb, op=ALU.mult), reads=['qtmp', 'cos'], writes=['r1'])
                            c.op('pool', lambda e: e.tensor_tensor(out=r2[:], in0=x2, in1=sb_, op=ALU.mult), reads=['qtmp', 'sin'], writes=['r2'])
                            c.op('dve', lambda e: e.tensor_tensor(out=qr[:, :, 64:80], in0=r1[:], in1=r2[:], op=ALU.subtract), reads=['r1', 'r2'], writes=['qr'])
                            c.op('dve', lambda e: e.tensor_tensor(out=r1[:], in0=x1, in1=sb_, op=ALU.mult), reads=['qtmp', 'sin', 'qr'], writes=['r1'])
                            c.op('pool', lambda e: e.tensor_tensor(out=r2[:], in0=x2, in1=cb, op=ALU.mult), reads=['qtmp', 'cos', 'qr'], writes=['r2'])
                            c.op('dve', lambda e: e.tensor_tensor(out=qr[:, :, 80:96], in0=r1[:], in1=r2[:], op=ALU.add), reads=['r1', 'r2'], writes=['qr'])
                            c.op('act', lambda e: e.activation(out=qr[:, :, 0:64], in_=qtmp[:, :, 0:64], func=AF.Copy), reads=['qtmp'], writes=['qr'])
                            tp = pall[:, 0:1024].bitcast(BF16) if which == 0 else pall[:, 1024:2048].bitcast(BF16)
                            tres = ['P0', 'P1'] if which == 0 else ['P2', 'P3']
                            for h in range(16):
                                c.mm(lambda e: e.transpose(out=tp[0:96, h * 128:(h + 1) * 128], in_=qr[:, h, :], identity=ident_b[:]),
                                     reads=['qr', 'ident_b'], writes=tres, last=(h == 15))
                            dst = qT if which == 0 else kTc
                            c.op('act' if which == 0 else 'dve', (lambda e: e.activation(out=dst[:, :, t * 128:(t + 1) * 128], in_=tp[0:96, :].rearrange("p (h n) -> p h n", h=16), func=AF.Copy))
                                 if which == 0 else (lambda e: e.tensor_copy(out=dst[:, :, t * 128:(t + 1) * 128], in_=tp[0:96, :].rearrange("p (h n) -> p h n", h=16))),
                                 reads=tres, writes=['qT' if which == 0 else 'kTc'])
                    if blk < NB - 1:
                        c.dma('pool', kc_d[:, :, blk * BLK:(blk + 1) * BLK].rearrange("h d s -> d h s"), kTc[:], reads=['kTc'], writes=['kc%d' % blk], key='kcw')
                    def kissue(hp):
                        if blk == 0 or hp >= 8:
                            return
                        c.dma('sp', kst2[hp % 2][:, :, 0:blk * BLK], kc_d[2 * hp:2 * hp + 2, :, 0:blk * BLK].rearrange("h d s -> d h s"),
                              reads=['kc%d' % bp_ for bp_ in range(blk)], writes=['kstp%d' % (hp % 2)], key='kst%d' % (hp % 2))
                    kissue(0)
                    work = []
                    for h in range(16):
                        kts = []
                        for bp in range(blk):
                            for j in range(TB):
                                kts.append((kst2[(h // 2) % 2][:, h % 2, bp * BLK + j * 128: bp * BLK + (j + 1) * 128], 'kstp%d' % ((h // 2) % 2), bp * TB + j, 0, False))
                        for j in range(TB):
                            kts.append((kTc[:, h, j * 128:(j + 1) * 128], 'kTc', blk * TB + j, j * 128, True))
                        for idx, kt_ in enumerate(kts):
                            work.append((h, idx, len(kts)) + kt_)

                    def emit_S(w):
                        h, idx, nk, kap, kres, tgk, q0, diag = work[w]
                        if idx == 0 and h % 2 == 0:
                            kissue(h // 2 + 1)
                        sbk = w % 2
                        p_ = pts[w % 3]; pn = 'pt%d' % (w % 3)
                        c.mm(lambda e: e.matmul(PB(sbk, q0, BLK), lhsT=kap, rhs=qT[:, h, q0:BLK], start=True, stop=True), reads=[kres, 'qT'], writes=['P%d' % sbk], last=True)
                        c.op('act', lambda e: e.activation(out=p_[:, q0:BLK], in_=PB(sbk, q0, BLK), func=AF.Exp, scale=SCALE), reads=['P%d' % sbk], writes=[pn])
                        if diag:
                            c.op('pool', lambda e: e.tensor_tensor(out=p_[:, q0:q0 + 128], in0=p_[:, q0:q0 + 128], in1=cmask_b[:], op=ALU.mult), reads=[pn, 'cmask_b'], writes=[pn])

                    def emit_PV(w):
                        h, idx, nk, kap, kres, tgk, q0, diag = work[w]
                        ob = 4 + (h % 2) * 2
                        first = idx == 0; lastk = idx == nk - 1
                        p_ = pts[w % 3]; pn = 'pt%d' % (w % 3)
                        c.mm(lambda e: e.matmul(PB(ob, q0, BLK, 0, 64), lhsT=Vc[:, tgk, h, :], rhs=p_[:, q0:BLK], start=first, stop=lastk),
                             reads=['Vc%d' % tgk, pn], writes=['P%d' % ob], last=lastk)
                        c.mm(lambda e: e.matmul(PB(ob + 1, q0, BLK, 0, 64), lhsT=ones_b[:, 0:64], rhs=p_[:, q0:BLK], start=first, stop=lastk),
                             reads=['ones_b', pn], writes=['P%d' % (ob + 1)], last=lastk)
                        if lastk:
                            c.op('dve', lambda e: e.reciprocal(out=rD[:], in_=PB(ob + 1, 0, BLK, 0, 64)), reads=['P%d' % (ob + 1)], writes=['rD'])
                            c.op('dve', lambda e: e.tensor_tensor(out=attnT[:, h, :], in0=PB(ob, 0, BLK, 0, 64), in1=rD[:], op=ALU.mult), reads=['P%d' % ob, 'rD'], writes=['attnT'])
                    emit_S(0)
                    for w in range(len(work)):
                        if w + 1 < len(work):
                            emit_S(w + 1)
                        emit_PV(w)
                    dump("attnT", attnT[:, 0, :], ['attnT'])
                    dump("qT", qT[:, 0, :], ['qT'])
                    dump("kT", kTc[:, 0, :], ['kTc'])
                    mixb = sbB("mixb", [128, 8, BLK], BF16)
                    for mb in range(4):
                        wv, wr = wload("w_mla_proj", 64, 16, [(mb * 256, 256)])
                        wg, wgr = wload("w_in", 128, 8, [(OFF_G + 1024 + mb * 256, 256)])
                        for mm_ in range(2):
                            m = mb * 2 + mm_
                            for kt in range(16):
                                c.mm(lambda e: e.matmul(PB(0, 0, BLK), lhsT=wv[:, kt, mm_ * 128:(mm_ + 1) * 128], rhs=attnT[:, kt, :], start=(kt == 0), stop=(kt == 15)),
                                     reads=['attnT', wr], writes=['P0'], last=(kt == 15))
                            for kt in range(8):
                                c.mm(lambda e: e.matmul(PB(1, 0, BLK), lhsT=wg[:, kt, mm_ * 128:(mm_ + 1) * 128], rhs=hT[:, kt, :], start=(kt == 0), stop=(kt == 7)),
                                     reads=['hT', wgr], writes=['P1'], last=(kt == 7))
                            gs = gsb2[m % 2]
                            c.op('act', lambda e: e.activation(out=gs[:], in_=PB(1, 0, BLK), func=AF.Sigmoid, bias=PR["p_gateb"][:, 8 + m:9 + m], scale=1.0),
                                 reads=['P1', 'p_gateb'], writes=['gsb%d' % (m % 2)])
                            c.op('dve', lambda e: e.tensor_tensor(out=tmpm[:], in0=PB(0, 0, BLK), in1=gs[:], op=ALU.mult), reads=['P0', 'gsb%d' % (m % 2)], writes=['tmpm'])
                            c.op('pool', lambda e: e.tensor_tensor(out=mixb[:, m, :], in0=tmpm[:], in1=mix[:, m, :], op=ALU.add), reads=['tmpm', 'mix%d' % m], writes=['mixb'])
                    for nb in range(2):
                        wv, wr = wload("w_o", 128, 8, [(nb * 512, 512)])
                        for t in range(TB):
                            pb = 2 + t % 2
                            for kt in range(8):
                                c.mm(lambda e: e.matmul(PB(pb), lhsT=mixb[:, kt, t * 128:(t + 1) * 128], rhs=wv[:, kt, :], start=(kt == 0), stop=(kt == 7)),
                                     reads=['mixb', wr], writes=['P%d' % pb], last=(kt == 7))
                            c.op('dve', lambda e: e.tensor_tensor(out=xbuf[:, t, nb * 512:(nb + 1) * 512], in0=PB(pb), in1=xbuf[:, t, nb * 512:(nb + 1) * 512], op=ALU.add),
                                 reads=['P%d' % pb, 'xbuf%d' % t], writes=['xbuf%d' % t])
                dump("x1", xbuf[:, 0, :], ['xbuf0'])
                c.barrier()
                with ExitStack() as esA:
                    def sbC(name, shape, dt=F32):
                        return esA.enter_context(nc.sbuf_tensor(un(name), list(shape), dt))
                    xn_bufs = [sbC("xnC%d" % i, [128, 1024], BF16) for i in range(2)]
                    actT = sbC("actT", [128, 22, BLK], BF16)
                    stg = [sbC("stgC%d" % i, [128, BLK + 2]) for i in range(4)]
                    accs = [sbC("accC%d" % i, [128, BLK]) for i in range(4)]
                    sg = [sbC("sg%d" % i, [128, BLK]) for i in range(2)]
                    rmsnorm_to_hT("p_gffn", xn_bufs)
                    for ch in range(11):
                        wv, wr = wload("w_up", 128, 8, [(ch * 256, 256), (D_FF + ch * 256, 256)])
                        for jj in range(2):
                            j = ch * 2 + jj
                            items = []
                            for half in range(2):
                                pb = (j % 2) * 2 + half
                                jc = j + 22 * half
                                for kt in range(8):
                                    c.mm(lambda e: e.matmul(PB(pb, 0, BLK), lhsT=wv[:, kt, half * 256 + jj * 128: half * 256 + (jj + 1) * 128], rhs=hT[:, kt, :],
                                                            start=(kt == 0), stop=(kt == 7)), reads=['hT', wr], writes=['P%d' % pb], last=(kt == 7))
                                si = (j % 2) * 2 + half
                                items.append((PB(pb, 0, BLK), stg[si], halo_f[:, jc, :], PR["p_cw_ffn"][:, jc, :], PR["p_cb_ffn"][:, jc:jc + 1],
                                              accs[si], 'stgC%d' % si, 'accC%d' % si, 'halo_f%d' % jc, 'P%d' % pb))
                            conv_multi(items, 2)
                            sgi = sg[j % 2]
                            c.op('act', lambda e: e.activation(out=sgi[:], in_=accs[(j % 2) * 2][:], func=AF.Silu), reads=['accC%d' % ((j % 2) * 2)], writes=['sg%d' % (j % 2)])
                            c.op('pool', lambda e: e.tensor_tensor(out=actT[:, j, :], in0=sgi[:], in1=accs[(j % 2) * 2 + 1][:], op=ALU.mult),
                                 reads=['sg%d' % (j % 2), 'accC%d' % ((j % 2) * 2 + 1)], writes=['actT'])
                    dump("actT", actT[:, 0, :], ['actT'])
                    for m4 in range(4):
                        for kh in range(2):
                            wv, wr = wload("w_down", 128, 11, [(m4 * 256, 256)], r0=kh * 1408)
                            for t in range(TB):
                                for kt in range(11):
                                    c.mm(lambda e: e.matmul(PB(4 + t, 0, 256), lhsT=actT[:, kh * 11 + kt, t * 128:(t + 1) * 128], rhs=wv[:, kt, :],
                                                            start=(kh == 0 and kt == 0), stop=(kh == 1 and kt == 10)),
                                         reads=['actT', wr], writes=['P%d' % (4 + t)], last=(kt == 10))
                        for t in range(TB):
                            c.op('dve', lambda e: e.tensor_tensor(out=xbuf[:, t, m4 * 256:(m4 + 1) * 256], in0=PB(4 + t, 0, 256),
                                                                  in1=xbuf[:, t, m4 * 256:(m4 + 1) * 256], op=ALU.add),
                                 reads=['P%d' % (4 + t), 'xbuf%d' % t], writes=['xbuf%d' % t])
                    c.dma('pool', out_d[tok0:tok0 + BLK, :].rearrange("(t p) f -> p t f", p=128), xbuf[:], reads=['xbuf%d' % t for t in range(TB)], key='ost')
                c.barrier()
        c.barrier()
        print("ninst", c.ninst, "cnt", c.cnt)
    return nc


def kernel(**inputs):
    x = np.asarray(inputs["x"], dtype=np.float32)
    B, S, D = x.shape
    NSEQ = B // NCORES
    nc = build(NSEQ=NSEQ, S=S, TB=2)
    consts = host_consts(S)
    params = host_params(inputs)
    maps = []
    for i in range(NCORES):
        m = {"x": np.ascontiguousarray(x[i * NSEQ:(i + 1) * NSEQ].reshape(NSEQ * S, D))}
        for k in WSHAPES:
            m[k] = np.ascontiguousarray(np.asarray(inputs[k], dtype=np.float32)[0])
        m.update(params)
        m.update(consts)
        maps.append(m)
    res = run_bass_kernel_spmd(nc, maps, core_ids=list(range(NCORES)))
    out = np.concatenate([np.asarray(r["out"]).reshape(NSEQ, S, D) for r in res.results], axis=0)
    return out.astype(np.float32)
```

```python
import numpy as np
from contextlib import ExitStack
import concourse.bass as bass
import concourse.mybir as mybir
from concourse.bass_utils import run_bass_kernel_spmd

F32 = mybir.dt.float32
BF16 = mybir.dt.bfloat16
AF = mybir.ActivationFunctionType
ALU = mybir.AluOpType
AX = mybir.AxisListType

D_MODEL = 1024
D_INNER = 2048
IN_COLS = 7616
D_FF = 2816
EPS = 1e-6
OFF_XBC = 2048
OFF_DT = 5120
OFF_G = 5568
NCORES = 8
WSHAPES = {
    "w_in": (1024, IN_COLS), "w_ssd_proj": (2048, 1024), "w_uq": (256, 1536), "w_ukv": (128, 2048),
    "w_mla_proj": (1024, 1024), "w_o": (1024, 1024), "w_up": (1024, 2 * D_FF), "w_down": (D_FF, 1024),
}
SLOT_ELEMS = 4096


def weight_chunks():
    ch = [("w_in", 128, 8, 0, ((OFF_DT, 448),))]
    ch += [("w_in", 128, 8, 0, ((OFF_XBC + i * 512, 512),)) for i in range(6)]
    ch += [("w_in", 128, 8, 0, ((i * 512, 512),)) for i in range(4)]
    for mb in range(4):
        ch += [("w_ssd_proj", 128, 16, 0, ((mb * 256, 256),)), ("w_in", 128, 8, 0, ((OFF_G + mb * 256, 256),))]
    ch += [("w_uq", 128, 2, 0, ((0, 1536),)), ("w_ukv", 128, 1, 0, ((0, 2048),))]
    for mb in range(4):
        ch += [("w_mla_proj", 64, 16, 0, ((mb * 256, 256),)), ("w_in", 128, 8, 0, ((OFF_G + 1024 + mb * 256, 256),))]
    ch += [("w_o", 128, 8, 0, ((nb * 512, 512),)) for nb in range(2)]
    ch += [("w_up", 128, 8, 0, ((i * 256, 256), (D_FF + i * 256, 256))) for i in range(11)]
    for m4 in range(4):
        for kh in range(2):
            ch.append(("w_down", 128, 11, kh * 1408, ((m4 * 256, 256),)))
    return ch
_UID = [0]


def un(name):
    _UID[0] += 1
    return "%s_u%d" % (name, _UID[0])

NSLOT = 4


class Ctx:
    def __init__(self, nc, es):
        self.nc = nc
        self.es = es
        self.engs = {'pe': nc.tensor, 'act': nc.scalar, 'dve': nc.vector, 'pool': nc.gpsimd, 'sp': nc.sync}
        self.sem = {}
        for k in ('pe', 'act', 'dve', 'pool'):
            self.sem[k] = es.enter_context(nc.semaphore("s_" + k))
        self.cnt = {k: 0 for k in self.sem}
        self.waited = {k: {} for k in self.engs}
        self.lastw = {}
        self.readers = {}
        self.dsem = {}
        self.dcnt = {}
        self.ninst = {k: 0 for k in self.engs}
        self.rr = 0

    def sb(self, name, shape, dt=F32):
        return self.es.enter_context(self.nc.sbuf_tensor(name, list(shape), dt))

    def _deps(self, reads, writes):
        deps = {}

        def add(d):
            if d is None:
                return
            k, v = d
            if deps.get(k, 0) < v:
                deps[k] = v
        for r in reads:
            add(self.lastw.get(r))
        for w in writes:
            add(self.lastw.get(w))
            for d in self.readers.get(w, ()):
                add(d)
        return deps

    def _wait(self, eng, deps):
        h = self.engs[eng]
        wd = self.waited[eng]
        for k, v in deps.items():
            if wd.get(k, 0) >= v:
                continue
            if k == 'pe' and eng == 'pe':
                continue
            s = self.sem[k] if k in self.sem else self.dsem[k]
            h.wait_ge(s, v)
            wd[k] = v
            self.ninst[eng] += 1

    def _commit(self, tok, reads, writes):
        for r in reads:
            self.readers.setdefault(r, []).append(tok)
        for w in writes:
            self.lastw[w] = tok
            self.readers[w] = []

    def op(self, eng, fn, reads=(), writes=()):
        self._wait(eng, self._deps(reads, writes))
        ins = fn(self.engs[eng])
        self.cnt[eng] += 1
        self.ninst[eng] += 1
        ins.then_inc(self.sem[eng], 1)
        self._commit((eng, self.cnt[eng]), reads, writes)
        return ins

    def any2(self, fn, reads=(), writes=()):
        self.rr += 1
        return self.op('dve' if self.rr % 3 else 'pool', fn, reads, writes)

    def mm(self, fn, reads=(), writes=(), last=True):
        self._wait('pe', self._deps(reads, writes))
        ins = fn(self.engs['pe'])
        self.ninst['pe'] += 1
        if last:
            self.cnt['pe'] += 1
            ins.then_inc(self.sem['pe'], 1)
            tok = ('pe', self.cnt['pe'])
        else:
            tok = ('pe', self.cnt['pe'] + 1)
        self._commit(tok, reads, writes)
        return ins

    def dma(self, q, out, in_, reads=(), writes=(), key=None, **kw):
        if key not in self.dsem:
            self.dsem[key] = self.es.enter_context(self.nc.semaphore("d_" + key))
            self.dcnt[key] = 0
        self._wait(q, self._deps(reads, writes))
        ins = self.engs[q].dma_start(out=out, in_=in_, **kw)
        self.dcnt[key] += 16
        self.ninst[q] += 1
        ins.then_inc(self.dsem[key], 16)
        self._commit((key, self.dcnt[key]), reads, writes)
        return ins

    def barrier(self):
        deps = {k: self.cnt[k] for k in self.sem if self.cnt[k]}
        for k in self.dsem:
            deps[k] = self.dcnt[k]
        for e in self.engs:
            self._wait(e, dict(deps))


def host_consts(S):
    import ml_dtypes
    i = np.arange(128)
    c = {}
    c["c_ident"] = np.eye(128, dtype=np.float32)
    c["c_tri_incl"] = (i[:, None] <= i[None, :]).astype(np.float32)
    c["c_tri_strict"] = (i[:, None] > i[None, :]).astype(np.float32)
    neg = np.where(i[None, :] < i[:, None], -30000.0, 0.0).astype(np.float32)
    c["c_negmask"] = np.tile(neg, (1, 4))
    inv = (1.0 / (np.float32(10000.0) ** (np.arange(0, 32, 2, dtype=np.float32) / np.float32(32)))).astype(np.float32)
    ang = np.arange(S, dtype=np.float32)[:, None] * inv[None, :]
    cos = np.cos(ang).astype(np.float32).reshape(S // 128, 128, 16).transpose(1, 0, 2)
    sin = np.sin(ang).astype(np.float32).reshape(S // 128, 128, 16).transpose(1, 0, 2)
    c["c_cos"] = np.ascontiguousarray(cos)
    c["c_sin"] = np.ascontiguousarray(sin)
    return c


def host_params(inp):
    f = lambda a: np.ascontiguousarray(np.asarray(a, dtype=np.float32))
    rep = lambda v: f(np.broadcast_to(np.asarray(v, np.float32).reshape(1, -1), (128, v.size)))
    colT = lambda v: f(np.asarray(v, np.float32).reshape(-1, 128).T)
    p = {}
    p["p_gmix"] = colT(inp["norm_mix_g"][0])
    p["p_cw_ssd"] = f(np.asarray(inp["conv_ssd_w"][0]).reshape(4, 24, 128).transpose(2, 1, 0))
    p["p_cb_ssd"] = colT(inp["conv_ssd_b"][0])
    p["p_dtb"] = rep(inp["dt_bias"][0])
    p["p_alog"] = rep(inp["a_log"][0])
    p["p_dskip"] = colT(np.repeat(np.asarray(inp["d_skip"][0]), 64))
    p["p_gssd"] = colT(inp["ssd_norm_g"][0])
    p["p_gateb"] = colT(inp["gate_b"][0])
    p["p_gffn"] = colT(inp["norm_ffn_g"][0])
    p["p_cw_ffn"] = f(np.asarray(inp["conv_ffn_w"][0]).reshape(3, 44, 128).transpose(2, 1, 0))
    p["p_cb_ffn"] = colT(inp["conv_ffn_b"][0])
    p["p_gqa"] = rep(inp["q_a_norm_g"][0])
    p["p_gkva"] = rep(inp["kv_a_norm_g"][0])
    p["p_gq"] = rep(inp["q_norm_g"][0])
    p["p_gk"] = rep(inp["k_norm_g"][0])
    return p


PSHAPES = {
    "p_gmix": (128, 8), "p_cw_ssd": (128, 24, 4), "p_cb_ssd": (128, 24), "p_dtb": (128, 32), "p_alog": (128, 32),
    "p_dskip": (128, 16), "p_gssd": (128, 16), "p_gateb": (128, 16), "p_gffn": (128, 8),
    "p_cw_ffn": (128, 44, 3), "p_cb_ffn": (128, 44), "p_gqa": (128, 256), "p_gkva": (128, 128),
    "p_gq": (128, 96), "p_gk": (128, 96),
}


def CSHAPES(S):
    return {"c_ident": (128, 128), "c_tri_incl": (128, 128), "c_tri_strict": (128, 128), "c_negmask": (128, 512),
            "c_cos": (128, S // 128, 16), "c_sin": (128, S // 128, 16)}


def build(NSEQ=4, S=2048, TB=2, dbg=(), stop_after=None):
    BLK = TB * 128
    NT = S // 128
    NB = S // BLK
    nc = bass.Bass("TRN2", target_bir_lowering=False)

    def din(name, shape, dt=F32):
        return nc.dram_tensor(name, list(shape), dt, kind="ExternalInput").ap()

    x_d = din("x", [NSEQ * S, D_MODEL])
    out_d = nc.dram_tensor("out", [NSEQ * S, D_MODEL], F32, kind="ExternalOutput").ap()
    W = {k: din(k, v) for k, v in WSHAPES.items()}
    CHUNKS = weight_chunks()
    NCH = len(CHUNKS)
    NTOTCH = NCH * NSEQ * (S // (TB * 128))
    Wc = nc.dram_tensor("wchunks_bf", [NCH, 128, SLOT_ELEMS], BF16, kind="Internal").ap()
    Pd = {k: din(k, v) for k, v in PSHAPES.items()}
    Cd = {k: din(k, v) for k, v in CSHAPES(S).items()}
    kc_d = nc.dram_tensor("kcache", [16, 96, S], BF16, kind="Internal").ap()
    acs_d = [nc.dram_tensor("acs%d" % i, [32, 2, 128], BF16, kind="Internal").ap() for i in range(2)]
    dbg_d = {k: nc.dram_tensor("dbg_" + k, list(shp), dt_, kind="ExternalOutput").ap() for k, shp, dt_ in dbg}

    with ExitStack() as es:
        c = Ctx(nc, es)
        pall = es.enter_context(nc.psum_tensor("pall", [128, 4096], F32))

        def PB(b, n0=0, n1=512, p0=0, p1=128):
            return pall[p0:p1, b * 512 + n0: b * 512 + n1]

        def PBh(b):
            return pall[:, b * 512:(b + 1) * 512].bitcast(BF16)

        ident_f = c.sb("ident_f", [128, 128]); ident_b = c.sb("ident_b", [128, 128], BF16)
        triI = c.sb("triI", [128, 128]); triS = c.sb("triS", [128, 128])
        ones_f = c.sb("ones_f", [128, 128]); ones_b = c.sb("ones_b", [128, 128], BF16)
        negm_f = c.sb("negm_f", [128, 512]); negm_b = c.sb("negm_b", [128, 512], BF16)
        cmask_b = c.sb("cmask_b", [128, 128], BF16)
        cos_t = c.sb("cos_t", [128, NT, 16]); sin_t = c.sb("sin_t", [128, NT, 16])
        PR = {k: c.sb("sb_" + k, list(v)) for k, v in PSHAPES.items()}
        A_rep = c.sb("A_rep", [128, 32])
        xbuf = c.sb("xbuf", [128, TB, 1024])
        hT = c.sb("hT", [128, 8, BLK], BF16)
        Vc = c.sb("Vc", [128, NT, 16, 64], BF16)
        state = c.sb("state", [128, 4, 512]); state_b = c.sb("state_b", [128, 4, 512], BF16)
        halo_s = c.sb("halo_s", [128, 24, 3]); halo_f = c.sb("halo_f", [128, 44, 2])
        wslot = [c.sb("wslot%d" % i, [128, SLOT_ELEMS], BF16) for i in range(NSLOT)]
        mix = c.sb("mix", [128, 8, BLK])
        small = c.sb("small", [128, TB, 448])
        ss1 = c.sb("ss1", [128, 8]); rs1 = c.sb("rs1", [128, 8])
        junk = c.sb("junk", [128, 1024], BF16)
        rhs2 = c.sb("rhs2", [2, 4096], BF16)
        ones2 = c.sb("ones2", [2, 128], BF16)

        c.dma('sp', ident_f[:], Cd["c_ident"], writes=['ident_f'], key='ld')
        c.dma('sp', triI[:], Cd["c_tri_incl"], writes=['triI'], key='ld')
        c.dma('sp', triS[:], Cd["c_tri_strict"], writes=['triS'], key='ld')
        c.dma('sp', negm_f[:], Cd["c_negmask"], writes=['negm_f'], key='ld')
        c.dma('sp', cos_t[:], Cd["c_cos"], writes=['cos'], key='ld')
        c.dma('sp', sin_t[:], Cd["c_sin"], writes=['sin'], key='ld')
        for k in PSHAPES:
            c.dma('sp', PR[k][:], Pd[k], writes=[k], key='ld')
        c.barrier()
        c.op('dve', lambda e: e.tensor_copy(out=ident_b[:], in_=ident_f[:]), reads=['ident_f'], writes=['ident_b'])
        c.op('dve', lambda e: e.tensor_copy(out=cmask_b[:], in_=triI[:]), reads=['triI'], writes=['cmask_b'])
        c.op('dve', lambda e: e.tensor_copy(out=negm_b[:], in_=negm_f[:]), reads=['negm_f'], writes=['negm_b'])
        c.op('dve', lambda e: e.memset(ones_f[:], 1.0), writes=['ones_f'])
        c.op('dve', lambda e: e.memset(ones_b[:], 1.0), writes=['ones_b'])
        c.op('dve', lambda e: e.memset(ones2[:], 1.0), writes=['ones2'])
        c.op('act', lambda e: e.activation(out=A_rep[:], in_=PR["p_alog"][:], func=AF.Exp), reads=['p_alog'], writes=['A_rep'])
        c.op('dve', lambda e: e.tensor_scalar(out=A_rep[:], in0=A_rep[:], scalar1=-1.0, scalar2=None, op0=ALU.mult),
             reads=['A_rep'], writes=['A_rep'])

        with ExitStack() as es2:
            st32 = [es2.enter_context(nc.sbuf_tensor("st32_%d" % i, [128, SLOT_ELEMS], F32)) for i in range(2)]
            st16 = [es2.enter_context(nc.sbuf_tensor("st16_%d" % i, [128, SLOT_ELEMS], BF16)) for i in range(2)]
            for ci, (name, P_, KT, r0, segs) in enumerate(CHUNKS):
                i = ci % 2
                ntot = sum(n for _, n in segs)
                assert KT * ntot <= SLOT_ELEMS
                v32 = st32[i][0:P_, 0:KT * ntot].rearrange("p (k n) -> p k n", k=KT)
                o = 0
                for (c0, n) in segs:
                    c.dma('sp', v32[:, :, o:o + n], W[name][r0:r0 + KT * P_, c0:c0 + n].rearrange("(k p) n -> p k n", p=P_),
                          writes=['st32_%d' % i], key='pl%d' % i)
                    o += n
                eng = ('dve', 'pool', 'act')[ci % 3]
                if eng == 'act':
                    c.op('act', lambda e: e.activation(out=st16[i][0:P_, 0:KT * ntot], in_=st32[i][0:P_, 0:KT * ntot], func=AF.Copy),
                         reads=['st32_%d' % i], writes=['st16_%d' % i])
                else:
                    c.op(eng, lambda e: e.tensor_copy(out=st16[i][0:P_, 0:KT * ntot], in_=st32[i][0:P_, 0:KT * ntot]),
                         reads=['st32_%d' % i], writes=['st16_%d' % i])
                c.dma('sp', Wc[ci, 0:P_, 0:KT * ntot], st16[i][0:P_, 0:KT * ntot], reads=['st16_%d' % i], writes=['wc%d' % ci], key='ps%d' % i)
        c.barrier()

        wstate = {'next': 0, 'loaded': 0}

        def wissue(upto):
            while wstate['loaded'] < upto:
                gi = wstate['loaded']
                ci = gi % NCH
                name, P_, KT, r0, segs = CHUNKS[ci]
                ntot = sum(n for _, n in segs)
                i = gi % NSLOT
                c.dma('sp', wslot[i][0:P_, 0:KT * ntot], Wc[ci, 0:P_, 0:KT * ntot], reads=['wc%d' % ci], writes=['wslot%d' % i], key='w%d' % i)
                wstate['loaded'] += 1

        def wload(name, P_, KT, segs, r0=0):
            gi = wstate['next']
            wstate['next'] += 1
            spec = CHUNKS[gi % NCH]
            assert spec == (name, P_, KT, r0, tuple(segs)), (spec, name, segs)
            wissue(min(gi + 3, NTOTCH))
            ntot = sum(n for _, n in segs)
            i = gi % NSLOT
            return wslot[i][0:P_, 0:KT * ntot].rearrange("p (k n) -> p k n", k=KT), 'wslot%d' % i

        def rstd(ss_ap, out_ap, n, reads, writes):
            c.op('act', lambda e: e.activation(out=out_ap, in_=ss_ap, func=AF.Sqrt, bias=EPS, scale=1.0 / n), reads=reads, writes=writes)
            c.op('dve', lambda e: e.reciprocal(out=out_ap, in_=out_ap), reads=writes, writes=writes)

        def rmsnorm_to_hT(gname, xn_bufs):
            for t in range(TB):
                c.op('act', lambda e: e.activation(out=junk[:], in_=xbuf[:, t, :], func=AF.Square, accum_out=ss1[:, t:t + 1]),
                     reads=['xbuf%d' % t], writes=['junk', 'ss1_%d' % t])
                rstd(ss1[:, t:t + 1], rs1[:, t:t + 1], 1024, ['ss1_%d' % t], ['rs1_%d' % t])
                xn = xn_bufs[t % 2]
                c.op('act', lambda e: e.activation(out=xn[:], in_=xbuf[:, t, :], func=AF.Copy, scale=rs1[:, t:t + 1]),
                     reads=['xbuf%d' % t, 'rs1_%d' % t], writes=['xn%d' % (t % 2)])
                pt = PBh(t % 2)
                for kt in range(8):
                    c.mm(lambda e: e.transpose(out=pt[:, kt * 128:(kt + 1) * 128], in_=xn[:, kt * 128:(kt + 1) * 128], identity=ident_b[:]),
                         reads=['xn%d' % (t % 2), 'ident_b'], writes=['P%d' % (t % 2)], last=(kt == 7))
                c.op('dve', lambda e: e.tensor_tensor(out=hT[:, :, t * 128:(t + 1) * 128], in0=pt.rearrange("p (k n) -> p k n", k=8),
                                                      in1=PR[gname][:, :].unsqueeze(2).to_broadcast([128, 8, 128]), op=ALU.mult),
                     reads=['P%d' % (t % 2), gname], writes=['hT'])

        def conv_multi(items, nh):
            for (ps_ap, stage, halo_ap, wt, bt, acc, sname, aname, hname, pname) in items:
                c.op('act', lambda e: e.activation(out=stage[:, 0:nh], in_=halo_ap, func=AF.Copy), reads=[hname], writes=[sname + 'h'])
            for (ps_ap, stage, halo_ap, wt, bt, acc, sname, aname, hname, pname) in items:
                c.op('act', lambda e: e.activation(out=stage[:, nh:nh + BLK], in_=ps_ap, func=AF.Copy), reads=[pname], writes=[sname])
            for (ps_ap, stage, halo_ap, wt, bt, acc, sname, aname, hname, pname) in items:
                c.op('act', lambda e: e.activation(out=halo_ap, in_=stage[:, BLK:BLK + nh], func=AF.Copy), reads=[sname, sname + 'h'], writes=[hname])
            for (ps_ap, stage, halo_ap, wt, bt, acc, sname, aname, hname, pname) in items:
                c.op('pool', lambda e: e.tensor_scalar(out=acc[:], in0=stage[:, nh:nh + BLK], scalar1=wt[:, nh:nh + 1], scalar2=bt,
                                                       op0=ALU.mult, op1=ALU.add), reads=[sname], writes=[aname])
            for k in range(nh - 1, -1, -1):
                for (ps_ap, stage, halo_ap, wt, bt, acc, sname, aname, hname, pname) in items:
                    c.op('dve', lambda e: e.scalar_tensor_tensor(out=acc[:], in0=stage[:, k:k + BLK], scalar=wt[:, k:k + 1], in1=acc[:],
                                                                 op0=ALU.mult, op1=ALU.add), reads=[sname, sname + 'h', aname], writes=[aname])

        def dump(name, ap_sb, res):
            if name in dbg_d:
                c.dma('sp', dbg_d[name], ap_sb, reads=res, key='dbg')

        nchunk = 0
        for seq in range(NSEQ):
            for blk in range(NB):
                tok0 = seq * S + blk * BLK
                c.dma('sp', xbuf[:], x_d[tok0:tok0 + BLK, :].rearrange("(t p) f -> p t f", p=128),
                      writes=['xbuf%d' % t for t in range(TB)], key='xld')
                if blk == 0:
                    c.op('pool', lambda e: e.memset(state[:], 0.0), writes=['state%d' % g for g in range(4)])
                    c.op('pool', lambda e: e.memset(state_b[:], 0.0), writes=['stateb%d' % g for g in range(4)])
                    c.op('pool', lambda e: e.memset(halo_s[:], 0.0), writes=['halo_s%d' % j for j in range(24)])
                    c.op('pool', lambda e: e.memset(halo_f[:], 0.0), writes=['halo_f%d' % j for j in range(44)])
                with ExitStack() as esA:
                    def sbA(name, shape, dt=F32):
                        return esA.enter_context(nc.sbuf_tensor(un(name), list(shape), dt))
                    xn_bufs = [sbA("xnA%d" % i, [128, 1024], BF16) for i in range(2)]
                    xbcT = sbA("xbcT", [128, 24, BLK], BF16)
                    szT = sbA("szT", [128, 16, BLK], BF16)
                    ynT = sbA("ynT", [128, 16, BLK], BF16)
                    stg = [sbA("stg%d" % i, [128, BLK + 3]) for i in range(4)]
                    accs = [sbA("acc%d" % i, [128, BLK]) for i in range(4)]
                    dtt = sbA("dtt", [128, TB, 32]); a_tok = sbA("a_tok", [128, TB, 32])
                    rmsnorm_to_hT("p_gmix", xn_bufs)
                    dump("hT", hT[:, 0, :], ['hT'])
                    wv, wr = wload("w_in", 128, 8, [(OFF_DT, 448)])
                    for t in range(TB):
                        for kt in range(8):
                            c.mm(lambda e: e.matmul(PB(2 + t % 2, 0, 448), lhsT=hT[:, kt, t * 128:(t + 1) * 128], rhs=wv[:, kt, :],
                                                    start=(kt == 0), stop=(kt == 7)), reads=['hT', wr], writes=['P%d' % (2 + t % 2)], last=(kt == 7))
                        c.op('act', lambda e: e.activation(out=small[:, t, :], in_=PB(2 + t % 2, 0, 448), func=AF.Copy),
                             reads=['P%d' % (2 + t % 2)], writes=['small%d' % t])
                    allsmall = ['small%d' % t for t in range(TB)]
                    c.op('dve', lambda e: e.tensor_tensor(out=dtt[:], in0=small[:, :, 0:32], in1=PR["p_dtb"][:, :].unsqueeze(1).to_broadcast([128, TB, 32]),
                                                          op=ALU.add), reads=allsmall + ['p_dtb'], writes=['dtt'])
                    c.op('act', lambda e: e.activation(out=dtt[:], in_=dtt[:], func=AF.Exp), reads=['dtt'], writes=['dtt'])
                    c.op('act', lambda e: e.activation(out=dtt[:], in_=dtt[:], func=AF.Ln, bias=1.0, scale=1.0), reads=['dtt'], writes=['dtt'])
                    c.op('dve', lambda e: e.tensor_tensor(out=a_tok[:], in0=dtt[:], in1=A_rep[:, :].unsqueeze(1).to_broadcast([128, TB, 32]),
                                                          op=ALU.mult), reads=['dtt', 'A_rep'], writes=['a_tok'])
                    for ch in range(6):
                        wv, wr = wload("w_in", 128, 8, [(OFF_XBC + ch * 512, 512)])
                        for pr in range(2):
                            items = []
                            for jj in (2 * pr, 2 * pr + 1):
                                j = ch * 4 + jj
                                pb = j % 4
                                for kt in range(8):
                                    c.mm(lambda e: e.matmul(PB(pb, 0, BLK), lhsT=wv[:, kt, jj * 128:(jj + 1) * 128], rhs=hT[:, kt, :],
                                                            start=(kt == 0), stop=(kt == 7)), reads=['hT', wr], writes=['P%d' % pb], last=(kt == 7))
                                items.append((PB(pb, 0, BLK), stg[j % 4], halo_s[:, j, :], PR["p_cw_ssd"][:, j, :], PR["p_cb_ssd"][:, j:j + 1],
                                              accs[j % 4], 'stg%d' % (j % 4), 'acc%d' % (j % 4), 'halo_s%d' % j, 'P%d' % pb))
                            conv_multi(items, 3)
                            for jj in (2 * pr, 2 * pr + 1):
                                j = ch * 4 + jj
                                c.op('act', lambda e: e.activation(out=xbcT[:, j, :], in_=accs[j % 4][:], func=AF.Silu), reads=['acc%d' % (j % 4)], writes=['xbcT%d' % j])
                    dump("xbcT", xbcT[:, 0, :], ['xbcT0'])
                    for ch in range(4):
                        wv, wr = wload("w_in", 128, 8, [(ch * 512, 512)])
                        for jj in range(4):
                            j = ch * 4 + jj
                            pb = j % 2
                            for kt in range(8):
                                c.mm(lambda e: e.matmul(PB(pb, 0, BLK), lhsT=wv[:, kt, jj * 128:(jj + 1) * 128], rhs=hT[:, kt, :],
                                                        start=(kt == 0), stop=(kt == 7)), reads=['hT', wr], writes=['P%d' % pb], last=(kt == 7))
                            c.op('act', lambda e: e.activation(out=szT[:, j, :], in_=PB(pb, 0, BLK), func=AF.Silu), reads=['P%d' % pb], writes=['szT%d' % j])
                    with ExitStack() as esS:
                        def sbS(name, shape, dt=F32):
                            return esS.enter_context(nc.sbuf_tensor(un(name), list(shape), dt))
                        eac = sbS("eac", [128, 32]); dte = sbS("dte", [128, 32]); cdr = sbS("cdr", [128, 32]); nac = sbS("nac", [128, 32])
                        dtdte = sbS("dtdte", [128, 32])
                        hl = sbS("hl", [32, 2, 128], BF16); rres = sbS("rres", [32, 128])
                        sc = sbS("sc", [128, 4, 128])
                        xdt = sbS("xdt", [128, 32, 64], BF16); xdtd = sbS("xdtd", [128, 32, 64], BF16)
                        btok = sbS("btok", [128, 4, 128], BF16)
                        dec = [sbS("dec%d" % i, [128, 8, 128]) for i in range(2)]
                        LT = [sbS("LT%d" % i, [128, 8, 128], BF16) for i in range(2)]
                        yoff = sbS("yoff", [128, 8, 64], BF16)
                        tt = sbS("tt", [128, 16, 128]); sq = sbS("sq", [128, 16, 128], BF16)
                        xd = sbS("xd", [128, 4, 128]); sttmp = sbS("sttmp", [128, 512]); rrr = sbS("rrr", [128, 4, 128])
                        for t in range(TB):
                            cs = slice(t * 128, (t + 1) * 128)
                            par = nchunk % 2
                            nchunk += 1
                            a_c = a_tok[:, t, :]
                            c.mm(lambda e: e.matmul(PB(2, 0, 32), lhsT=triI[:], rhs=a_c, start=True, stop=True), reads=['a_tok', 'triI'], writes=['P2'], last=False)
                            c.mm(lambda e: e.matmul(PB(2, 32, 64), lhsT=triS[:], rhs=a_c, start=True, stop=True), reads=['a_tok', 'triS'], writes=['P2'], last=False)
                            c.mm(lambda e: e.matmul(PB(2, 64, 96), lhsT=ones_f[:], rhs=a_c, start=True, stop=True), reads=['a_tok', 'ones_f'], writes=['P2'], last=False)
                            c.mm(lambda e: e.matmul(PB(2, 128, 256, 0, 32), lhsT=a_c, rhs=triI[:], start=True, stop=True), reads=['a_tok', 'triI'], writes=['P2'], last=True)
                            c.op('act', lambda e: e.activation(out=eac[:], in_=PB(2, 0, 32), func=AF.Exp), reads=['P2'], writes=['eac'])
                            c.op('act', lambda e: e.activation(out=dte[:], in_=PB(2, 32, 64), func=AF.Exp), reads=['P2'], writes=['dte'])
                            c.op('act', lambda e: e.activation(out=cdr[:], in_=PB(2, 64, 96), func=AF.Exp), reads=['P2'], writes=['cdr'])
                            c.op('dve', lambda e: e.tensor_scalar(out=nac[:], in0=PB(2, 0, 32), scalar1=-1.0, scalar2=None, op0=ALU.mult), reads=['P2'], writes=['nac'])
                            c.op('dve', lambda e: e.tensor_tensor(out=dtdte[:], in0=dtt[:, t, :], in1=dte[:], op=ALU.mult), reads=['dtt', 'dte'], writes=['dtdte'])
                            c.op('dve', lambda e: e.tensor_copy(out=hl[:, 0, :], in_=PB(2, 128, 256, 0, 32)), reads=['P2'], writes=['hl0'])
                            c.op('dve', lambda e: e.tensor_tensor(out=rres[:], in0=PB(2, 128, 256, 0, 32), in1=hl[:, 0, :], op=ALU.subtract), reads=['P2', 'hl0'], writes=['rres'])
                            c.op('dve', lambda e: e.tensor_copy(out=hl[:, 1, :], in_=rres[:]), reads=['rres'], writes=['hl1'])
                            c.dma('pool', acs_d[par], hl[:], reads=['hl0', 'hl1'], writes=['acs%d' % par], key='acs_w')
                            c.dma('sp', rhs2[:, :].rearrange("j (h l) -> j h l", h=32), acs_d[par].rearrange("h j l -> j h l"), reads=['acs%d' % par], writes=['rhs2'], key='acs_r')
                            for g in range(4):
                                c.mm(lambda e: e.matmul(PB(3, g * 128, (g + 1) * 128), lhsT=xbcT[:, 16 + g, cs], rhs=xbcT[:, 20 + g, cs], start=True, stop=True),
                                     reads=['xbcT%d' % (16 + g), 'xbcT%d' % (20 + g)], writes=['P3'], last=(g == 3))
                            c.op('act', lambda e: e.activation(out=sc[:], in_=PB(3).rearrange("p (g n) -> p g n", g=4), func=AF.Copy), reads=['P3'], writes=['sc'])
                            xtp = pall[:, 4 * 512:6 * 512].bitcast(BF16)
                            for i in range(16):
                                c.mm(lambda e: e.transpose(out=xtp[:, i * 128:(i + 1) * 128], in_=xbcT[:, i, cs], identity=ident_b[:]),
                                     reads=['xbcT%d' % i, 'ident_b'], writes=['P4', 'P5'], last=(i == 15))
                            xtp3 = xtp.rearrange("p (h d) -> p h d", h=32)
                            c.op('dve', lambda e: e.tensor_tensor(out=xdt[:], in0=xtp3, in1=dtt[:, t, :].unsqueeze(2).to_broadcast([128, 32, 64]), op=ALU.mult),
                                 reads=['P4', 'P5', 'dtt'], writes=['xdt'])
                            c.op('dve', lambda e: e.tensor_tensor(out=xdtd[:], in0=xtp3, in1=dtdte[:, :].unsqueeze(2).to_broadcast([128, 32, 64]), op=ALU.mult),
                                 reads=['P4', 'P5', 'dtdte'], writes=['xdtd'])
                            btp = PBh(6)
                            for g in range(4):
                                c.mm(lambda e: e.transpose(out=btp[:, g * 128:(g + 1) * 128], in_=xbcT[:, 16 + g, cs], identity=ident_b[:]),
                                     reads=['xbcT%d' % (16 + g), 'ident_b'], writes=['P6'], last=(g == 3))
                            c.op('act', lambda e: e.activation(out=btok[:], in_=btp[:, 0:512].rearrange("p (g n) -> p g n", g=4), func=AF.Copy), reads=['P6'], writes=['btok'])
                            def stageA(g):
                                d_ = dec[g % 2]; L_ = LT[g % 2]
                                dn = 'dec%d' % (g % 2); Ln_ = 'LT%d' % (g % 2)
                                for half in range(2):
                                    h0 = g * 8 + half * 4
                                    c.mm(lambda e: e.matmul(PB(half), lhsT=ones2[:, :], rhs=rhs2[:, h0 * 128:(h0 + 4) * 128], start=True, stop=False),
                                         reads=['rhs2', 'ones2'], writes=['P%d' % half], last=False)
                                    c.mm(lambda e: e.matmul(PB(half), lhsT=ident_b[:], rhs=negm_b[:], start=False, stop=True),
                                         reads=['ident_b', 'negm_b'], writes=['P%d' % half], last=True)
                                for hh in range(8):
                                    h = g * 8 + hh
                                    c.op('act', lambda e: e.activation(out=d_[:, hh, :], in_=PB(hh // 4, (hh % 4) * 128, (hh % 4 + 1) * 128), func=AF.Exp,
                                                                       bias=nac[:, h:h + 1], scale=1.0), reads=['P%d' % (hh // 4), 'nac'], writes=[dn])
                                c.any2(lambda e: e.tensor_tensor(out=L_[:], in0=d_[:], in1=sc[:, g, :].unsqueeze(1).to_broadcast([128, 8, 128]), op=ALU.mult),
                                       reads=[dn, 'sc'], writes=[Ln_])

                            def stageB(g):
                                d_ = dec[g % 2]; L_ = LT[g % 2]
                                dn = 'dec%d' % (g % 2); Ln_ = 'LT%d' % (g % 2)
                                c.mm(lambda e: e.matmul(PB(6), lhsT=xbcT[:, 20 + g, cs], rhs=state_b[:, g, :], start=True, stop=True),
                                     reads=['xbcT%d' % (20 + g), 'stateb%d' % g], writes=['P6'], last=True)
                                c.op('dve', lambda e: e.tensor_tensor(out=yoff[:], in0=PB(6).rearrange("p (h d) -> p h d", h=8),
                                                                      in1=eac[:, g * 8:(g + 1) * 8].unsqueeze(2).to_broadcast([128, 8, 64]), op=ALU.mult),
                                     reads=['P6', 'eac'], writes=['yoff'])
                                yof = yoff[:].rearrange("p h d -> p (h d)")
                                for il in range(4):
                                    c.mm(lambda e: e.matmul(PB(7, il * 128, (il + 1) * 128), lhsT=yof[:, il * 128:(il + 1) * 128], rhs=ident_b[:],
                                                            start=True, stop=False), reads=['yoff', 'ident_b'], writes=['P7'], last=False)
                                    for hf in range(2):
                                        hh = il * 2 + hf
                                        h = g * 8 + hh
                                        c.mm(lambda e: e.matmul(PB(7, il * 128, (il + 1) * 128, hf * 64, hf * 64 + 64), lhsT=xdt[:, h, :], rhs=L_[:, hh, :],
                                                                start=False, stop=(hf == 1), tile_position=(0, hf * 64)),
                                             reads=['xdt', Ln_], writes=['P7'], last=(hh == 7))
                                tl = ['xbcT%d' % (g * 4 + i) for i in range(4)]
                                c.op('pool', lambda e: e.tensor_tensor(out=xd[:], in0=xbcT[:, g * 4:(g + 1) * 4, cs],
                                                                       in1=PR["p_dskip"][:, g * 4:(g + 1) * 4].unsqueeze(2).to_broadcast([128, 4, 128]), op=ALU.mult),
                                     reads=tl + ['p_dskip'], writes=['xd'])
                                c.op('dve', lambda e: e.tensor_tensor(out=tt[:, g * 4:(g + 1) * 4, :], in0=PB(7).rearrange("p (i n) -> p i n", i=4), in1=xd[:], op=ALU.add),
                                     reads=['P7', 'xd'], writes=['tt%d' % g])
                                c.op('pool', lambda e: e.tensor_tensor(out=tt[:, g * 4:(g + 1) * 4, :], in0=tt[:, g * 4:(g + 1) * 4, :], in1=szT[:, g * 4:(g + 1) * 4, cs], op=ALU.mult),
                                     reads=['tt%d' % g] + ['szT%d' % (g * 4 + i) for i in range(4)], writes=['tt%d' % g])
                                c.mm(lambda e: e.matmul(PB(6), lhsT=btok[:, g, :], rhs=xdtd[:, g * 8:(g + 1) * 8, :].rearrange("p h d -> p (h d)"), start=True, stop=True),
                                     reads=['btok', 'xdtd'], writes=['P6'], last=True)
                                c.op('pool', lambda e: e.tensor_tensor(out=sttmp[:].rearrange("p (h d) -> p h d", h=8), in0=state[:, g, :].rearrange("p (h d) -> p h d", h=8),
                                                                       in1=cdr[:, g * 8:(g + 1) * 8].unsqueeze(2).to_broadcast([128, 8, 64]), op=ALU.mult),
                                     reads=['state%d' % g, 'cdr'], writes=['sttmp'])
                                c.op('dve', lambda e: e.tensor_tensor(out=state[:, g, :], in0=PB(6), in1=sttmp[:], op=ALU.add), reads=['P6', 'sttmp'], writes=['state%d' % g])
                                c.op('act', lambda e: e.activation(out=state_b[:, g, :], in_=state[:, g, :], func=AF.Copy), reads=['state%d' % g], writes=['stateb%d' % g])
                            stageA(0)
                            for g in range(4):
                                if g < 3:
                                    stageA(g + 1)
                                stageB(g)
                            allt = ['tt%d' % g for g in range(4)]
                            c.op('act', lambda e: e.activation(out=sq[:], in_=tt[:], func=AF.Square), reads=allt, writes=['sq'])
                            for g in range(4):
                                for il in range(4):
                                    c.mm(lambda e: e.matmul(PB(3, g * 128, (g + 1) * 128), lhsT=ones_b[:], rhs=sq[:, g * 4 + il, :], start=(il == 0), stop=(il == 3)),
                                         reads=['sq', 'ones_b'], writes=['P3'], last=(g == 3 and il == 3))
                            rstd(PB(3).rearrange("p (g n) -> p g n", g=4), rrr[:], 512, ['P3'], ['rrr'])
                            c.op('pool', lambda e: e.tensor_tensor(out=tt[:], in0=tt[:], in1=PR["p_gssd"][:, :].unsqueeze(2).to_broadcast([128, 16, 128]), op=ALU.mult),
                                 reads=allt + ['p_gssd'], writes=allt)
                            for g in range(4):
                                c.op('dve', lambda e: e.tensor_tensor(out=ynT[:, g * 4:(g + 1) * 4, cs], in0=tt[:, g * 4:(g + 1) * 4, :],
                                                                      in1=rrr[:, g, :].unsqueeze(1).to_broadcast([128, 4, 128]), op=ALU.mult),
                                     reads=['tt%d' % g, 'rrr'], writes=['ynT'])
                    dump("ynT", ynT[:, 0, :], ['ynT'])
                    dump("state0", state[:, 0, :], ['state0'])
                    for mb in range(4):
                        wv, wr = wload("w_ssd_proj", 128, 16, [(mb * 256, 256)])
                        wg, wgr = wload("w_in", 128, 8, [(OFF_G + mb * 256, 256)])
                        for mm_ in range(2):
                            m = mb * 2 + mm_
                            for kt in range(16):
                                c.mm(lambda e: e.matmul(PB(0, 0, BLK), lhsT=wv[:, kt, mm_ * 128:(mm_ + 1) * 128], rhs=ynT[:, kt, :], start=(kt == 0), stop=(kt == 15)),
                                     reads=['ynT', wr], writes=['P0'], last=(kt == 15))
                            for kt in range(8):
                                c.mm(lambda e: e.matmul(PB(1, 0, BLK), lhsT=wg[:, kt, mm_ * 128:(mm_ + 1) * 128], rhs=hT[:, kt, :], start=(kt == 0), stop=(kt == 7)),
                                     reads=['hT', wgr], writes=['P1'], last=(kt == 7))
                            gsb = accs[m % 2]
                            c.op('act', lambda e: e.activation(out=gsb[:], in_=PB(1, 0, BLK), func=AF.Sigmoid, bias=PR["p_gateb"][:, m:m + 1], scale=1.0),
                                 reads=['P1', 'p_gateb'], writes=['acc%d' % (m % 2)])
                            c.op('dve', lambda e: e.tensor_tensor(out=mix[:, m, :], in0=PB(0, 0, BLK), in1=gsb[:], op=ALU.mult),
                                 reads=['P0', 'acc%d' % (m % 2)], writes=['mix%d' % m])
                c.barrier()
                dump("mix_ssd", mix[:, 0, :], ['mix0'])
                if stop_after == 'A':
                    continue
                SCALE = 96.0 ** -0.5
                with ExitStack() as esA:
                    def sbB(name, shape, dt=F32):
                        return esA.enter_context(nc.sbuf_tensor(un(name), list(shape), dt))
                    qkT = sbB("qkT", [128, 3, BLK], BF16)
                    qT = sbB("qT", [96, 16, BLK], BF16); kTc = sbB("kTc", [96, 16, BLK], BF16)
                    attnT = sbB("attnT", [64, 16, BLK], BF16)
                    qan = sbB("qan", [128, 384], BF16)
                    ssq = sbB("ssq", [128, 2 * TB]); rq = sbB("rq", [128, 2 * TB])
                    qsbs = [sbB("qsb%d" % i, [128, 16, 96]) for i in range(2)]; qtmps = [sbB("qtmp%d" % i, [128, 16, 96]) for i in range(2)]
                    sshs = [sbB("ssh%d" % i, [128, 16]) for i in range(2)]; rhs_ = [sbB("rh%d" % i, [128, 16]) for i in range(2)]
                    qrs = [sbB("qr%d" % i, [128, 16, 96], BF16) for i in range(2)]
                    r1s = [sbB("r1%d" % i, [128, 16, 16]) for i in range(2)]; r2s = [sbB("r2%d" % i, [128, 16, 16]) for i in range(2)]
                    pts = [sbB("pt%d" % i, [128, BLK], BF16) for i in range(3)]
                    kst2 = [sbB("kstp%d" % i, [96, 2, max(1, NB - 1) * BLK], BF16) for i in range(2)]
                    rD = sbB("rD", [64, BLK])
                    gsb2 = [sbB("gsb%d" % i, [128, BLK]) for i in range(2)]
                    tmpm = sbB("tmpm", [128, BLK])
                    wuq, wuqr = wload("w_uq", 128, 2, [(0, 1536)])
                    wukv, wukvr = wload("w_ukv", 128, 1, [(0, 2048)])
                    for t in range(TB):
                        tg = blk * TB + t
                        c.op('act', lambda e: e.activation(out=junk[:, 0:256], in_=small[:, t, 32:288], func=AF.Square, accum_out=ssq[:, 2 * t:2 * t + 1]),
                             reads=['small%d' % t], writes=['junk', 'ssq%d' % t])
                        c.op('act', lambda e: e.activation(out=junk[:, 256:384], in_=small[:, t, 288:416], func=AF.Square, accum_out=ssq[:, 2 * t + 1:2 * t + 2]),
                             reads=['small%d' % t], writes=['junk', 'ssq%d' % t])
                        rstd(ssq[:, 2 * t:2 * t + 1], rq[:, 2 * t:2 * t + 1], 256, ['ssq%d' % t], ['rq%da' % t])
                        rstd(ssq[:, 2 * t + 1:2 * t + 2], rq[:, 2 * t + 1:2 * t + 2], 128, ['ssq%d' % t], ['rq%db' % t])
                        c.op('dve', lambda e: e.scalar_tensor_tensor(out=qan[:, 0:256], in0=small[:, t, 32:288], scalar=rq[:, 2 * t:2 * t + 1], in1=PR["p_gqa"][:],
                                                                     op0=ALU.mult, op1=ALU.mult), reads=['small%d' % t, 'rq%da' % t, 'p_gqa'], writes=['qan'])
                        c.op('dve', lambda e: e.scalar_tensor_tensor(out=qan[:, 256:384], in0=small[:, t, 288:416], scalar=rq[:, 2 * t + 1:2 * t + 2], in1=PR["p_gkva"][:],
                                                                     op0=ALU.mult, op1=ALU.mult), reads=['small%d' % t, 'rq%db' % t, 'p_gkva'], writes=['qan'])
                        pt_ = PBh(0)
                        for i in range(3):
                            c.mm(lambda e: e.transpose(out=pt_[:, i * 128:(i + 1) * 128], in_=qan[:, i * 128:(i + 1) * 128], identity=ident_b[:]),
                                 reads=['qan', 'ident_b'], writes=['P0'], last=(i == 2))
                        c.op('act', lambda e: e.activation(out=qkT[:, :, t * 128:(t + 1) * 128], in_=pt_[:, 0:384].rearrange("p (k n) -> p k n", k=3), func=AF.Copy),
                             reads=['P0'], writes=['qkT'])
                        for nb in range(3):
                            for kt in range(2):
                                c.mm(lambda e: e.matmul(PB(1 + nb), lhsT=qkT[:, kt, t * 128:(t + 1) * 128], rhs=wuq[:, kt, nb * 512:(nb + 1) * 512], start=(kt == 0), stop=(kt == 1)),
                                     reads=['qkT', wuqr], writes=['P%d' % (1 + nb)], last=(kt == 1))
                        for nb in range(4):
                            c.mm(lambda e: e.matmul(PB(4 + nb), lhsT=qkT[:, 2, t * 128:(t + 1) * 128], rhs=wukv[:, 0, nb * 512:(nb + 1) * 512], start=True, stop=True),
                                 reads=['qkT', wukvr], writes=['P%d' % (4 + nb)], last=True)
                        kvp = pall[:, 4 * 512:8 * 512].rearrange("p (h d) -> p h d", h=16)
                        c.op('act', lambda e: e.activation(out=Vc[:, tg, :, :], in_=kvp[:, :, 64:128], func=AF.Copy), reads=['P4', 'P5', 'P6', 'P7'], writes=['Vc%d' % tg])
                        def qk_chain(which):
                            qsb = qsbs[which]; qtmp = qtmps[which]; ssh = sshs[which]; rh = rhs_[which]; qr = qrs[which]; r1 = r1s[which]; r2 = r2s[which]
                            gname = "p_gq" if which == 0 else "p_gk"
                            if which == 0:
                                c.op('act', lambda e: e.activation(out=qsb[:], in_=pall[:, 512:4 * 512].rearrange("p (h d) -> p h d", h=16), func=AF.Copy),
                                     reads=['P1', 'P2', 'P3'], writes=['qsb%d' % which])
                                yield
                            else:
                                c.op('act', lambda e: e.activation(out=qsb[:, :, 0:64], in_=kvp[:, :, 0:64], func=AF.Copy), reads=['P4', 'P5', 'P6', 'P7'], writes=['qsb%d' % which])
                                c.op('pool', lambda e: e.tensor_copy(out=qsb[:, :, 64:96], in_=small[:, t, 416:448].unsqueeze(1).to_broadcast([128, 16, 32])),
                                     reads=['small%d' % t, 'qsb%d' % which], writes=['qsb%d' % which])
                                yield
                            c.op('pool', lambda e: e.tensor_tensor(out=qtmp[:], in0=qsb[:], in1=qsb[:], op=ALU.mult), reads=['qsb%d' % which], writes=['qtmp%d' % which])
                            yield
                            c.op('dve', lambda e: e.tensor_reduce(out=ssh[:], in_=qtmp[:], axis=AX.X, op=ALU.add), reads=['qtmp%d' % which], writes=['ssh%d' % which])
                            yield
                            rstd(ssh[:], rh[:], 96, ['ssh%d' % which], ['rh%d' % which])
                            yield
                            c.op('dve', lambda e: e.tensor_tensor(out=qtmp[:], in0=qsb[:], in1=rh[:, :].unsqueeze(2).to_broadcast([128, 16, 96]), op=ALU.mult),
                                 reads=['qsb%d' % which, 'rh%d' % which], writes=['qtmp%d' % which])
                            yield
                            c.op('pool', lambda e: e.tensor_tensor(out=qtmp[:], in0=qtmp[:], in1=PR[gname][:, :].unsqueeze(1).to_broadcast([128, 16, 96]), op=ALU.mult),
                                 reads=['qtmp%d' % which, gname], writes=['qtmp%d' % which])
                            yield
                            x1 = qtmp[:, :, 64:80]; x2 = qtmp[:, :, 80:96]
                            cb = cos_t[:, tg, :].unsqueeze(1).to_broadcast([128, 16, 16]); sb_ = sin_t[:, tg, :].unsqueeze(1).to_broadcast([128, 16, 16])
                            c.op('dve', lambda e: e.tensor_tensor(out=r1[:], in0=x1, in1=cb, op=ALU.mult), reads=['qtmp%d' % which, 'cos'], writes=['r1%d' % which])
                            c.op('pool', lambda e: e.tensor_tensor(out=r2[:], in0=x2, in1=sb_, op=ALU.mult), reads=['qtmp%d' % which, 'sin'], writes=['r2%d' % which])
                            yield
                            c.op('dve', lambda e: e.tensor_tensor(out=qr[:, :, 64:80], in0=r1[:], in1=r2[:], op=ALU.subtract), reads=['r1%d' % which, 'r2%d' % which], writes=['qr%d' % which])
                            yield
                            c.op('dve', lambda e: e.tensor_tensor(out=r1[:], in0=x1, in1=sb_, op=ALU.mult), reads=['qtmp%d' % which, 'sin', 'qr%d' % which], writes=['r1%d' % which])
                            c.op('pool', lambda e: e.tensor_tensor(out=r2[:], in0=x2, in1=cb, op=ALU.mult), reads=['qtmp%d' % which, 'cos', 'qr%d' % which], writes=['r2%d' % which])
                            yield
                            c.op('dve', lambda e: e.tensor_tensor(out=qr[:, :, 80:96], in0=r1[:], in1=r2[:], op=ALU.add), reads=['r1%d' % which, 'r2%d' % which], writes=['qr%d' % which])
                            yield
                            c.op('act', lambda e: e.activation(out=qr[:, :, 0:64], in_=qtmp[:, :, 0:64], func=AF.Copy), reads=['qtmp%d' % which], writes=['qr%d' % which])
                            yield
                            tp = pall[:, 0:1024].bitcast(BF16) if which == 0 else pall[:, 1024:2048].bitcast(BF16)
                            tres = ['P0', 'P1'] if which == 0 else ['P2', 'P3']
                            for h in range(16):
                                c.mm(lambda e: e.transpose(out=tp[0:96, h * 128:(h + 1) * 128], in_=qr[:, h, :], identity=ident_b[:]),
                                     reads=['qr%d' % which, 'ident_b'], writes=tres, last=(h == 15))
                            yield
                            dst = qT if which == 0 else kTc
                            c.op('act' if which == 0 else 'dve', (lambda e: e.activation(out=dst[:, :, t * 128:(t + 1) * 128], in_=tp[0:96, :].rearrange("p (h n) -> p h n", h=16), func=AF.Copy))
                                 if which == 0 else (lambda e: e.tensor_copy(out=dst[:, :, t * 128:(t + 1) * 128], in_=tp[0:96, :].rearrange("p (h n) -> p h n", h=16))),
                                 reads=tres, writes=['qT' if which == 0 else 'kTc'])
                        gens = [qk_chain(0), qk_chain(1)]
                        while gens:
                            for g_ in list(gens):
                                try:
                                    next(g_)
                                except StopIteration:
                                    gens.remove(g_)
                    if blk < NB - 1:
                        c.dma('pool', kc_d[:, :, blk * BLK:(blk + 1) * BLK].rearrange("h d s -> d h s"), kTc[:], reads=['kTc'], writes=['kc%d' % blk], key='kcw')
                    def kissue(hp):
                        if blk == 0 or hp >= 8:
                            return
                        c.dma('sp', kst2[hp % 2][:, :, 0:blk * BLK], kc_d[2 * hp:2 * hp + 2, :, 0:blk * BLK].rearrange("h d s -> d h s"),
                              reads=['kc%d' % bp_ for bp_ in range(blk)], writes=['kstp%d' % (hp % 2)], key='kst%d' % (hp % 2))
                    kissue(0)
                    work = []
                    for h in range(16):
                        kts = []
                        for bp in range(blk):
                            for j in range(TB):
                                kts.append((kst2[(h // 2) % 2][:, h % 2, bp * BLK + j * 128: bp * BLK + (j + 1) * 128], 'kstp%d' % ((h // 2) % 2), bp * TB + j, 0, False))
                        for j in range(TB):
                            kts.append((kTc[:, h, j * 128:(j + 1) * 128], 'kTc', blk * TB + j, j * 128, True))
                        for idx, kt_ in enumerate(kts):
                            work.append((h, idx, len(kts)) + kt_)

                    def emit_S(w):
                        h, idx, nk, kap, kres, tgk, q0, diag = work[w]
                        if idx == 0 and h % 2 == 0:
                            kissue(h // 2 + 1)
                        sbk = w % 2
                        p_ = pts[w % 3]; pn = 'pt%d' % (w % 3)
                        c.mm(lambda e: e.matmul(PB(sbk, q0, BLK), lhsT=kap, rhs=qT[:, h, q0:BLK], start=True, stop=True), reads=[kres, 'qT'], writes=['P%d' % sbk], last=True)
                        c.op('act', lambda e: e.activation(out=p_[:, q0:BLK], in_=PB(sbk, q0, BLK), func=AF.Exp, scale=SCALE), reads=['P%d' % sbk], writes=[pn])
                        if diag:
                            c.op('pool', lambda e: e.tensor_tensor(out=p_[:, q0:q0 + 128], in0=p_[:, q0:q0 + 128], in1=cmask_b[:], op=ALU.mult), reads=[pn, 'cmask_b'], writes=[pn])

                    def emit_PV(w):
                        h, idx, nk, kap, kres, tgk, q0, diag = work[w]
                        ob = 4 + (h % 2) * 2
                        first = idx == 0; lastk = idx == nk - 1
                        p_ = pts[w % 3]; pn = 'pt%d' % (w % 3)
                        c.mm(lambda e: e.matmul(PB(ob, q0, BLK, 0, 64), lhsT=Vc[:, tgk, h, :], rhs=p_[:, q0:BLK], start=first, stop=lastk),
                             reads=['Vc%d' % tgk, pn], writes=['P%d' % ob], last=lastk)
                        c.mm(lambda e: e.matmul(PB(ob + 1, q0, BLK, 0, 64), lhsT=ones_b[:, 0:64], rhs=p_[:, q0:BLK], start=first, stop=lastk),
                             reads=['ones_b', pn], writes=['P%d' % (ob + 1)], last=lastk)
                        if lastk:
                            c.op('dve', lambda e: e.reciprocal(out=rD[:], in_=PB(ob + 1, 0, BLK, 0, 64)), reads=['P%d' % (ob + 1)], writes=['rD'])
                            c.op('dve', lambda e: e.tensor_tensor(out=attnT[:, h, :], in0=PB(ob, 0, BLK, 0, 64), in1=rD[:], op=ALU.mult), reads=['P%d' % ob, 'rD'], writes=['attnT'])
                    emit_S(0)
                    for w in range(len(work)):
                        if w + 1 < len(work):
                            emit_S(w + 1)
                        emit_PV(w)
                    dump("attnT", attnT[:, 0, :], ['attnT'])
                    dump("qT", qT[:, 0, :], ['qT'])
                    dump("kT", kTc[:, 0, :], ['kTc'])
                    mixb = sbB("mixb", [128, 8, BLK], BF16)
                    for mb in range(4):
                        wv, wr = wload("w_mla_proj", 64, 16, [(mb * 256, 256)])
                        wg, wgr = wload("w_in", 128, 8, [(OFF_G + 1024 + mb * 256, 256)])
                        for mm_ in range(2):
                            m = mb * 2 + mm_
                            for kt in range(16):
                                c.mm(lambda e: e.matmul(PB(0, 0, BLK), lhsT=wv[:, kt, mm_ * 128:(mm_ + 1) * 128], rhs=attnT[:, kt, :], start=(kt == 0), stop=(kt == 15)),
                                     reads=['attnT', wr], writes=['P0'], last=(kt == 15))
                            for kt in range(8):
                                c.mm(lambda e: e.matmul(PB(1, 0, BLK), lhsT=wg[:, kt, mm_ * 128:(mm_ + 1) * 128], rhs=hT[:, kt, :], start=(kt == 0), stop=(kt == 7)),
                                     reads=['hT', wgr], writes=['P1'], last=(kt == 7))
                            gs = gsb2[m % 2]
                            c.op('act', lambda e: e.activation(out=gs[:], in_=PB(1, 0, BLK), func=AF.Sigmoid, bias=PR["p_gateb"][:, 8 + m:9 + m], scale=1.0),
                                 reads=['P1', 'p_gateb'], writes=['gsb%d' % (m % 2)])
                            c.op('dve', lambda e: e.tensor_tensor(out=tmpm[:], in0=PB(0, 0, BLK), in1=gs[:], op=ALU.mult), reads=['P0', 'gsb%d' % (m % 2)], writes=['tmpm'])
                            c.op('pool', lambda e: e.tensor_tensor(out=mixb[:, m, :], in0=tmpm[:], in1=mix[:, m, :], op=ALU.add), reads=['tmpm', 'mix%d' % m], writes=['mixb'])
                    for nb in range(2):
                        wv, wr = wload("w_o", 128, 8, [(nb * 512, 512)])
                        for t in range(TB):
                            pb = 2 + t % 2
                            for kt in range(8):
                                c.mm(lambda e: e.matmul(PB(pb), lhsT=mixb[:, kt, t * 128:(t + 1) * 128], rhs=wv[:, kt, :], start=(kt == 0), stop=(kt == 7)),
                                     reads=['mixb', wr], writes=['P%d' % pb], last=(kt == 7))
                            c.op('dve', lambda e: e.tensor_tensor(out=xbuf[:, t, nb * 512:(nb + 1) * 512], in0=PB(pb), in1=xbuf[:, t, nb * 512:(nb + 1) * 512], op=ALU.add),
                                 reads=['P%d' % pb, 'xbuf%d' % t], writes=['xbuf%d' % t])
                dump("x1", xbuf[:, 0, :], ['xbuf0'])
                c.barrier()
                with ExitStack() as esA:
                    def sbC(name, shape, dt=F32):
                        return esA.enter_context(nc.sbuf_tensor(un(name), list(shape), dt))
                    xn_bufs = [sbC("xnC%d" % i, [128, 1024], BF16) for i in range(2)]
                    actT = sbC("actT", [128, 22, BLK], BF16)
                    stg = [sbC("stgC%d" % i, [128, BLK + 2]) for i in range(4)]
                    accs = [sbC("accC%d" % i, [128, BLK]) for i in range(4)]
                    sg = [sbC("sg%d" % i, [128, BLK]) for i in range(2)]
                    rmsnorm_to_hT("p_gffn", xn_bufs)
                    for ch in range(11):
                        wv, wr = wload("w_up", 128, 8, [(ch * 256, 256), (D_FF + ch * 256, 256)])
                        for jj in range(2):
                            j = ch * 2 + jj
                            items = []
                            for half in range(2):
                                pb = (j % 2) * 2 + half
                                jc = j + 22 * half
                                for kt in range(8):
                                    c.mm(lambda e: e.matmul(PB(pb, 0, BLK), lhsT=wv[:, kt, half * 256 + jj * 128: half * 256 + (jj + 1) * 128], rhs=hT[:, kt, :],
                                                            start=(kt == 0), stop=(kt == 7)), reads=['hT', wr], writes=['P%d' % pb], last=(kt == 7))
                                si = (j % 2) * 2 + half
                                items.append((PB(pb, 0, BLK), stg[si], halo_f[:, jc, :], PR["p_cw_ffn"][:, jc, :], PR["p_cb_ffn"][:, jc:jc + 1],
                                              accs[si], 'stgC%d' % si, 'accC%d' % si, 'halo_f%d' % jc, 'P%d' % pb))
                            conv_multi(items, 2)
                            sgi = sg[j % 2]
                            c.op('act', lambda e: e.activation(out=sgi[:], in_=accs[(j % 2) * 2][:], func=AF.Silu), reads=['accC%d' % ((j % 2) * 2)], writes=['sg%d' % (j % 2)])
                            c.op('pool', lambda e: e.tensor_tensor(out=actT[:, j, :], in0=sgi[:], in1=accs[(j % 2) * 2 + 1][:], op=ALU.mult),
                                 reads=['sg%d' % (j % 2), 'accC%d' % ((j % 2) * 2 + 1)], writes=['actT'])
                    dump("actT", actT[:, 0, :], ['actT'])
                    for m4 in range(4):
                        for kh in range(2):
                            wv, wr = wload("w_down", 128, 11, [(m4 * 256, 256)], r0=kh * 1408)
                            for t in range(TB):
                                for kt in range(11):
                                    c.mm(lambda e: e.matmul(PB(4 + t, 0, 256), lhsT=actT[:, kh * 11 + kt, t * 128:(t + 1) * 128], rhs=wv[:, kt, :],
                                                            start=(kh == 0 and kt == 0), stop=(kh == 1 and kt == 10)),
                                         reads=['actT', wr], writes=['P%d' % (4 + t)], last=(kt == 10))
                        for t in range(TB):
                            c.op('dve', lambda e: e.tensor_tensor(out=xbuf[:, t, m4 * 256:(m4 + 1) * 256], in0=PB(4 + t, 0, 256),
                                                                  in1=xbuf[:, t, m4 * 256:(m4 + 1) * 256], op=ALU.add),
                                 reads=['P%d' % (4 + t), 'xbuf%d' % t], writes=['xbuf%d' % t])
                    c.dma('pool', out_d[tok0:tok0 + BLK, :].rearrange("(t p) f -> p t f", p=128), xbuf[:], reads=['xbuf%d' % t for t in range(TB)], key='ost')
                c.barrier()
        c.barrier()
        print("ninst", c.ninst, "cnt", c.cnt)
    return nc


def kernel(**inputs):
    x = np.asarray(inputs["x"], dtype=np.float32)
    B, S, D = x.shape
    NSEQ = B // NCORES
    nc = build(NSEQ=NSEQ, S=S, TB=2)
    consts = host_consts(S)
    params = host_params(inputs)
    maps = []
    for i in range(NCORES):
        m = {"x": np.ascontiguousarray(x[i * NSEQ:(i + 1) * NSEQ].reshape(NSEQ * S, D))}
        for k in WSHAPES:
            m[k] = np.ascontiguousarray(np.asarray(inputs[k], dtype=np.float32)[0])
        m.update(params)
        m.update(consts)
        maps.append(m)
    res = run_bass_kernel_spmd(nc, maps, core_ids=list(range(NCORES)))
    out = np.concatenate([np.asarray(r["out"]).reshape(NSEQ, S, D) for r in res.results], axis=0)
    return out.astype(np.float32)
```

```python
import numpy as np
from contextlib import ExitStack
import concourse.bass as bass
import concourse.mybir as mybir
from concourse.bass_utils import run_bass_kernel_spmd

F32 = mybir.dt.float32
BF16 = mybir.dt.bfloat16
AF = mybir.ActivationFunctionType
ALU = mybir.AluOpType
AX = mybir.AxisListType

D_MODEL = 1024
D_INNER = 2048
IN_COLS = 7616
D_FF = 2816
EPS = 1e-6
OFF_XBC = 2048
OFF_DT = 5120
OFF_G = 5568
NCORES = 8
WSHAPES = {
    "w_in": (1024, IN_COLS), "w_ssd_proj": (2048, 1024), "w_uq": (256, 1536), "w_ukv": (128, 2048),
    "w_mla_proj": (1024, 1024), "w_o": (1024, 1024), "w_up": (1024, 2 * D_FF), "w_down": (D_FF, 1024),
}
SLOT_ELEMS = 4096


def weight_chunks():
    ch = [("w_in", 128, 8, 0, ((OFF_DT, 448),))]
    ch += [("w_in", 128, 8, 0, ((OFF_XBC + i * 512, 512),)) for i in range(6)]
    ch += [("w_in", 128, 8, 0, ((i * 512, 512),)) for i in range(4)]
    for mb in range(4):
        ch += [("w_ssd_proj", 128, 16, 0, ((mb * 256, 256),)), ("w_in", 128, 8, 0, ((OFF_G + mb * 256, 256),))]
    ch += [("w_uq", 128, 2, 0, ((0, 1536),)), ("w_ukv", 128, 1, 0, ((0, 2048),))]
    for mb in range(4):
        ch += [("w_mla_proj", 64, 16, 0, ((mb * 256, 256),)), ("w_in", 128, 8, 0, ((OFF_G + 1024 + mb * 256, 256),))]
    ch += [("w_o", 128, 8, 0, ((nb * 512, 512),)) for nb in range(2)]
    ch += [("w_up", 128, 8, 0, ((i * 256, 256), (D_FF + i * 256, 256))) for i in range(11)]
    for m4 in range(4):
        for kh in range(2):
            ch.append(("w_down", 128, 11, kh * 1408, ((m4 * 256, 256),)))
    return ch
_UID = [0]


def un(name):
    _UID[0] += 1
    return "%s_u%d" % (name, _UID[0])

NSLOT = 3


class Ctx:
    def __init__(self, nc, es):
        self.nc = nc
        self.es = es
        self.engs = {'pe': nc.tensor, 'act': nc.scalar, 'dve': nc.vector, 'pool': nc.gpsimd, 'sp': nc.sync}
        self.sem = {}
        for k in ('pe', 'act', 'dve', 'pool'):
            self.sem[k] = es.enter_context(nc.semaphore("s_" + k))
        self.cnt = {k: 0 for k in self.sem}
        self.waited = {k: {} for k in self.engs}
        self.lastw = {}
        self.readers = {}
        self.dsem = {}
        self.dcnt = {}
        self.ninst = {k: 0 for k in self.engs}
        self.rr = 0

    def sb(self, name, shape, dt=F32):
        return self.es.enter_context(self.nc.sbuf_tensor(name, list(shape), dt))

    def _deps(self, reads, writes):
        deps = {}

        def add(d):
            if d is None:
                return
            k, v = d
            if deps.get(k, 0) < v:
                deps[k] = v
        for r in reads:
            add(self.lastw.get(r))
        for w in writes:
            add(self.lastw.get(w))
            for d in self.readers.get(w, ()):
                add(d)
        return deps

    def _wait(self, eng, deps):
        h = self.engs[eng]
        wd = self.waited[eng]
        for k, v in deps.items():
            if wd.get(k, 0) >= v:
                continue
            if k == 'pe' and eng == 'pe':
                continue
            s = self.sem[k] if k in self.sem else self.dsem[k]
            h.wait_ge(s, v)
            wd[k] = v
            self.ninst[eng] += 1

    def _commit(self, tok, reads, writes):
        for r in reads:
            self.readers.setdefault(r, []).append(tok)
        for w in writes:
            self.lastw[w] = tok
            self.readers[w] = []

    def op(self, eng, fn, reads=(), writes=()):
        self._wait(eng, self._deps(reads, writes))
        ins = fn(self.engs[eng])
        self.cnt[eng] += 1
        self.ninst[eng] += 1
        ins.then_inc(self.sem[eng], 1)
        self._commit((eng, self.cnt[eng]), reads, writes)
        return ins

    def any2(self, fn, reads=(), writes=()):
        self.rr += 1
        return self.op('dve' if self.rr % 3 else 'pool', fn, reads, writes)

    def mm(self, fn, reads=(), writes=(), last=True):
        self._wait('pe', self._deps(reads, writes))
        ins = fn(self.engs['pe'])
        self.ninst['pe'] += 1
        if last:
            self.cnt['pe'] += 1
            ins.then_inc(self.sem['pe'], 1)
            tok = ('pe', self.cnt['pe'])
        else:
            tok = ('pe', self.cnt['pe'] + 1)
        self._commit(tok, reads, writes)
        return ins

    def dma(self, q, out, in_, reads=(), writes=(), key=None, **kw):
        if key not in self.dsem:
            self.dsem[key] = self.es.enter_context(self.nc.semaphore("d_" + key))
            self.dcnt[key] = 0
        self._wait(q, self._deps(reads, writes))
        ins = self.engs[q].dma_start(out=out, in_=in_, **kw)
        self.dcnt[key] += 16
        self.ninst[q] += 1
        ins.then_inc(self.dsem[key], 16)
        self._commit((key, self.dcnt[key]), reads, writes)
        return ins

    def barrier(self):
        deps = {k: self.cnt[k] for k in self.sem if self.cnt[k]}
        for k in self.dsem:
            deps[k] = self.dcnt[k]
        for e in self.engs:
            self._wait(e, dict(deps))


def host_consts(S):
    import ml_dtypes
    i = np.arange(128)
    c = {}
    c["c_ident"] = np.eye(128, dtype=np.float32)
    c["c_tri_incl"] = (i[:, None] <= i[None, :]).astype(np.float32)
    c["c_tri_strict"] = (i[:, None] > i[None, :]).astype(np.float32)
    neg = np.where(i[None, :] < i[:, None], -30000.0, 0.0).astype(np.float32)
    c["c_negmask"] = np.tile(neg, (1, 4))
    inv = (1.0 / (np.float32(10000.0) ** (np.arange(0, 32, 2, dtype=np.float32) / np.float32(32)))).astype(np.float32)
    ang = np.arange(S, dtype=np.float32)[:, None] * inv[None, :]
    cos = np.cos(ang).astype(np.float32).reshape(S // 128, 128, 16).transpose(1, 0, 2)
    sin = np.sin(ang).astype(np.float32).reshape(S // 128, 128, 16).transpose(1, 0, 2)
    c["c_cos"] = np.ascontiguousarray(cos)
    c["c_sin"] = np.ascontiguousarray(sin)
    return c


def host_params(inp):
    f = lambda a: np.ascontiguousarray(np.asarray(a, dtype=np.float32))
    rep = lambda v: f(np.broadcast_to(np.asarray(v, np.float32).reshape(1, -1), (128, v.size)))
    colT = lambda v: f(np.asarray(v, np.float32).reshape(-1, 128).T)
    p = {}
    p["p_gmix"] = colT(inp["norm_mix_g"][0])
    p["p_cw_ssd"] = f(np.asarray(inp["conv_ssd_w"][0]).reshape(4, 24, 128).transpose(2, 1, 0))
    p["p_cb_ssd"] = colT(inp["conv_ssd_b"][0])
    p["p_dtb"] = rep(inp["dt_bias"][0])
    p["p_alog"] = rep(inp["a_log"][0])
    p["p_dskip"] = colT(np.repeat(np.asarray(inp["d_skip"][0]), 64))
    p["p_gssd"] = colT(inp["ssd_norm_g"][0])
    p["p_gateb"] = colT(inp["gate_b"][0])
    p["p_gffn"] = colT(inp["norm_ffn_g"][0])
    p["p_cw_ffn"] = f(np.asarray(inp["conv_ffn_w"][0]).reshape(3, 44, 128).transpose(2, 1, 0))
    p["p_cb_ffn"] = colT(inp["conv_ffn_b"][0])
    p["p_gqa"] = rep(inp["q_a_norm_g"][0])
    p["p_gkva"] = rep(inp["kv_a_norm_g"][0])
    p["p_gq"] = rep(inp["q_norm_g"][0])
    p["p_gk"] = rep(inp["k_norm_g"][0])
    return p


PSHAPES = {
    "p_gmix": (128, 8), "p_cw_ssd": (128, 24, 4), "p_cb_ssd": (128, 24), "p_dtb": (128, 32), "p_alog": (128, 32),
    "p_dskip": (128, 16), "p_gssd": (128, 16), "p_gateb": (128, 16), "p_gffn": (128, 8),
    "p_cw_ffn": (128, 44, 3), "p_cb_ffn": (128, 44), "p_gqa": (128, 256), "p_gkva": (128, 128),
    "p_gq": (128, 96), "p_gk": (128, 96),
}


def CSHAPES(S):
    return {"c_ident": (128, 128), "c_tri_incl": (128, 128), "c_tri_strict": (128, 128), "c_negmask": (128, 512),
            "c_cos": (128, S // 128, 16), "c_sin": (128, S // 128, 16)}


def build(NSEQ=4, S=2048, TB=2, dbg=(), stop_after=None):
    BLK = TB * 128
    NT = S // 128
    NB = S // BLK
    nc = bass.Bass("TRN2", target_bir_lowering=False)

    def din(name, shape, dt=F32):
        return nc.dram_tensor(name, list(shape), dt, kind="ExternalInput").ap()

    x_d = din("x", [NSEQ * S, D_MODEL])
    out_d = nc.dram_tensor("out", [NSEQ * S, D_MODEL], F32, kind="ExternalOutput").ap()
    W = {k: din(k, v) for k, v in WSHAPES.items()}
    CHUNKS = weight_chunks()
    NCH = len(CHUNKS)
    NTOTCH = NCH * NSEQ * (S // (TB * 128))
    Wc = nc.dram_tensor("wchunks_bf", [NCH, 128, SLOT_ELEMS], BF16, kind="Internal").ap()
    Pd = {k: din(k, v) for k, v in PSHAPES.items()}
    Cd = {k: din(k, v) for k, v in CSHAPES(S).items()}
    kc_d = nc.dram_tensor("kcache", [16, 96, S], BF16, kind="Internal").ap()
    acs_d = [nc.dram_tensor("acs%d" % i, [32, 2, 128], BF16, kind="Internal").ap() for i in range(2)]
    dbg_d = {k: nc.dram_tensor("dbg_" + k, list(shp), dt_, kind="ExternalOutput").ap() for k, shp, dt_ in dbg}

    with ExitStack() as es:
        c = Ctx(nc, es)
        pall = es.enter_context(nc.psum_tensor("pall", [128, 4096], F32))

        def PB(b, n0=0, n1=512, p0=0, p1=128):
            return pall[p0:p1, b * 512 + n0: b * 512 + n1]

        def PBh(b):
            return pall[:, b * 512:(b + 1) * 512].bitcast(BF16)

        ident_f = c.sb("ident_f", [128, 128]); ident_b = c.sb("ident_b", [128, 128], BF16)
        triI = c.sb("triI", [128, 128]); triS = c.sb("triS", [128, 128])
        ones_f = c.sb("ones_f", [128, 128]); ones_b = c.sb("ones_b", [128, 128], BF16)
        negm_f = c.sb("negm_f", [128, 512]); negm_b = c.sb("negm_b", [128, 512], BF16)
        cmask_b = c.sb("cmask_b", [128, 128], BF16)
        cos_t = c.sb("cos_t", [128, NT, 16]); sin_t = c.sb("sin_t", [128, NT, 16])
        PR = {k: c.sb("sb_" + k, list(v)) for k, v in PSHAPES.items()}
        A_rep = c.sb("A_rep", [128, 32])
        xbuf = c.sb("xbuf", [128, TB, 1024])
        hT = c.sb("hT", [128, 8, BLK], BF16)
        Vc = c.sb("Vc", [128, NT, 16, 64], BF16)
        state = c.sb("state", [128, 4, 512]); state_b = c.sb("state_b", [128, 4, 512], BF16)
        halo_s = c.sb("halo_s", [128, 24, 3]); halo_f = c.sb("halo_f", [128, 44, 2])
        wslot = [c.sb("wslot%d" % i, [128, SLOT_ELEMS], BF16) for i in range(NSLOT)]
        mix = c.sb("mix", [128, 8, BLK])
        small = c.sb("small", [128, TB, 448])
        ss1 = c.sb("ss1", [128, 8]); rs1 = c.sb("rs1", [128, 8])
        junk = c.sb("junk", [128, 1024], BF16)
        rhs2 = c.sb("rhs2", [2, 4096], BF16)
        ones2 = c.sb("ones2", [2, 128], BF16)

        c.dma('sp', ident_f[:], Cd["c_ident"], writes=['ident_f'], key='ld')
        c.dma('sp', triI[:], Cd["c_tri_incl"], writes=['triI'], key='ld')
        c.dma('sp', triS[:], Cd["c_tri_strict"], writes=['triS'], key='ld')
        c.dma('sp', negm_f[:], Cd["c_negmask"], writes=['negm_f'], key='ld')
        c.dma('sp', cos_t[:], Cd["c_cos"], writes=['cos'], key='ld')
        c.dma('sp', sin_t[:], Cd["c_sin"], writes=['sin'], key='ld')
        for k in PSHAPES:
            c.dma('sp', PR[k][:], Pd[k], writes=[k], key='ld')
        c.barrier()
        c.op('dve', lambda e: e.tensor_copy(out=ident_b[:], in_=ident_f[:]), reads=['ident_f'], writes=['ident_b'])
        c.op('dve', lambda e: e.tensor_copy(out=cmask_b[:], in_=triI[:]), reads=['triI'], writes=['cmask_b'])
        c.op('dve', lambda e: e.tensor_copy(out=negm_b[:], in_=negm_f[:]), reads=['negm_f'], writes=['negm_b'])
        c.op('dve', lambda e: e.memset(ones_f[:], 1.0), writes=['ones_f'])
        c.op('dve', lambda e: e.memset(ones_b[:], 1.0), writes=['ones_b'])
        c.op('dve', lambda e: e.memset(ones2[:], 1.0), writes=['ones2'])
        c.op('act', lambda e: e.activation(out=A_rep[:], in_=PR["p_alog"][:], func=AF.Exp), reads=['p_alog'], writes=['A_rep'])
        c.op('dve', lambda e: e.tensor_scalar(out=A_rep[:], in0=A_rep[:], scalar1=-1.0, scalar2=None, op0=ALU.mult),
             reads=['A_rep'], writes=['A_rep'])

        with ExitStack() as es2:
            st32 = [es2.enter_context(nc.sbuf_tensor("st32_%d" % i, [128, SLOT_ELEMS], F32)) for i in range(2)]
            st16 = [es2.enter_context(nc.sbuf_tensor("st16_%d" % i, [128, SLOT_ELEMS], BF16)) for i in range(2)]
            for ci, (name, P_, KT, r0, segs) in enumerate(CHUNKS):
                i = ci % 2
                ntot = sum(n for _, n in segs)
                assert KT * ntot <= SLOT_ELEMS
                v32 = st32[i][0:P_, 0:KT * ntot].rearrange("p (k n) -> p k n", k=KT)
                o = 0
                for (c0, n) in segs:
                    c.dma('sp', v32[:, :, o:o + n], W[name][r0:r0 + KT * P_, c0:c0 + n].rearrange("(k p) n -> p k n", p=P_),
                          writes=['st32_%d' % i], key='pl%d' % i)
                    o += n
                eng = ('dve', 'pool', 'act')[ci % 3]
                if eng == 'act':
                    c.op('act', lambda e: e.activation(out=st16[i][0:P_, 0:KT * ntot], in_=st32[i][0:P_, 0:KT * ntot], func=AF.Copy),
                         reads=['st32_%d' % i], writes=['st16_%d' % i])
                else:
                    c.op(eng, lambda e: e.tensor_copy(out=st16[i][0:P_, 0:KT * ntot], in_=st32[i][0:P_, 0:KT * ntot]),
                         reads=['st32_%d' % i], writes=['st16_%d' % i])
                c.dma('sp', Wc[ci, 0:P_, 0:KT * ntot], st16[i][0:P_, 0:KT * ntot], reads=['st16_%d' % i], writes=['wc%d' % ci], key='ps%d' % i)
        c.barrier()

        wstate = {'next': 0, 'loaded': 0}

        def wissue(upto):
            while wstate['loaded'] < upto:
                gi = wstate['loaded']
                ci = gi % NCH
                name, P_, KT, r0, segs = CHUNKS[ci]
                ntot = sum(n for _, n in segs)
                i = gi % NSLOT
                c.dma('sp', wslot[i][0:P_, 0:KT * ntot], Wc[ci, 0:P_, 0:KT * ntot], reads=['wc%d' % ci], writes=['wslot%d' % i], key='w%d' % i)
                wstate['loaded'] += 1

        def wload(name, P_, KT, segs, r0=0):
            gi = wstate['next']
            wstate['next'] += 1
            spec = CHUNKS[gi % NCH]
            assert spec == (name, P_, KT, r0, tuple(segs)), (spec, name, segs)
            wissue(min(gi + 2, NTOTCH))
            ntot = sum(n for _, n in segs)
            i = gi % NSLOT
            return wslot[i][0:P_, 0:KT * ntot].rearrange("p (k n) -> p k n", k=KT), 'wslot%d' % i

        def rstd(ss_ap, out_ap, n, reads, writes):
            c.op('act', lambda e: e.activation(out=out_ap, in_=ss_ap, func=AF.Sqrt, bias=EPS, scale=1.0 / n), reads=reads, writes=writes)
            c.op('dve', lambda e: e.reciprocal(out=out_ap, in_=out_ap), reads=writes, writes=writes)

        def rmsnorm_to_hT(gname, xn_bufs):
            for t in range(TB):
                c.op('act', lambda e: e.activation(out=junk[:], in_=xbuf[:, t, :], func=AF.Square, accum_out=ss1[:, t:t + 1]),
                     reads=['xbuf%d' % t], writes=['junk', 'ss1_%d' % t])
                rstd(ss1[:, t:t + 1], rs1[:, t:t + 1], 1024, ['ss1_%d' % t], ['rs1_%d' % t])
                xn = xn_bufs[t % 2]
                c.op('act', lambda e: e.activation(out=xn[:], in_=xbuf[:, t, :], func=AF.Copy, scale=rs1[:, t:t + 1]),
                     reads=['xbuf%d' % t, 'rs1_%d' % t], writes=['xn%d' % (t % 2)])
                pt = PBh(t % 2)
                for kt in range(8):
                    c.mm(lambda e: e.transpose(out=pt[:, kt * 128:(kt + 1) * 128], in_=xn[:, kt * 128:(kt + 1) * 128], identity=ident_b[:]),
                         reads=['xn%d' % (t % 2), 'ident_b'], writes=['P%d' % (t % 2)], last=(kt == 7))
                c.op('dve', lambda e: e.tensor_tensor(out=hT[:, :, t * 128:(t + 1) * 128], in0=pt.rearrange("p (k n) -> p k n", k=8),
                                                      in1=PR[gname][:, :].unsqueeze(2).to_broadcast([128, 8, 128]), op=ALU.mult),
                     reads=['P%d' % (t % 2), gname], writes=['hT'])

        def conv_multi(items, nh):
            for (ps_ap, stage, halo_ap, wt, bt, acc, sname, aname, hname, pname) in items:
                c.op('act', lambda e: e.activation(out=stage[:, 0:nh], in_=halo_ap, func=AF.Copy), reads=[hname], writes=[sname + 'h'])
            for (ps_ap, stage, halo_ap, wt, bt, acc, sname, aname, hname, pname) in items:
                c.op('act', lambda e: e.activation(out=stage[:, nh:nh + BLK], in_=ps_ap, func=AF.Copy), reads=[pname], writes=[sname])
            for (ps_ap, stage, halo_ap, wt, bt, acc, sname, aname, hname, pname) in items:
                c.op('act', lambda e: e.activation(out=halo_ap, in_=stage[:, BLK:BLK + nh], func=AF.Copy), reads=[sname, sname + 'h'], writes=[hname])
            for (ps_ap, stage, halo_ap, wt, bt, acc, sname, aname, hname, pname) in items:
                c.op('pool', lambda e: e.tensor_scalar(out=acc[:], in0=stage[:, nh:nh + BLK], scalar1=wt[:, nh:nh + 1], scalar2=bt,
                                                       op0=ALU.mult, op1=ALU.add), reads=[sname], writes=[aname])
            for k in range(nh - 1, -1, -1):
                for (ps_ap, stage, halo_ap, wt, bt, acc, sname, aname, hname, pname) in items:
                    c.op('dve', lambda e: e.scalar_tensor_tensor(out=acc[:], in0=stage[:, k:k + BLK], scalar=wt[:, k:k + 1], in1=acc[:],
                                                                 op0=ALU.mult, op1=ALU.add), reads=[sname, sname + 'h', aname], writes=[aname])

        def dump(name, ap_sb, res):
            if name in dbg_d:
                c.dma('sp', dbg_d[name], ap_sb, reads=res, key='dbg')

        nchunk = 0
        for seq in range(NSEQ):
            for blk in range(NB):
                tok0 = seq * S + blk * BLK
                c.dma('sp', xbuf[:], x_d[tok0:tok0 + BLK, :].rearrange("(t p) f -> p t f", p=128),
                      writes=['xbuf%d' % t for t in range(TB)], key='xld')
                if blk == 0:
                    c.op('pool', lambda e: e.memset(state[:], 0.0), writes=['state%d' % g for g in range(4)])
                    c.op('pool', lambda e: e.memset(state_b[:], 0.0), writes=['stateb%d' % g for g in range(4)])
                    c.op('pool', lambda e: e.memset(halo_s[:], 0.0), writes=['halo_s%d' % j for j in range(24)])
                    c.op('pool', lambda e: e.memset(halo_f[:], 0.0), writes=['halo_f%d' % j for j in range(44)])
                with ExitStack() as esA:
                    def sbA(name, shape, dt=F32):
                        return esA.enter_context(nc.sbuf_tensor(un(name), list(shape), dt))
                    xn_bufs = [sbA("xnA%d" % i, [128, 1024], BF16) for i in range(2)]
                    xbcT = sbA("xbcT", [128, 24, BLK], BF16)
                    szT = sbA("szT", [128, 16, BLK], BF16)
                    ynT = sbA("ynT", [128, 16, BLK], BF16)
                    stg = [sbA("stg%d" % i, [128, BLK + 3]) for i in range(4)]
                    accs = [sbA("acc%d" % i, [128, BLK]) for i in range(4)]
                    dtt = sbA("dtt", [128, TB, 32]); a_tok = sbA("a_tok", [128, TB, 32])
                    rmsnorm_to_hT("p_gmix", xn_bufs)
                    dump("hT", hT[:, 0, :], ['hT'])
                    wv, wr = wload("w_in", 128, 8, [(OFF_DT, 448)])
                    for t in range(TB):
                        for kt in range(8):
                            c.mm(lambda e: e.matmul(PB(2 + t % 2, 0, 448), lhsT=hT[:, kt, t * 128:(t + 1) * 128], rhs=wv[:, kt, :],
                                                    start=(kt == 0), stop=(kt == 7)), reads=['hT', wr], writes=['P%d' % (2 + t % 2)], last=(kt == 7))
                        c.op('act', lambda e: e.activation(out=small[:, t, :], in_=PB(2 + t % 2, 0, 448), func=AF.Copy),
                             reads=['P%d' % (2 + t % 2)], writes=['small%d' % t])
                    allsmall = ['small%d' % t for t in range(TB)]
                    c.op('dve', lambda e: e.tensor_tensor(out=dtt[:], in0=small[:, :, 0:32], in1=PR["p_dtb"][:, :].unsqueeze(1).to_broadcast([128, TB, 32]),
                                                          op=ALU.add), reads=allsmall + ['p_dtb'], writes=['dtt'])
                    c.op('act', lambda e: e.activation(out=dtt[:], in_=dtt[:], func=AF.Exp), reads=['dtt'], writes=['dtt'])
                    c.op('act', lambda e: e.activation(out=dtt[:], in_=dtt[:], func=AF.Ln, bias=1.0, scale=1.0), reads=['dtt'], writes=['dtt'])
                    c.op('dve', lambda e: e.tensor_tensor(out=a_tok[:], in0=dtt[:], in1=A_rep[:, :].unsqueeze(1).to_broadcast([128, TB, 32]),
                                                          op=ALU.mult), reads=['dtt', 'A_rep'], writes=['a_tok'])
                    for ch in range(6):
                        wv, wr = wload("w_in", 128, 8, [(OFF_XBC + ch * 512, 512)])
                        for pr in range(2):
                            items = []
                            for jj in (2 * pr, 2 * pr + 1):
                                j = ch * 4 + jj
                                pb = j % 4
                                for kt in range(8):
                                    c.mm(lambda e: e.matmul(PB(pb, 0, BLK), lhsT=wv[:, kt, jj * 128:(jj + 1) * 128], rhs=hT[:, kt, :],
                                                            start=(kt == 0), stop=(kt == 7)), reads=['hT', wr], writes=['P%d' % pb], last=(kt == 7))
                                items.append((PB(pb, 0, BLK), stg[j % 4], halo_s[:, j, :], PR["p_cw_ssd"][:, j, :], PR["p_cb_ssd"][:, j:j + 1],
                                              accs[j % 4], 'stg%d' % (j % 4), 'acc%d' % (j % 4), 'halo_s%d' % j, 'P%d' % pb))
                            conv_multi(items, 3)
                            for jj in (2 * pr, 2 * pr + 1):
                                j = ch * 4 + jj
                                c.op('act', lambda e: e.activation(out=xbcT[:, j, :], in_=accs[j % 4][:], func=AF.Silu), reads=['acc%d' % (j % 4)], writes=['xbcT%d' % j])
                    dump("xbcT", xbcT[:, 0, :], ['xbcT0'])
                    for ch in range(4):
                        wv, wr = wload("w_in", 128, 8, [(ch * 512, 512)])
                        for jj in range(4):
                            j = ch * 4 + jj
                            pb = j % 2
                            for kt in range(8):
                                c.mm(lambda e: e.matmul(PB(pb, 0, BLK), lhsT=wv[:, kt, jj * 128:(jj + 1) * 128], rhs=hT[:, kt, :],
                                                        start=(kt == 0), stop=(kt == 7)), reads=['hT', wr], writes=['P%d' % pb], last=(kt == 7))
                            c.op('act', lambda e: e.activation(out=szT[:, j, :], in_=PB(pb, 0, BLK), func=AF.Silu), reads=['P%d' % pb], writes=['szT%d' % j])
                    with ExitStack() as esS:
                        def sbS(name, shape, dt=F32):
                            return esS.enter_context(nc.sbuf_tensor(un(name), list(shape), dt))
                        eac = sbS("eac", [128, 32]); dte = sbS("dte", [128, 32]); cdr = sbS("cdr", [128, 32]); nac = sbS("nac", [128, 32])
                        dtdte = sbS("dtdte", [128, 32])
                        hl = sbS("hl", [32, 2, 128], BF16); rres = sbS("rres", [32, 128])
                        sc = sbS("sc", [128, 4, 128])
                        xdt = sbS("xdt", [128, 32, 64], BF16); xdtd = sbS("xdtd", [128, 32, 64], BF16)
                        btok = sbS("btok", [128, 4, 128], BF16)
                        dec = [sbS("dec%d" % i, [128, 8, 128]) for i in range(2)]
                        LT = [sbS("LT%d" % i, [128, 8, 128], BF16) for i in range(2)]
                        yoff = sbS("yoff", [128, 8, 64], BF16)
                        tt = sbS("tt", [128, 16, 128]); sq = sbS("sq", [128, 16, 128], BF16)
                        xd = sbS("xd", [128, 4, 128]); sttmp = sbS("sttmp", [128, 512]); rrr = sbS("rrr", [128, 4, 128])
                        for t in range(TB):
                            cs = slice(t * 128, (t + 1) * 128)
                            par = nchunk % 2
                            nchunk += 1
                            a_c = a_tok[:, t, :]
                            c.mm(lambda e: e.matmul(PB(2, 0, 32), lhsT=triI[:], rhs=a_c, start=True, stop=True), reads=['a_tok', 'triI'], writes=['P2'], last=False)
                            c.mm(lambda e: e.matmul(PB(2, 32, 64), lhsT=triS[:], rhs=a_c, start=True, stop=True), reads=['a_tok', 'triS'], writes=['P2'], last=False)
                            c.mm(lambda e: e.matmul(PB(2, 64, 96), lhsT=ones_f[:], rhs=a_c, start=True, stop=True), reads=['a_tok', 'ones_f'], writes=['P2'], last=False)
                            c.mm(lambda e: e.matmul(PB(2, 128, 256, 0, 32), lhsT=a_c, rhs=triI[:], start=True, stop=True), reads=['a_tok', 'triI'], writes=['P2'], last=True)
                            c.op('act', lambda e: e.activation(out=eac[:], in_=PB(2, 0, 32), func=AF.Exp), reads=['P2'], writes=['eac'])
                            c.op('act', lambda e: e.activation(out=dte[:], in_=PB(2, 32, 64), func=AF.Exp), reads=['P2'], writes=['dte'])
                            c.op('act', lambda e: e.activation(out=cdr[:], in_=PB(2, 64, 96), func=AF.Exp), reads=['P2'], writes=['cdr'])
                            c.op('dve', lambda e: e.tensor_scalar(out=nac[:], in0=PB(2, 0, 32), scalar1=-1.0, scalar2=None, op0=ALU.mult), reads=['P2'], writes=['nac'])
                            c.op('dve', lambda e: e.tensor_tensor(out=dtdte[:], in0=dtt[:, t, :], in1=dte[:], op=ALU.mult), reads=['dtt', 'dte'], writes=['dtdte'])
                            c.op('dve', lambda e: e.tensor_copy(out=hl[:, 0, :], in_=PB(2, 128, 256, 0, 32)), reads=['P2'], writes=['hl0'])
                            c.op('dve', lambda e: e.tensor_tensor(out=rres[:], in0=PB(2, 128, 256, 0, 32), in1=hl[:, 0, :], op=ALU.subtract), reads=['P2', 'hl0'], writes=['rres'])
                            c.op('dve', lambda e: e.tensor_copy(out=hl[:, 1, :], in_=rres[:]), reads=['rres'], writes=['hl1'])
                            c.dma('pool', acs_d[par], hl[:], reads=['hl0', 'hl1'], writes=['acs%d' % par], key='acs_w')
                            c.dma('sp', rhs2[:, :].rearrange("j (h l) -> j h l", h=32), acs_d[par].rearrange("h j l -> j h l"), reads=['acs%d' % par], writes=['rhs2'], key='acs_r')
                            for g in range(4):
                                c.mm(lambda e: e.matmul(PB(3, g * 128, (g + 1) * 128), lhsT=xbcT[:, 16 + g, cs], rhs=xbcT[:, 20 + g, cs], start=True, stop=True),
                                     reads=['xbcT%d' % (16 + g), 'xbcT%d' % (20 + g)], writes=['P3'], last=(g == 3))
                            c.op('act', lambda e: e.activation(out=sc[:], in_=PB(3).rearrange("p (g n) -> p g n", g=4), func=AF.Copy), reads=['P3'], writes=['sc'])
                            xtp = pall[:, 4 * 512:6 * 512].bitcast(BF16)
                            for i in range(16):
                                c.mm(lambda e: e.transpose(out=xtp[:, i * 128:(i + 1) * 128], in_=xbcT[:, i, cs], identity=ident_b[:]),
                                     reads=['xbcT%d' % i, 'ident_b'], writes=['P4', 'P5'], last=(i == 15))
                            xtp3 = xtp.rearrange("p (h d) -> p h d", h=32)
                            c.op('dve', lambda e: e.tensor_tensor(out=xdt[:], in0=xtp3, in1=dtt[:, t, :].unsqueeze(2).to_broadcast([128, 32, 64]), op=ALU.mult),
                                 reads=['P4', 'P5', 'dtt'], writes=['xdt'])
                            c.op('dve', lambda e: e.tensor_tensor(out=xdtd[:], in0=xtp3, in1=dtdte[:, :].unsqueeze(2).to_broadcast([128, 32, 64]), op=ALU.mult),
                                 reads=['P4', 'P5', 'dtdte'], writes=['xdtd'])
                            btp = PBh(6)
                            for g in range(4):
                                c.mm(lambda e: e.transpose(out=btp[:, g * 128:(g + 1) * 128], in_=xbcT[:, 16 + g, cs], identity=ident_b[:]),
                                     reads=['xbcT%d' % (16 + g), 'ident_b'], writes=['P6'], last=(g == 3))
                            c.op('act', lambda e: e.activation(out=btok[:], in_=btp[:, 0:512].rearrange("p (g n) -> p g n", g=4), func=AF.Copy), reads=['P6'], writes=['btok'])
                            def stageA(g):
                                d_ = dec[g % 2]; L_ = LT[g % 2]
                                dn = 'dec%d' % (g % 2); Ln_ = 'LT%d' % (g % 2)
                                for half in range(2):
                                    h0 = g * 8 + half * 4
                                    c.mm(lambda e: e.matmul(PB(half), lhsT=ones2[:, :], rhs=rhs2[:, h0 * 128:(h0 + 4) * 128], start=True, stop=False),
                                         reads=['rhs2', 'ones2'], writes=['P%d' % half], last=False)
                                    c.mm(lambda e: e.matmul(PB(half), lhsT=ident_b[:], rhs=negm_b[:], start=False, stop=True),
                                         reads=['ident_b', 'negm_b'], writes=['P%d' % half], last=True)
                                for hh in range(8):
                                    h = g * 8 + hh
                                    c.op('act', lambda e: e.activation(out=d_[:, hh, :], in_=PB(hh // 4, (hh % 4) * 128, (hh % 4 + 1) * 128), func=AF.Exp,
                                                                       bias=nac[:, h:h + 1], scale=1.0), reads=['P%d' % (hh // 4), 'nac'], writes=[dn])
                                c.any2(lambda e: e.tensor_tensor(out=L_[:], in0=d_[:], in1=sc[:, g, :].unsqueeze(1).to_broadcast([128, 8, 128]), op=ALU.mult),
                                       reads=[dn, 'sc'], writes=[Ln_])

                            def stageB(g):
                                d_ = dec[g % 2]; L_ = LT[g % 2]
                                dn = 'dec%d' % (g % 2); Ln_ = 'LT%d' % (g % 2)
                                c.mm(lambda e: e.matmul(PB(6), lhsT=xbcT[:, 20 + g, cs], rhs=state_b[:, g, :], start=True, stop=True),
                                     reads=['xbcT%d' % (20 + g), 'stateb%d' % g], writes=['P6'], last=True)
                                c.op('dve', lambda e: e.tensor_tensor(out=yoff[:], in0=PB(6).rearrange("p (h d) -> p h d", h=8),
                                                                      in1=eac[:, g * 8:(g + 1) * 8].unsqueeze(2).to_broadcast([128, 8, 64]), op=ALU.mult),
                                     reads=['P6', 'eac'], writes=['yoff'])
                                yof = yoff[:].rearrange("p h d -> p (h d)")
                                for il in range(4):
                                    c.mm(lambda e: e.matmul(PB(7, il * 128, (il + 1) * 128), lhsT=yof[:, il * 128:(il + 1) * 128], rhs=ident_b[:],
                                                            start=True, stop=False), reads=['yoff', 'ident_b'], writes=['P7'], last=False)
                                    for hf in range(2):
                                        hh = il * 2 + hf
                                        h = g * 8 + hh
                                        c.mm(lambda e: e.matmul(PB(7, il * 128, (il + 1) * 128, hf * 64, hf * 64 + 64), lhsT=xdt[:, h, :], rhs=L_[:, hh, :],
                                                                start=False, stop=(hf == 1), tile_position=(0, hf * 64)),
                                             reads=['xdt', Ln_], writes=['P7'], last=(hh == 7))
                                tl = ['xbcT%d' % (g * 4 + i) for i in range(4)]
                                c.op('pool', lambda e: e.tensor_tensor(out=xd[:], in0=xbcT[:, g * 4:(g + 1) * 4, cs],
                                                                       in1=PR["p_dskip"][:, g * 4:(g + 1) * 4].unsqueeze(2).to_broadcast([128, 4, 128]), op=ALU.mult),
                                     reads=tl + ['p_dskip'], writes=['xd'])
                                c.op('dve', lambda e: e.tensor_tensor(out=tt[:, g * 4:(g + 1) * 4, :], in0=PB(7).rearrange("p (i n) -> p i n", i=4), in1=xd[:], op=ALU.add),
                                     reads=['P7', 'xd'], writes=['tt%d' % g])
                                c.op('pool', lambda e: e.tensor_tensor(out=tt[:, g * 4:(g + 1) * 4, :], in0=tt[:, g * 4:(g + 1) * 4, :], in1=szT[:, g * 4:(g + 1) * 4, cs], op=ALU.mult),
                                     reads=['tt%d' % g] + ['szT%d' % (g * 4 + i) for i in range(4)], writes=['tt%d' % g])
                                c.mm(lambda e: e.matmul(PB(6), lhsT=btok[:, g, :], rhs=xdtd[:, g * 8:(g + 1) * 8, :].rearrange("p h d -> p (h d)"), start=True, stop=True),
                                     reads=['btok', 'xdtd'], writes=['P6'], last=True)
                                c.op('pool', lambda e: e.tensor_tensor(out=sttmp[:].rearrange("p (h d) -> p h d", h=8), in0=state[:, g, :].rearrange("p (h d) -> p h d", h=8),
                                                                       in1=cdr[:, g * 8:(g + 1) * 8].unsqueeze(2).to_broadcast([128, 8, 64]), op=ALU.mult),
                                     reads=['state%d' % g, 'cdr'], writes=['sttmp'])
                                c.op('dve', lambda e: e.tensor_tensor(out=state[:, g, :], in0=PB(6), in1=sttmp[:], op=ALU.add), reads=['P6', 'sttmp'], writes=['state%d' % g])
                                c.op('act', lambda e: e.activation(out=state_b[:, g, :], in_=state[:, g, :], func=AF.Copy), reads=['state%d' % g], writes=['stateb%d' % g])
                            stageA(0)
                            for g in range(4):
                                if g < 3:
                                    stageA(g + 1)
                                stageB(g)
                            allt = ['tt%d' % g for g in range(4)]
                            c.op('act', lambda e: e.activation(out=sq[:], in_=tt[:], func=AF.Square), reads=allt, writes=['sq'])
                            for g in range(4):
                                for il in range(4):
                                    c.mm(lambda e: e.matmul(PB(3, g * 128, (g + 1) * 128), lhsT=ones_b[:], rhs=sq[:, g * 4 + il, :], start=(il == 0), stop=(il == 3)),
                                         reads=['sq', 'ones_b'], writes=['P3'], last=(g == 3 and il == 3))
                            rstd(PB(3).rearrange("p (g n) -> p g n", g=4), rrr[:], 512, ['P3'], ['rrr'])
                            c.op('pool', lambda e: e.tensor_tensor(out=tt[:], in0=tt[:], in1=PR["p_gssd"][:, :].unsqueeze(2).to_broadcast([128, 16, 128]), op=ALU.mult),
                                 reads=allt + ['p_gssd'], writes=allt)
                            for g in range(4):
                                c.op('dve', lambda e: e.tensor_tensor(out=ynT[:, g * 4:(g + 1) * 4, cs], in0=tt[:, g * 4:(g + 1) * 4, :],
                                                                      in1=rrr[:, g, :].unsqueeze(1).to_broadcast([128, 4, 128]), op=ALU.mult),
                                     reads=['tt%d' % g, 'rrr'], writes=['ynT'])
                    dump("ynT", ynT[:, 0, :], ['ynT'])
                    dump("state0", state[:, 0, :], ['state0'])
                    for mb in range(4):
                        wv, wr = wload("w_ssd_proj", 128, 16, [(mb * 256, 256)])
                        wg, wgr = wload("w_in", 128, 8, [(OFF_G + mb * 256, 256)])
                        for mm_ in range(2):
                            m = mb * 2 + mm_
                            for kt in range(16):
                                c.mm(lambda e: e.matmul(PB(0, 0, BLK), lhsT=wv[:, kt, mm_ * 128:(mm_ + 1) * 128], rhs=ynT[:, kt, :], start=(kt == 0), stop=(kt == 15)),
                                     reads=['ynT', wr], writes=['P0'], last=(kt == 15))
                            for kt in range(8):
                                c.mm(lambda e: e.matmul(PB(1, 0, BLK), lhsT=wg[:, kt, mm_ * 128:(mm_ + 1) * 128], rhs=hT[:, kt, :], start=(kt == 0), stop=(kt == 7)),
                                     reads=['hT', wgr], writes=['P1'], last=(kt == 7))
                            gsb = accs[m % 2]
                            c.op('act', lambda e: e.activation(out=gsb[:], in_=PB(1, 0, BLK), func=AF.Sigmoid, bias=PR["p_gateb"][:, m:m + 1], scale=1.0),
                                 reads=['P1', 'p_gateb'], writes=['acc%d' % (m % 2)])
                            c.op('dve', lambda e: e.tensor_tensor(out=mix[:, m, :], in0=PB(0, 0, BLK), in1=gsb[:], op=ALU.mult),
                                 reads=['P0', 'acc%d' % (m % 2)], writes=['mix%d' % m])
                c.barrier()
                dump("mix_ssd", mix[:, 0, :], ['mix0'])
                if stop_after == 'A':
                    continue
                SCALE = 96.0 ** -0.5
                with ExitStack() as esA:
                    def sbB(name, shape, dt=F32):
                        return esA.enter_context(nc.sbuf_tensor(un(name), list(shape), dt))
                    qkT = sbB("qkT", [128, 3, BLK], BF16)
                    qT = sbB("qT", [96, 16, BLK], BF16); kTc = sbB("kTc", [96, 16, BLK], BF16)
                    attnT = sbB("attnT", [64, 16, BLK], BF16)
                    qan = sbB("qan", [128, 384], BF16)
                    ssq = sbB("ssq", [128, 2 * TB]); rq = sbB("rq", [128, 2 * TB])
                    qsbs = [sbB("qsb%d" % i, [128, 16, 96]) for i in range(2)]; qtmps = [sbB("qtmp%d" % i, [128, 16, 96]) for i in range(2)]
                    sshs = [sbB("ssh%d" % i, [128, 16]) for i in range(2)]; rhs_ = [sbB("rh%d" % i, [128, 16]) for i in range(2)]
                    qrs = [sbB("qr%d" % i, [128, 16, 96], BF16) for i in range(2)]
                    r1s = [sbB("r1%d" % i, [128, 16, 16]) for i in range(2)]; r2s = [sbB("r2%d" % i, [128, 16, 16]) for i in range(2)]
                    pts = [sbB("pt%d" % i, [128, BLK], BF16) for i in range(3)]
                    kst2 = [sbB("kstp%d" % i, [96, 2, max(1, NB - 1) * BLK], BF16) for i in range(2)]
                    rD = sbB("rD", [128, BLK])
                    vaug = [sbB("vaug%d" % i, [128, NT, 128], BF16) for i in range(2)]
                    for i in range(2):
                        c.op('pool', lambda e: e.memset(vaug[i][:, :, 64:128], 1.0), writes=['vaug%d' % i])
                    gsb2 = [sbB("gsb%d" % i, [128, BLK]) for i in range(2)]
                    tmpm = sbB("tmpm", [128, BLK])
                    wuq, wuqr = wload("w_uq", 128, 2, [(0, 1536)])
                    wukv, wukvr = wload("w_ukv", 128, 1, [(0, 2048)])
                    for t in range(TB):
                        tg = blk * TB + t
                        c.op('act', lambda e: e.activation(out=junk[:, 0:256], in_=small[:, t, 32:288], func=AF.Square, accum_out=ssq[:, 2 * t:2 * t + 1]),
                             reads=['small%d' % t], writes=['junk', 'ssq%d' % t])
                        c.op('act', lambda e: e.activation(out=junk[:, 256:384], in_=small[:, t, 288:416], func=AF.Square, accum_out=ssq[:, 2 * t + 1:2 * t + 2]),
                             reads=['small%d' % t], writes=['junk', 'ssq%d' % t])
                        rstd(ssq[:, 2 * t:2 * t + 1], rq[:, 2 * t:2 * t + 1], 256, ['ssq%d' % t], ['rq%da' % t])
                        rstd(ssq[:, 2 * t + 1:2 * t + 2], rq[:, 2 * t + 1:2 * t + 2], 128, ['ssq%d' % t], ['rq%db' % t])
                        c.op('dve', lambda e: e.scalar_tensor_tensor(out=qan[:, 0:256], in0=small[:, t, 32:288], scalar=rq[:, 2 * t:2 * t + 1], in1=PR["p_gqa"][:],
                                                                     op0=ALU.mult, op1=ALU.mult), reads=['small%d' % t, 'rq%da' % t, 'p_gqa'], writes=['qan'])
                        c.op('dve', lambda e: e.scalar_tensor_tensor(out=qan[:, 256:384], in0=small[:, t, 288:416], scalar=rq[:, 2 * t + 1:2 * t + 2], in1=PR["p_gkva"][:],
                                                                     op0=ALU.mult, op1=ALU.mult), reads=['small%d' % t, 'rq%db' % t, 'p_gkva'], writes=['qan'])
                        pt_ = PBh(0)
                        for i in range(3):
                            c.mm(lambda e: e.transpose(out=pt_[:, i * 128:(i + 1) * 128], in_=qan[:, i * 128:(i + 1) * 128], identity=ident_b[:]),
                                 reads=['qan', 'ident_b'], writes=['P0'], last=(i == 2))
                        c.op('act', lambda e: e.activation(out=qkT[:, :, t * 128:(t + 1) * 128], in_=pt_[:, 0:384].rearrange("p (k n) -> p k n", k=3), func=AF.Copy),
                             reads=['P0'], writes=['qkT'])
                        for nb in range(3):
                            for kt in range(2):
                                c.mm(lambda e: e.matmul(PB(1 + nb), lhsT=qkT[:, kt, t * 128:(t + 1) * 128], rhs=wuq[:, kt, nb * 512:(nb + 1) * 512], start=(kt == 0), stop=(kt == 1)),
                                     reads=['qkT', wuqr], writes=['P%d' % (1 + nb)], last=(kt == 1))
                        for nb in range(4):
                            c.mm(lambda e: e.matmul(PB(4 + nb), lhsT=qkT[:, 2, t * 128:(t + 1) * 128], rhs=wukv[:, 0, nb * 512:(nb + 1) * 512], start=True, stop=True),
                                 reads=['qkT', wukvr], writes=['P%d' % (4 + nb)], last=True)
                        kvp = pall[:, 4 * 512:8 * 512].rearrange("p (h d) -> p h d", h=16)
                        c.op('act', lambda e: e.activation(out=Vc[:, tg, :, :], in_=kvp[:, :, 64:128], func=AF.Copy), reads=['P4', 'P5', 'P6', 'P7'], writes=['Vc%d' % tg])
                        def qk_chain(which):
                            qsb = qsbs[which]; qtmp = qtmps[which]; ssh = sshs[which]; rh = rhs_[which]; qr = qrs[which]; r1 = r1s[which]; r2 = r2s[which]
                            gname = "p_gq" if which == 0 else "p_gk"
                            if which == 0:
                                c.op('act', lambda e: e.activation(out=qsb[:], in_=pall[:, 512:4 * 512].rearrange("p (h d) -> p h d", h=16), func=AF.Copy),
                                     reads=['P1', 'P2', 'P3'], writes=['qsb%d' % which])
                                yield
                            else:
                                c.op('act', lambda e: e.activation(out=qsb[:, :, 0:64], in_=kvp[:, :, 0:64], func=AF.Copy), reads=['P4', 'P5', 'P6', 'P7'], writes=['qsb%d' % which])
                                c.op('pool', lambda e: e.tensor_copy(out=qsb[:, :, 64:96], in_=small[:, t, 416:448].unsqueeze(1).to_broadcast([128, 16, 32])),
                                     reads=['small%d' % t, 'qsb%d' % which], writes=['qsb%d' % which])
                                yield
                            c.op('pool', lambda e: e.tensor_tensor(out=qtmp[:], in0=qsb[:], in1=qsb[:], op=ALU.mult), reads=['qsb%d' % which], writes=['qtmp%d' % which])
                            yield
                            c.op('dve', lambda e: e.tensor_reduce(out=ssh[:], in_=qtmp[:], axis=AX.X, op=ALU.add), reads=['qtmp%d' % which], writes=['ssh%d' % which])
                            yield
                            rstd(ssh[:], rh[:], 96, ['ssh%d' % which], ['rh%d' % which])
                            yield
                            c.op('dve', lambda e: e.tensor_tensor(out=qtmp[:], in0=qsb[:], in1=rh[:, :].unsqueeze(2).to_broadcast([128, 16, 96]), op=ALU.mult),
                                 reads=['qsb%d' % which, 'rh%d' % which], writes=['qtmp%d' % which])
                            yield
                            c.op('pool', lambda e: e.tensor_tensor(out=qtmp[:], in0=qtmp[:], in1=PR[gname][:, :].unsqueeze(1).to_broadcast([128, 16, 96]), op=ALU.mult),
                                 reads=['qtmp%d' % which, gname], writes=['qtmp%d' % which])
                            yield
                            x1 = qtmp[:, :, 64:80]; x2 = qtmp[:, :, 80:96]
                            cb = cos_t[:, tg, :].unsqueeze(1).to_broadcast([128, 16, 16]); sb_ = sin_t[:, tg, :].unsqueeze(1).to_broadcast([128, 16, 16])
                            c.op('dve', lambda e: e.tensor_tensor(out=r1[:], in0=x1, in1=cb, op=ALU.mult), reads=['qtmp%d' % which, 'cos'], writes=['r1%d' % which])
                            c.op('pool', lambda e: e.tensor_tensor(out=r2[:], in0=x2, in1=sb_, op=ALU.mult), reads=['qtmp%d' % which, 'sin'], writes=['r2%d' % which])
                            yield
                            c.op('dve', lambda e: e.tensor_tensor(out=qr[:, :, 64:80], in0=r1[:], in1=r2[:], op=ALU.subtract), reads=['r1%d' % which, 'r2%d' % which], writes=['qr%d' % which])
                            yield
                            c.op('dve', lambda e: e.tensor_tensor(out=r1[:], in0=x1, in1=sb_, op=ALU.mult), reads=['qtmp%d' % which, 'sin', 'qr%d' % which], writes=['r1%d' % which])
                            c.op('pool', lambda e: e.tensor_tensor(out=r2[:], in0=x2, in1=cb, op=ALU.mult), reads=['qtmp%d' % which, 'cos', 'qr%d' % which], writes=['r2%d' % which])
                            yield
                            c.op('dve', lambda e: e.tensor_tensor(out=qr[:, :, 80:96], in0=r1[:], in1=r2[:], op=ALU.add), reads=['r1%d' % which, 'r2%d' % which], writes=['qr%d' % which])
                            yield
                            c.op('act', lambda e: e.activation(out=qr[:, :, 0:64], in_=qtmp[:, :, 0:64], func=AF.Copy), reads=['qtmp%d' % which], writes=['qr%d' % which])
                            yield
                            tp = pall[:, 0:1024].bitcast(BF16) if which == 0 else pall[:, 1024:2048].bitcast(BF16)
                            tres = ['P0', 'P1'] if which == 0 else ['P2', 'P3']
                            for h in range(16):
                                c.mm(lambda e: e.transpose(out=tp[0:96, h * 128:(h + 1) * 128], in_=qr[:, h, :], identity=ident_b[:]),
                                     reads=['qr%d' % which, 'ident_b'], writes=tres, last=(h == 15))
                            yield
                            dst = qT if which == 0 else kTc
                            c.op('act' if which == 0 else 'dve', (lambda e: e.activation(out=dst[:, :, t * 128:(t + 1) * 128], in_=tp[0:96, :].rearrange("p (h n) -> p h n", h=16), func=AF.Copy))
                                 if which == 0 else (lambda e: e.tensor_copy(out=dst[:, :, t * 128:(t + 1) * 128], in_=tp[0:96, :].rearrange("p (h n) -> p h n", h=16))),
                                 reads=tres, writes=['qT' if which == 0 else 'kTc'])
                        gens = [qk_chain(0), qk_chain(1)]
                        while gens:
                            for g_ in list(gens):
                                try:
                                    next(g_)
                                except StopIteration:
                                    gens.remove(g_)
                    if blk < NB - 1:
                        c.dma('pool', kc_d[:, :, blk * BLK:(blk + 1) * BLK].rearrange("h d s -> d h s"), kTc[:], reads=['kTc'], writes=['kc%d' % blk], key='kcw')
                    def kissue(hp):
                        if blk == 0 or hp >= 8:
                            return
                        c.dma('sp', kst2[hp % 2][:, :, 0:blk * BLK], kc_d[2 * hp:2 * hp + 2, :, 0:blk * BLK].rearrange("h d s -> d h s"),
                              reads=['kc%d' % bp_ for bp_ in range(blk)], writes=['kstp%d' % (hp % 2)], key='kst%d' % (hp % 2))
                    kissue(0)
                    work = []
                    for h in range(16):
                        kts = []
                        for bp in range(blk):
                            for j in range(TB):
                                kts.append((kst2[(h // 2) % 2][:, h % 2, bp * BLK + j * 128: bp * BLK + (j + 1) * 128], 'kstp%d' % ((h // 2) % 2), bp * TB + j, 0, False))
                        for j in range(TB):
                            kts.append((kTc[:, h, j * 128:(j + 1) * 128], 'kTc', blk * TB + j, j * 128, True))
                        for idx, kt_ in enumerate(kts):
                            work.append((h, idx, len(kts)) + kt_)

                    def emit_S(w):
                        h, idx, nk, kap, kres, tgk, q0, diag = work[w]
                        if idx == 0 and h % 2 == 0:
                            kissue(h // 2 + 1)
                        if idx == 0:
                            ntile = (blk + 1) * TB
                            c.op('pool', lambda e: e.tensor_copy(out=vaug[h % 2][:, 0:ntile, 0:64], in_=Vc[:, 0:ntile, h, :]),
                                 reads=['Vc%d' % tg_ for tg_ in range(ntile)] + ['vaug%d' % (h % 2)], writes=['vaug%d' % (h % 2)])
                        sbk = w % 2
                        p_ = pts[w % 3]; pn = 'pt%d' % (w % 3)
                        c.mm(lambda e: e.matmul(PB(sbk, q0, BLK), lhsT=kap, rhs=qT[:, h, q0:BLK], start=True, stop=True), reads=[kres, 'qT'], writes=['P%d' % sbk], last=True)
                        c.op('act', lambda e: e.activation(out=p_[:, q0:BLK], in_=PB(sbk, q0, BLK), func=AF.Exp, scale=SCALE), reads=['P%d' % sbk], writes=[pn])
                        if diag:
                            c.op('pool', lambda e: e.tensor_tensor(out=p_[:, q0:q0 + 128], in0=p_[:, q0:q0 + 128], in1=cmask_b[:], op=ALU.mult), reads=[pn, 'cmask_b'], writes=[pn])

                    def emit_PV(w):
                        h, idx, nk, kap, kres, tgk, q0, diag = work[w]
                        ob = 4 + (h % 2)
                        first = idx == 0; lastk = idx == nk - 1
                        p_ = pts[w % 3]; pn = 'pt%d' % (w % 3)
                        c.mm(lambda e: e.matmul(PB(ob, q0, BLK), lhsT=vaug[h % 2][:, tgk, :], rhs=p_[:, q0:BLK], start=first, stop=lastk),
                             reads=['vaug%d' % (h % 2), pn], writes=['P%d' % ob], last=lastk)
                        if lastk:
                            c.op('dve', lambda e: e.reciprocal(out=rD[64:128, :], in_=PB(ob, 0, BLK, 64, 128)), reads=['P%d' % ob], writes=['rD'])
                            c.op('dve', lambda e: e.tensor_tensor(out=attnT[:, h, :], in0=PB(ob, 0, BLK, 0, 64), in1=rD[64:128, :], op=ALU.mult), reads=['P%d' % ob, 'rD'], writes=['attnT'])
                    emit_S(0)
                    for w in range(len(work)):
                        if w + 1 < len(work):
                            emit_S(w + 1)
                        emit_PV(w)
                    dump("attnT", attnT[:, 0, :], ['attnT'])
                    dump("qT", qT[:, 0, :], ['qT'])
                    dump("kT", kTc[:, 0, :], ['kTc'])
                    mixb = sbB("mixb", [128, 8, BLK], BF16)
                    for mb in range(4):
                        wv, wr = wload("w_mla_proj", 64, 16, [(mb * 256, 256)])
                        wg, wgr = wload("w_in", 128, 8, [(OFF_G + 1024 + mb * 256, 256)])
                        for mm_ in range(2):
                            m = mb * 2 + mm_
                            for kt in range(16):
                                c.mm(lambda e: e.matmul(PB(0, 0, BLK), lhsT=wv[:, kt, mm_ * 128:(mm_ + 1) * 128], rhs=attnT[:, kt, :], start=(kt == 0), stop=(kt == 15)),
                                     reads=['attnT', wr], writes=['P0'], last=(kt == 15))
                            for kt in range(8):
                                c.mm(lambda e: e.matmul(PB(1, 0, BLK), lhsT=wg[:, kt, mm_ * 128:(mm_ + 1) * 128], rhs=hT[:, kt, :], start=(kt == 0), stop=(kt == 7)),
                                     reads=['hT', wgr], writes=['P1'], last=(kt == 7))
                            gs = gsb2[m % 2]
                            c.op('act', lambda e: e.activation(out=gs[:], in_=PB(1, 0, BLK), func=AF.Sigmoid, bias=PR["p_gateb"][:, 8 + m:9 + m], scale=1.0),
                                 reads=['P1', 'p_gateb'], writes=['gsb%d' % (m % 2)])
                            c.op('dve', lambda e: e.tensor_tensor(out=tmpm[:], in0=PB(0, 0, BLK), in1=gs[:], op=ALU.mult), reads=['P0', 'gsb%d' % (m % 2)], writes=['tmpm'])
                            c.op('pool', lambda e: e.tensor_tensor(out=mixb[:, m, :], in0=tmpm[:], in1=mix[:, m, :], op=ALU.add), reads=['tmpm', 'mix%d' % m], writes=['mixb'])
                    for nb in range(2):
                        wv, wr = wload("w_o", 128, 8, [(nb * 512, 512)])
                        for t in range(TB):
                            pb = 2 + t % 2
                            for kt in range(8):
                                c.mm(lambda e: e.matmul(PB(pb), lhsT=mixb[:, kt, t * 128:(t + 1) * 128], rhs=wv[:, kt, :], start=(kt == 0), stop=(kt == 7)),
                                     reads=['mixb', wr], writes=['P%d' % pb], last=(kt == 7))
                            c.op('dve', lambda e: e.tensor_tensor(out=xbuf[:, t, nb * 512:(nb + 1) * 512], in0=PB(pb), in1=xbuf[:, t, nb * 512:(nb + 1) * 512], op=ALU.add),
                                 reads=['P%d' % pb, 'xbuf%d' % t], writes=['xbuf%d' % t])
                dump("x1", xbuf[:, 0, :], ['xbuf0'])
                c.barrier()
                with ExitStack() as esA:
                    def sbC(name, shape, dt=F32):
                        return esA.enter_context(nc.sbuf_tensor(un(name), list(shape), dt))
                    xn_bufs = [sbC("xnC%d" % i, [128, 1024], BF16) for i in range(2)]
                    actT = sbC("actT", [128, 22, BLK], BF16)
                    stg = [sbC("stgC%d" % i, [128, BLK + 2]) for i in range(4)]
                    accs = [sbC("accC%d" % i, [128, BLK]) for i in range(4)]
                    sg = [sbC("sg%d" % i, [128, BLK]) for i in range(2)]
                    rmsnorm_to_hT("p_gffn", xn_bufs)
                    for ch in range(11):
                        wv, wr = wload("w_up", 128, 8, [(ch * 256, 256), (D_FF + ch * 256, 256)])
                        for jj in range(2):
                            j = ch * 2 + jj
                            items = []
                            for half in range(2):
                                pb = (j % 2) * 2 + half
                                jc = j + 22 * half
                                for kt in range(8):
                                    c.mm(lambda e: e.matmul(PB(pb, 0, BLK), lhsT=wv[:, kt, half * 256 + jj * 128: half * 256 + (jj + 1) * 128], rhs=hT[:, kt, :],
                                                            start=(kt == 0), stop=(kt == 7)), reads=['hT', wr], writes=['P%d' % pb], last=(kt == 7))
                                si = (j % 2) * 2 + half
                                items.append((PB(pb, 0, BLK), stg[si], halo_f[:, jc, :], PR["p_cw_ffn"][:, jc, :], PR["p_cb_ffn"][:, jc:jc + 1],
                                              accs[si], 'stgC%d' % si, 'accC%d' % si, 'halo_f%d' % jc, 'P%d' % pb))
                            conv_multi(items, 2)
                            sgi = sg[j % 2]
                            c.op('act', lambda e: e.activation(out=sgi[:], in_=accs[(j % 2) * 2][:], func=AF.Silu), reads=['accC%d' % ((j % 2) * 2)], writes=['sg%d' % (j % 2)])
                            c.op('pool', lambda e: e.tensor_tensor(out=actT[:, j, :], in0=sgi[:], in1=accs[(j % 2) * 2 + 1][:], op=ALU.mult),
                                 reads=['sg%d' % (j % 2), 'accC%d' % ((j % 2) * 2 + 1)], writes=['actT'])
                    dump("actT", actT[:, 0, :], ['actT'])
                    for m4 in range(4):
                        for kh in range(2):
                            wv, wr = wload("w_down", 128, 11, [(m4 * 256, 256)], r0=kh * 1408)
                            for t in range(TB):
                                for kt in range(11):
                                    c.mm(lambda e: e.matmul(PB(4 + t, 0, 256), lhsT=actT[:, kh * 11 + kt, t * 128:(t + 1) * 128], rhs=wv[:, kt, :],
                                                            start=(kh == 0 and kt == 0), stop=(kh == 1 and kt == 10)),
                                         reads=['actT', wr], writes=['P%d' % (4 + t)], last=(kt == 10))
                        for t in range(TB):
                            c.op('dve', lambda e: e.tensor_tensor(out=xbuf[:, t, m4 * 256:(m4 + 1) * 256], in0=PB(4 + t, 0, 256),
                                                                  in1=xbuf[:, t, m4 * 256:(m4 + 1) * 256], op=ALU.add),
                                 reads=['P%d' % (4 + t), 'xbuf%d' % t], writes=['xbuf%d' % t])
                    c.dma('pool', out_d[tok0:tok0 + BLK, :].rearrange("(t p) f -> p t f", p=128), xbuf[:], reads=['xbuf%d' % t for t in range(TB)], key='ost')
                c.barrier()
        c.barrier()
        print("ninst", c.ninst, "cnt", c.cnt)
    return nc


def kernel(**inputs):
    x = np.asarray(inputs["x"], dtype=np.float32)
    B, S, D = x.shape
    NSEQ = B // NCORES
    nc = build(NSEQ=NSEQ, S=S, TB=2)
    consts = host_consts(S)
    params = host_params(inputs)
    maps = []
    for i in range(NCORES):
        m = {"x": np.ascontiguousarray(x[i * NSEQ:(i + 1) * NSEQ].reshape(NSEQ * S, D))}
        for k in WSHAPES:
            m[k] = np.ascontiguousarray(np.asarray(inputs[k], dtype=np.float32)[0])
        m.update(params)
        m.update(consts)
        maps.append(m)
    res = run_bass_kernel_spmd(nc, maps, core_ids=list(range(NCORES)))
    out = np.concatenate([np.asarray(r["out"]).reshape(NSEQ, S, D) for r in res.results], axis=0)
    return out.astype(np.float32)
```
